# Optimizing a Trainium2 kernel written in Bass

```python
import math
import jax
import jax.numpy as jnp
from jax import lax
import numpy as np

D_MODEL = 1024
BATCH = 4
SEQ = 4096
DEPTH = 2

GRID_W = 64
CTX_LEN = 256
HEAD_DIM = 64
EPS = 1e-6
S5_WIDTH = 512
S5_GROUP = 16
S5_GROUPS = S5_WIDTH // S5_GROUP
S5_STATE = 64
NA_HEADS = 8
NA_WIDTH = NA_HEADS * HEAD_DIM
NA_ROWS = 8
NA_COLS = 16
LRU_WIDTH = 512
LRU_BLOCKS = 8
LRU_BLOCK = LRU_WIDTH // LRU_BLOCKS
LRU_CONV = 4
LRU_CONV_PAD = (1, 2)
LRU_C = 8.0
GQA_HEADS = 8
GQA_KV_HEADS = 2
GQA_WIDTH = GQA_HEADS * HEAD_DIM
GQA_KV_WIDTH = GQA_KV_HEADS * HEAD_DIM
Q_BLOCK = 128
ROPE_THETA = 10000.0
N_EVEN = (DEPTH + 1) // 2
N_ODD = DEPTH // 2
EVEN_SPLITS = (S5_WIDTH, 2 * S5_WIDTH, 2 * S5_WIDTH + NA_WIDTH, 2 * S5_WIDTH + 2 * NA_WIDTH, 2 * S5_WIDTH + 3 * NA_WIDTH)
EVEN_IN = 2 * S5_WIDTH + 4 * NA_WIDTH
EVEN_MIX = S5_WIDTH + NA_WIDTH
ODD_SPLITS = (LRU_WIDTH, 2 * LRU_WIDTH, 2 * LRU_WIDTH + GQA_WIDTH, 2 * LRU_WIDTH + GQA_WIDTH + GQA_KV_WIDTH, 2 * LRU_WIDTH + GQA_WIDTH + 2 * GQA_KV_WIDTH)
ODD_IN = 2 * LRU_WIDTH + 2 * GQA_WIDTH + 2 * GQA_KV_WIDTH
ODD_MIX = LRU_WIDTH + GQA_WIDTH

kernel_name = "hybrid_s5_natten_rglru_gqa_prefix_dit"

F32 = jnp.float32


def rms_norm(x, g):
    xf = x.astype(F32)
    y = xf * lax.rsqrt(jnp.mean(xf * xf, axis=-1, keepdims=True) + EPS)
    return (y * g.astype(F32)).astype(x.dtype)


def heads(t, n):
    return t.reshape(*t.shape[:-1], n, HEAD_DIM)


def linear_scan(a, b, h0, reverse):
    first = -1 if reverse else 0
    b = b.at[:, first].add(a[:, first] * h0)

    def combine(left, right):
        a_l, b_l = left
        a_r, b_r = right
        return a_r * a_l, a_r * b_l + b_r

    _, h = lax.associative_scan(combine, (a, b), axis=1, reverse=reverse)
    return h


def s5_states(u, h0s, lam_re, lam_im, log_dt, b_re, b_im):
    bsz, length, _ = u.shape
    ug = u.astype(F32).reshape(bsz, length, S5_GROUPS, S5_GROUP).astype(jnp.complex64)
    hs, finals = [], []
    for dirn in range(2):
        lam = lax.complex(lam_re[dirn].astype(F32), lam_im[dirn].astype(F32))
        dt = jnp.exp(log_dt[dirn].astype(F32))[:, None]
        lam_bar = jnp.exp(lam * dt)
        b = lax.complex(b_re[dirn].astype(F32), b_im[dirn].astype(F32))
        b_bar = ((lam_bar - 1.0) / lam)[..., None] * b
        bu = jnp.einsum('blgh,gph->blgp', ug, b_bar)
        a = jnp.broadcast_to(lam_bar, bu.shape)
        h = linear_scan(a, bu, h0s[dirn], reverse=(dirn == 1))
        hs.append(h)
        finals.append(h[:, 0] if dirn == 1 else h[:, -1])
    return hs, finals


def s5_readout(u, hs, c_re, c_im, d_skip, w_glu, b_glu):
    bsz, length, width = u.shape
    y = d_skip.astype(F32) * u.astype(F32)
    for dirn in range(2):
        cm = lax.complex(c_re[dirn].astype(F32), c_im[dirn].astype(F32))
        y = y + jnp.einsum('blgp,ghp->blgh', hs[dirn], cm).real.reshape(bsz, length, width)
    y = jax.nn.gelu(y)
    y = y * jax.nn.sigmoid(y @ w_glu.astype(F32) + b_glu.astype(F32))
    return y.astype(u.dtype)


def neighbourhood_attention(q, k, v, k_ctx, v_ctx, rel_bias):
    bsz, length, n_h, dh = q.shape
    rows = length // GRID_W
    kr = min(NA_ROWS, rows)
    kc = NA_COLS
    scale = dh ** -0.5
    qg = q.reshape(bsz, rows, GRID_W, n_h, dh)
    kg = k.reshape(bsz, rows, GRID_W, n_h, dh)
    vg = v.reshape(bsz, rows, GRID_W, n_h, dh)
    col = jnp.arange(GRID_W)
    col_idx = jnp.clip(col - kc // 2, 0, GRID_W - kc)[:, None] + jnp.arange(kc)[None, :]
    dc = col_idx - col[:, None] + (NA_COLS - 1)

    def one_row(r):
        row_start = jnp.clip(r - kr // 2, 0, rows - kr)
        k_win = lax.dynamic_slice_in_dim(kg, row_start, kr, axis=1)[:, :, col_idx]
        v_win = lax.dynamic_slice_in_dim(vg, row_start, kr, axis=1)[:, :, col_idx]
        q_r = lax.dynamic_index_in_dim(qg, r, axis=1, keepdims=False)
        dr = row_start + jnp.arange(kr) - r + (NA_ROWS - 1)
        bias = rel_bias[:, dr][:, :, dc].transpose(0, 2, 1, 3).astype(F32)
        s_loc = jnp.einsum('bwhd,brwjhd->bhwrj', q_r, k_win).astype(F32) * scale + bias[None]
        s_ctx = jnp.einsum('bwhd,bchd->bhwc', q_r, k_ctx).astype(F32) * scale
        s = jnp.concatenate([s_loc.reshape(bsz, n_h, GRID_W, kr * kc), s_ctx], axis=-1)
        p = jax.nn.softmax(s, axis=-1).astype(v.dtype)
        p_loc = p[..., :kr * kc].reshape(bsz, n_h, GRID_W, kr, kc)
        o = jnp.einsum('bhwrj,brwjhd->bwhd', p_loc, v_win) + jnp.einsum('bhwc,bchd->bwhd', p[..., kr * kc:], v_ctx)
        return o

    out = lax.map(one_row, jnp.arange(rows))
    return out.transpose(1, 0, 2, 3, 4).reshape(bsz, length, n_h * dh)


def context_attention(q, k, v):
    bsz, lc, hq, dh = q.shape
    hkv = k.shape[2]
    qg = q.reshape(bsz, lc, hkv, hq // hkv, dh)
    s = jnp.einsum('bqkgd,bskd->bkgqs', qg, k).astype(F32) * dh ** -0.5
    p = jax.nn.softmax(s, axis=-1).astype(v.dtype)
    return jnp.einsum('bkgqs,bskd->bqkgd', p, v).reshape(bsz, lc, hq * dh)


def centred_depthwise_conv(x, w, b):
    y = lax.conv_general_dilated(x, w[:, None, :], window_strides=(1,), padding=[LRU_CONV_PAD],
                                 dimension_numbers=('NWC', 'WIO', 'NWC'), feature_group_count=x.shape[-1])
    return y + b


def rglru_states(x, h0s, lam, w_a, b_a, w_x, b_x):
    bsz, length, width = x.shape
    xf = x.astype(F32)
    xb = xf.reshape(bsz, length, LRU_BLOCKS, LRU_BLOCK)
    hs, finals = [], []
    for dirn in range(2):
        gate_r = jax.nn.sigmoid(jnp.einsum('blni,nij->blnj', xb, w_a[dirn].astype(F32)).reshape(bsz, length, width) + b_a[dirn].astype(F32))
        gate_i = jax.nn.sigmoid(jnp.einsum('blni,nij->blnj', xb, w_x[dirn].astype(F32)).reshape(bsz, length, width) + b_x[dirn].astype(F32))
        log_a = -LRU_C * gate_r * jax.nn.softplus(-lam[dirn].astype(F32))
        a = jnp.exp(log_a)
        mult = jnp.sqrt(jnp.maximum(-jnp.expm1(2.0 * log_a), 0.0))
        h = linear_scan(a, mult * gate_i * xf, h0s[dirn], reverse=(dirn == 1))
        hs.append(h)
        finals.append(h[:, 0] if dirn == 1 else h[:, -1])
    return hs, finals


def axial_rope_tables(n_tokens):
    t = jnp.arange(n_tokens)
    row = (t // GRID_W).astype(F32)
    col = (t % GRID_W).astype(F32)
    half = HEAD_DIM // 2
    inv = ROPE_THETA ** (-jnp.arange(0, half, 2, dtype=F32) / half)
    ang = jnp.concatenate([row[:, None] * inv, col[:, None] * inv], axis=-1)
    return jnp.cos(ang), jnp.sin(ang)


def apply_rope(x, cos, sin):
    xf = x.astype(F32)
    x1, x2 = xf[..., 0::2], xf[..., 1::2]
    cs, sn = cos[None, :, None, :], sin[None, :, None, :]
    return jnp.stack([x1 * cs - x2 * sn, x1 * sn + x2 * cs], axis=-1).reshape(x.shape).astype(x.dtype)


def gqa_latent(q, k, v, k_ctx, v_ctx):
    bsz, length, hq, dh = q.shape
    g = hq // GQA_KV_HEADS
    k_all = jnp.concatenate([k_ctx, k], axis=1)
    v_all = jnp.concatenate([v_ctx, v], axis=1)
    nb = length // Q_BLOCK
    qb = jnp.moveaxis(q.reshape(bsz, nb, Q_BLOCK, GQA_KV_HEADS, g, dh), 1, 0)
    scale = dh ** -0.5

    def block(q_blk):
        s = jnp.einsum('bqkgd,bskd->bkgqs', q_blk, k_all).astype(F32) * scale
        p = jax.nn.softmax(s, axis=-1).astype(v.dtype)
        return jnp.einsum('bkgqs,bskd->bqkgd', p, v_all)

    o = lax.map(block, qb)
    return jnp.moveaxis(o, 0, 1).reshape(bsz, length, hq * dh)


def even_layer(xl, xc, w_in, w_out, lam_re, lam_im, log_dt, b_re, b_im, c_re, c_im, d_skip, w_glu, b_glu, rel_bias, ctx_out):
    bsz = xl.shape[0]
    u_l, ga_l, q_l, k_l, v_l, gb_l = jnp.split(xl @ w_in, list(EVEN_SPLITS), axis=-1)
    u_c, ga_c, q_c, k_c, v_c, gb_c = jnp.split(xc @ w_in, list(EVEN_SPLITS), axis=-1)
    zero = jnp.zeros((bsz, S5_GROUPS, S5_STATE), jnp.complex64)
    hs_c, fin_c = s5_states(u_c, (zero, zero), lam_re, lam_im, log_dt, b_re, b_im)
    hs_l, _ = s5_states(u_l, fin_c, lam_re, lam_im, log_dt, b_re, b_im)
    y_a = s5_readout(u_l, hs_l, c_re, c_im, d_skip, w_glu, b_glu) * jax.nn.silu(ga_l)
    kc, vc = heads(k_c, NA_HEADS), heads(v_c, NA_HEADS)
    y_b = neighbourhood_attention(heads(q_l, NA_HEADS), heads(k_l, NA_HEADS), heads(v_l, NA_HEADS), kc, vc, rel_bias) * jax.nn.silu(gb_l)
    out_l = jnp.concatenate([y_a, y_b], axis=-1) @ w_out
    if not ctx_out:
        return out_l, None
    y_ac = s5_readout(u_c, hs_c, c_re, c_im, d_skip, w_glu, b_glu) * jax.nn.silu(ga_c)
    y_bc = context_attention(heads(q_c, NA_HEADS), kc, vc) * jax.nn.silu(gb_c)
    out_c = jnp.concatenate([y_ac, y_bc], axis=-1) @ w_out
    return out_l, out_c


def odd_layer(xl, xc, w_in, w_out, conv_w, conv_b, lam, w_a, b_a, w_x, b_x, q_norm, k_norm, ctx_out):
    bsz, length, _ = xl.shape
    x_l, gc_l, q_l, k_l, v_l, gd_l = jnp.split(xl @ w_in, list(ODD_SPLITS), axis=-1)
    x_c, gc_c, q_c, k_c, v_c, gd_c = jnp.split(xc @ w_in, list(ODD_SPLITS), axis=-1)
    zero = jnp.zeros((bsz, LRU_WIDTH), F32)
    hs_c, fin_c = rglru_states(centred_depthwise_conv(x_c, conv_w, conv_b), (zero, zero), lam, w_a, b_a, w_x, b_x)
    hs_l, _ = rglru_states(centred_depthwise_conv(x_l, conv_w, conv_b), fin_c, lam, w_a, b_a, w_x, b_x)
    y_c = (hs_l[0] + hs_l[1]).astype(xl.dtype) * jax.nn.silu(gc_l)
    cos, sin = axial_rope_tables(length)
    q = apply_rope(rms_norm(heads(q_l, GQA_HEADS), q_norm), cos, sin)
    k = apply_rope(rms_norm(heads(k_l, GQA_KV_HEADS), k_norm), cos, sin)
    kc = rms_norm(heads(k_c, GQA_KV_HEADS), k_norm)
    vc = heads(v_c, GQA_KV_HEADS)
    y_d = gqa_latent(q, k, heads(v_l, GQA_KV_HEADS), kc, vc) * jax.nn.silu(gd_l)
    out_l = jnp.concatenate([y_c, y_d], axis=-1) @ w_out
    if not ctx_out:
        return out_l, None
    y_cc = (hs_c[0] + hs_c[1]).astype(xc.dtype) * jax.nn.silu(gc_c)
    y_dc = context_attention(rms_norm(heads(q_c, GQA_HEADS), q_norm), kc, vc) * jax.nn.silu(gd_c)
    out_c = jnp.concatenate([y_cc, y_dc], axis=-1) @ w_out
    return out_l, out_c


def setup_inputs(seed: int = 0) -> dict:
    key = jax.random.key(seed)
    ks = jax.random.split(key, 32)
    d = D_MODEL
    g, p, hgrp = S5_GROUPS, S5_STATE, S5_GROUP

    def nrm(k, shape, s):
        return jax.random.normal(k, shape, F32) * s

    lru_u = jax.random.uniform(ks[25], (N_ODD, 2, LRU_WIDTH), F32, 0.9, 0.999)
    lru_p = lru_u ** (1.0 / LRU_C)
    return {
        "x": nrm(ks[0], (BATCH, SEQ, d), 1.0),
        "c": nrm(ks[1], (BATCH, d), 1.0),
        "ctx": nrm(ks[2], (BATCH, CTX_LEN, d), 1.0),
        "c_ctx": nrm(ks[3], (d,), 1.0),
        "ada_w": nrm(ks[4], (DEPTH, d, 3 * d), 0.5 * d ** -0.5),
        "ada_b": nrm(ks[5], (DEPTH, 3 * d), 0.02),
        "pre_g": 1.0 + nrm(ks[6], (DEPTH, d), 0.05),
        "post_g": 1.0 + nrm(ks[7], (DEPTH, d), 0.05),
        "ev_w_in": nrm(ks[8], (N_EVEN, d, EVEN_IN), d ** -0.5),
        "ev_w_out": nrm(ks[9], (N_EVEN, EVEN_MIX, d), EVEN_MIX ** -0.5),
        "s5_lam_re": -0.5 * jnp.exp(nrm(ks[10], (N_EVEN, 2, g, p), 0.1)),
        "s5_lam_im": math.pi * jnp.arange(p, dtype=F32) + nrm(ks[11], (N_EVEN, 2, g, p), 0.05),
        "s5_log_dt": jax.random.uniform(ks[12], (N_EVEN, 2, g), F32, math.log(1e-3), math.log(1e-1)),
        "s5_b_re": nrm(ks[13], (N_EVEN, 2, g, p, hgrp), (2 * hgrp) ** -0.5),
        "s5_b_im": nrm(ks[14], (N_EVEN, 2, g, p, hgrp), (2 * hgrp) ** -0.5),
        "s5_c_re": nrm(ks[15], (N_EVEN, 2, g, hgrp, p), p ** -0.5),
        "s5_c_im": nrm(ks[16], (N_EVEN, 2, g, hgrp, p), p ** -0.5),
        "s5_d": nrm(ks[17], (N_EVEN, S5_WIDTH), 1.0),
        "s5_w_glu": nrm(ks[18], (N_EVEN, S5_WIDTH, S5_WIDTH), S5_WIDTH ** -0.5),
        "s5_b_glu": nrm(ks[19], (N_EVEN, S5_WIDTH), 0.02),
        "na_rel_bias": nrm(ks[20], (N_EVEN, NA_HEADS, 2 * NA_ROWS - 1, 2 * NA_COLS - 1), 0.1),
        "od_w_in": nrm(ks[21], (N_ODD, d, ODD_IN), d ** -0.5),
        "od_w_out": nrm(ks[22], (N_ODD, ODD_MIX, d), ODD_MIX ** -0.5),
        "lru_conv_w": nrm(ks[23], (N_ODD, LRU_CONV, LRU_WIDTH), LRU_CONV ** -0.5),
        "lru_conv_b": nrm(ks[24], (N_ODD, LRU_WIDTH), 0.02),
        "lru_lam": jnp.log(lru_p) - jnp.log1p(-lru_p),
        "lru_w_a": nrm(ks[26], (N_ODD, 2, LRU_BLOCKS, LRU_BLOCK, LRU_BLOCK), LRU_BLOCK ** -0.5),
        "lru_b_a": nrm(ks[27], (N_ODD, 2, LRU_WIDTH), 0.02),
        "lru_w_x": nrm(ks[28], (N_ODD, 2, LRU_BLOCKS, LRU_BLOCK, LRU_BLOCK), LRU_BLOCK ** -0.5),
        "lru_b_x": nrm(ks[29], (N_ODD, 2, LRU_WIDTH), 0.02),
        "gqa_q_norm": 1.0 + nrm(ks[30], (N_ODD, HEAD_DIM), 0.05),
        "gqa_k_norm": 1.0 + nrm(ks[31], (N_ODD, HEAD_DIM), 0.05),
    }


def reference(x, c, ctx, c_ctx, ada_w, ada_b, pre_g, post_g,
              ev_w_in, ev_w_out, s5_lam_re, s5_lam_im, s5_log_dt, s5_b_re, s5_b_im, s5_c_re, s5_c_im,
              s5_d, s5_w_glu, s5_b_glu, na_rel_bias,
              od_w_in, od_w_out, lru_conv_w, lru_conv_b, lru_lam, lru_w_a, lru_b_a, lru_w_x, lru_b_x,
              gqa_q_norm, gqa_k_norm):
    h_lat, h_ctx = x, ctx
    for i in range(DEPTH):
        j = i // 2
        last = i == DEPTH - 1
        sh_l, sc_l, gt_l = jnp.split(jax.nn.silu(c) @ ada_w[i] + ada_b[i], 3, axis=-1)
        sh_c, sc_c, gt_c = jnp.split(jax.nn.silu(c_ctx) @ ada_w[i] + ada_b[i], 3, axis=-1)
        xl = rms_norm(h_lat, pre_g[i]) * (1.0 + sc_l[:, None]) + sh_l[:, None]
        xc = rms_norm(h_ctx, pre_g[i]) * (1.0 + sc_c) + sh_c
        if i % 2 == 0:
            out_l, out_c = even_layer(xl, xc, ev_w_in[j], ev_w_out[j], s5_lam_re[j], s5_lam_im[j], s5_log_dt[j],
                                      s5_b_re[j], s5_b_im[j], s5_c_re[j], s5_c_im[j], s5_d[j], s5_w_glu[j],
                                      s5_b_glu[j], na_rel_bias[j], not last)
        else:
            out_l, out_c = odd_layer(xl, xc, od_w_in[j], od_w_out[j], lru_conv_w[j], lru_conv_b[j], lru_lam[j],
                                     lru_w_a[j], lru_b_a[j], lru_w_x[j], lru_b_x[j], gqa_q_norm[j], gqa_k_norm[j],
                                     not last)
        h_lat = h_lat + gt_l[:, None] * rms_norm(out_l, post_g[i])
        if not last:
            h_ctx = h_ctx + gt_c * rms_norm(out_c, post_g[i])
    return h_lat
```

```python
import math
from concourse.bass_utils import run_bass_kernel_spmd
import numpy as np
from contextlib import ExitStack
import concourse.bass as bass
import concourse.mybir as mybir

F32 = mybir.dt.float32
BF16 = mybir.dt.bfloat16
I32 = mybir.dt.int32
AF = mybir.ActivationFunctionType
ALU = mybir.AluOpType
AX = mybir.AxisListType

SAME_ENGINE_SYNC = True
N_DMA_SEMS = 12
FULL_SAME_SYNC = True


class _St:
    __slots__ = ("w", "r")

    def __init__(self):
        self.w = None
        self.r = {}


class V:
    __slots__ = ("ap", "buf", "key")

    def __init__(self, ap, buf, key=None):
        self.ap = ap
        self.buf = buf
        self.key = key


class _Keyed:
    def __init__(self, tb, key):
        self.tb = tb
        self.key = key

    def __getitem__(self, idx):
        return V(self.tb.t[idx], self.tb, self.key)


class TB:
    def __init__(self, name, t):
        self.name = name
        self.t = t
        self.parts = {}
        self.default = _St()

    def k(self, key=None):
        return _Keyed(self, key)

    def __getitem__(self, idx):
        return V(self.t[idx], self, None)

    def states(self, key):
        if key is None:
            return [self.default] + list(self.parts.values())
        if key not in self.parts:
            s = _St()
            s.w = self.default.w
            s.r = dict(self.default.r)
            self.parts[key] = s
        return [self.parts[key]]


class SubTB(TB):
    def __init__(self, name, parent, pkey, ap):
        self.name = name
        self.t = ap
        self.parent = parent
        self.pkey = pkey

    def states(self, key):
        return self.parent.states(self.pkey)


class Rec:
    __slots__ = ("eng", "idx", "fn", "dma", "deps", "raw", "rawsc", "accum", "signal", "semval", "dsem", "name")

    def __init__(self, eng, idx, fn, dma, name):
        self.eng = eng
        self.idx = idx
        self.fn = fn
        self.dma = dma
        self.deps = set()
        self.raw = set()
        self.rawsc = set()
        self.accum = False
        self.signal = False
        self.semval = None
        self.dsem = None
        self.name = name


ENGS = ["pe", "act", "dve", "pool", "sp"]


def need_same(rec, d):
    if not SAME_ENGINE_SYNC:
        return False
    if rec.eng == "pe":
        return False
    if d not in rec.raw:
        return False
    if FULL_SAME_SYNC or rec.eng == "pool":
        return True
    return (d in rec.rawsc) or d.accum


class Sched:
    def __init__(self, nc):
        self.nc = nc
        self.q = {e: [] for e in ENGS}
        self.es = ExitStack()
        self.n_dma = {e: 0 for e in ENGS}

    def sbuf(self, name, shape, dtype):
        t = self.es.enter_context(self.nc.sbuf_tensor(name, list(shape), dtype))
        return TB(name, t)

    def psum(self, name, shape, dtype):
        t = self.es.enter_context(self.nc.psum_tensor(name, list(shape), dtype))
        return TB(name, t)

    def dram(self, name, shape, dtype, kind):
        t = self.nc.dram_tensor(name, list(shape), dtype, kind=kind)
        return TB(name, t.ap())

    def do(self, eng, fn, w=(), r=(), dma=False, name="", rs=(), accum=False):
        rec = Rec(eng, len(self.q[eng]), fn, dma, name)
        rec.accum = accum
        for v in rs:
            for st in v.buf.states(v.key):
                if st.w is not None:
                    rec.rawsc.add(st.w)
        r = list(r) + list(rs)
        for v in r:
            for st in v.buf.states(v.key):
                if st.w is not None:
                    rec.deps.add(st.w)
                    rec.raw.add(st.w)
        for v in w:
            for st in v.buf.states(v.key):
                if st.w is not None:
                    rec.deps.add(st.w)
                for rr in st.r.values():
                    rec.deps.add(rr)
        for v in r:
            for st in v.buf.states(v.key):
                if dma:
                    st.r[("dma", eng, rec.idx)] = rec
                else:
                    st.r[eng] = rec
        for v in w:
            for st in v.buf.states(v.key):
                st.w = rec
                st.r = {}
        rec.deps.discard(rec)
        self.q[eng].append(rec)
        return rec

    def dma(self, eng, out, in_, **kw):
        return self.do(eng, lambda e: e.dma_start(out=out.ap, in_=in_.ap, **kw), w=[out], r=[in_], dma=True)

    def finalize(self):
        nc = self.nc
        for e in ENGS:
            for rec in self.q[e]:
                for d in rec.deps:
                    if d.dma:
                        continue
                    if d.eng == rec.eng and not rec.dma and not need_same(rec, d):
                        continue
                    d.signal = True
        sems = {e: self.es.enter_context(nc.semaphore("s_" + e)) for e in ENGS}
        dsems = {e: [self.es.enter_context(nc.semaphore("d_%s%d" % (e, i))) for i in range(N_DMA_SEMS)]
                 for e in ENGS if any(r.dma for r in self.q[e])}
        for e in ENGS:
            cnt = 0
            nd = 0
            for rec in self.q[e]:
                if rec.dma:
                    rec.dsem = (dsems[e][nd % N_DMA_SEMS], 16 * (nd // N_DMA_SEMS + 1), nd)
                    nd += 1
                elif rec.signal:
                    cnt += 1
                    rec.semval = cnt
        block = self.es.enter_context(nc.Block())
        sched = self

        def run(e, engobj):
            waited = {}
            for rec in sched.q[e]:
                need = {}
                for d in rec.deps:
                    if d.dma:
                        s, val, _ = d.dsem
                        key = ("d", d.eng, id(s))
                        if need.get(key, (None, 0))[1] < val:
                            need[key] = (s, val)
                    else:
                        if d.eng == e and not rec.dma and not need_same(rec, d):
                            continue
                        key = ("c", d.eng)
                        if need.get(key, (None, 0))[1] < d.semval:
                            need[key] = (sems[d.eng], d.semval)
                if rec.dma:
                    s, val, nd = rec.dsem
                    if nd >= N_DMA_SEMS:
                        key = ("d", e, id(s))
                        pv = val - 16
                        if need.get(key, (None, 0))[1] < pv:
                            need[key] = (s, pv)
                for key, (s, val) in need.items():
                    if waited.get(key, 0) >= val:
                        continue
                    waited[key] = val
                    engobj.wait_ge(s, val)
                ins = rec.fn(engobj)
                if rec.dma:
                    ins.then_inc(rec.dsem[0], 16)
                elif rec.signal:
                    ins.then_inc(sems[e], 1)

        names = {"pe": "tensor", "act": "scalar", "dve": "vector", "pool": "gpsimd", "sp": "sync"}
        for e in ENGS:
            if not self.q[e]:
                continue

            def mk(e):
                def f(engobj):
                    run(e, engobj)
                return f
            getattr(block, names[e])(mk(e))
        self.es.close()


D = 1024
NCTX = 256
NLAT = 4096
NT = NCTX + NLAT
NTILE = NT // 128
EPS = 1e-6
COLT = [(0, 256)] + [(256 + 512 * i, 512) for i in range(8)]
TWO_PI = 2.0 * math.pi
TP_HI = 6.28125
TP_LO = TWO_PI - 6.28125


def unsq(ap):
    nd = len(ap.shape)
    names = ["d%d" % i for i in range(nd)]
    src = " ".join(names[:-1] + ["(%s o)" % names[-1]])
    dst = " ".join(names + ["o"])
    return ap.rearrange("%s -> %s" % (src, dst), o=1)


def skew(n, stages, lags):
    for step in range(n + max(lags)):
        for st, lag in zip(stages, lags):
            i = step - lag
            if 0 <= i < n:
                st(i)


def tiles_of(t0, n):
    return list(range(t0 // 128, (t0 + n + 127) // 128))


class KB:
    def __init__(self, debug=None):
        self.debug = debug or {}
        nc = bass.Bass("TRN2", target_bir_lowering=False)
        self.nc = nc
        S = Sched(nc)
        self.S = S
        self.dbg_outs = []
        di = lambda name, shape, dt=F32: S.dram(name, shape, dt, "ExternalInput")
        self.xs = di("xs", [NT, D])
        self.cT = di("cT", [128, 8, 2])
        self.ada_w = di("ada_w", [2, D, 3 * D])
        self.ada_b = di("ada_b", [2, 3 * D])
        self.ada_b_pp = di("ada_b_pp", [2, 128, 24])
        self.pre_g_pp = di("pre_g_pp", [2, 128, 8])
        self.post_g = di("post_g", [2, D])
        self.w_in = [di("ev_w_in", [D, 3072]), di("od_w_in", [D, 2304])]
        self.w_out = [di("ev_w_out", [D, D]), di("od_w_out", [D, D])]
        self.s5_lam_re = di("s5_lam_re", [64, 64])
        self.s5_lam_im = di("s5_lam_im", [64, 64])
        self.s5_log_dt = di("s5_log_dt", [1, 64])
        self.s5_b_re = di("s5_b_re", [2, 32, 64, 16])
        self.s5_b_im = di("s5_b_im", [2, 32, 64, 16])
        self.s5_c_re = di("s5_c_re", [2, 32, 16, 64])
        self.s5_c_im = di("s5_c_im", [2, 32, 16, 64])
        self.s5_d_pp = di("s5_d_pp", [128, 4])
        self.s5_w_glu = di("s5_w_glu", [512, 512])
        self.s5_b_glu_pp = di("s5_b_glu_pp", [128, 4])
        self.na_bias = di("na_bias", [8, 21, 128, 128])
        self.lru_conv_w_pp = di("lru_conv_w_pp", [128, 4, 4])
        self.lru_conv_b_pp = di("lru_conv_b_pp", [128, 4])
        self.lru_lam_pp = di("lru_lam_pp", [128, 2, 4])
        self.lru_w_a = di("lru_w_a", [2, 8, 64, 64])
        self.lru_b_a_pp = di("lru_b_a_pp", [128, 2, 4])
        self.lru_w_x = di("lru_w_x", [2, 8, 64, 64])
        self.lru_b_x_pp = di("lru_b_x_pp", [128, 2, 4])
        self.qn_pp = di("qn_pp", [128, 1])
        self.kn_pp = di("kn_pp", [128, 1])
        self.cosT = di("cosT", [128, NLAT])
        self.sinT = di("sinT", [128, NLAT])
        self.c_swap = di("c_swap", [128, 128])
        self.c_onesbd = di("c_onesbd", [128, 128])
        self.c_iota = di("c_iota", [64, 272])
        self.out = S.dram("out", [NLAT, D], F32, "ExternalOutput")
        self.hscr = S.dram("hscr", [NT, D], F32, "Internal")
        self.xlT = S.sbuf("xlT", [128, 8, NT], BF16)
        self.yscr = S.dram("yscr", [8, 128, NT], BF16, "Internal")
        self.ident_f = S.sbuf("ident_f", [128, 128], F32)
        self.ident_b = S.sbuf("ident_b", [128, 128], BF16)
        self.modA = S.sbuf("modA", [128, 2, 8, 2], F32)
        self.modB = S.sbuf("modB", [128, 2, 8, 2], F32)
        self.GP = S.sbuf("GP", [128, 2, 2, D], BF16)
        self.small = S.sbuf("small", [128, 64], F32)
        self.RSZ = 130 * 1024
        self.R = S.sbuf("R", [128, self.RSZ // 4], F32)
        self.PA = S.psum("PA", [128, 1024], F32)
        self.PB = S.psum("PB", [128, 1024], F32)
        self.PCD = S.psum("PCD", [128, 1024], F32)
        self.PC = SubTB("PC", self.PCD, 0, self.PCD.t[:, 0:512])
        self.PD = SubTB("PD", self.PCD, 1, self.PCD.t[:, 512:1024])
        self.PE_ = S.psum("PE", [128, 512], F32)
        self.PT = S.psum("PT", [128, 1024], BF16)
        self._carve_off = 0
        self._phase = 0
        self._bar_idx = {}
        self.eps_col = self.small.k(60)[:, 60:61]
        self.one_col = self.small.k(61)[:, 61:62]

    def barrier(self):
        S = self.S
        lasts = []
        for e in ENGS:
            comp = [r for r in S.q[e] if not r.dma]
            if comp:
                lasts.append(comp[-1])
            lasts += [r for r in S.q[e][self._bar_idx.get(e, 0):] if r.dma]
        for e in ENGS:
            rec = S.do(e, lambda en: en.nop())
            for l in lasts:
                rec.deps.add(l)
        for e in ENGS:
            self._bar_idx[e] = len(S.q[e])

    def new_phase(self):
        self.barrier()
        self._carve_off = 0
        self._phase += 1

    def carve(self, name, shape, dtype):
        esz = 4 if dtype in (F32, I32) else 2
        n = 1
        for s in shape[1:]:
            n *= s
        nbytes = (n * esz + 15) // 16 * 16
        off = self._carve_off
        assert off + nbytes <= self.RSZ, (name, off, nbytes)
        self._carve_off += nbytes
        ap = self.R.t[:, off // 4:(off + nbytes) // 4]
        if dtype != F32:
            ap = ap.bitcast(dtype)
        ap = ap[:, 0:n]
        if len(shape) > 2:
            names = " ".join("a%d" % i for i in range(len(shape) - 1))
            kw = {"a%d" % i: shape[i + 1] for i in range(len(shape) - 1)}
            ap = ap.rearrange("p (%s) -> p %s" % (names, names), **kw)
        return TB("%s_%d" % (name, self._phase), ap)

    def dump(self, name, view, shape, dtype=F32):
        o = self.S.dram(name, shape, dtype, "ExternalOutput")
        self.S.dma("sp", o[tuple(slice(None) for _ in shape)], view)
        self.dbg_outs.append(o)

    def finish(self):
        S = self.S
        allouts = [self.out] + self.dbg_outs
        S.do("sp", lambda e: e.nop(), r=[V(o.t, o) for o in allouts])
        S.finalize()
        return self.nc

    def xl_views(self, t0, n):
        return [self.xlT.k(t)[:, :, :] for t in tiles_of(t0, n)]

    def y_store(self, kb, src, t0, n, eng="pool"):
        for t in tiles_of(t0, n):
            pass
        self.S.do(eng, lambda e: e.dma_start(out=self.yscr.t[kb, :, t0:t0 + n], in_=src.ap),
                  w=[self.yscr.k(t)[:, :, :] for t in tiles_of(t0, n)], r=[src], dma=True)

    def phase_consts(self):
        S = self
        s = self.S
        s.do("pool", lambda e: e.memset(self.ident_f.t[:, :], 1.0), w=[self.ident_f[:, :]])
        s.do("pool", lambda e: e.affine_select(out=self.ident_f.t[:, :], in_=self.ident_f.t[:, :], pattern=[[-1, 128]],
                                                compare_op=ALU.is_equal, fill=0.0, base=0, channel_multiplier=1),
             w=[self.ident_f[:, :]], r=[self.ident_f[:, :]])
        s.do("pool", lambda e: e.tensor_copy(out=self.ident_b.t[:, :], in_=self.ident_f.t[:, :]), w=[self.ident_b[:, :]], r=[self.ident_f[:, :]])
        s.do("pool", lambda e: e.memset(self.small.t[:, 60:61], EPS), w=[self.eps_col])
        s.do("pool", lambda e: e.memset(self.small.t[:, 61:62], 1.0), w=[self.one_col])

    def phase_adaln(self):
        s = self.S
        self.new_phase()
        cT = self.carve("cT", [128, 8, 2], F32)
        sc = self.carve("sc", [128, 8, 2], F32)
        scb = self.carve("scb", [128, 8, 2, 128], F32)
        abpp = self.carve("abpp", [128, 2, 24], F32)
        pgpp = self.carve("pgpp", [128, 2, 8], F32)
        mod = self.carve("mod", [128, 16, 2], F32)
        abrow = self.carve("abrow", [128, D], F32)
        pgrow = self.carve("pgrow", [128, D], F32)
        wbuf = [self.carve("adaw%d" % i, [128, 8, 512], F32) for i in range(6)]
        gptmp = self.carve("gptmp", [128, 512], F32)
        s.dma("sp", cT[:, :, :], self.cT[:, :, :])
        s.dma("sp", abpp[:, :, :], V(self.ada_b_pp.t.rearrange("l p f -> p l f"), self.ada_b_pp))
        s.dma("sp", pgpp[:, :, :], V(self.pre_g_pp.t.rearrange("l p f -> p l f"), self.pre_g_pp))
        s.do("act", lambda e: e.activation(out=sc.t[:, :, :], in_=cT.t[:, :, :], func=AF.Silu), w=[sc[:, :, :]], r=[cT[:, :, :]])
        s.do("dve", lambda e: e.tensor_copy(out=scb.t[:, :, :, :], in_=unsq(sc.t[:, :, :]).to_broadcast([128, 8, 2, 128])),
             w=[scb[:, :, :, :]], r=[sc[:, :, :]])
        ci = 0
        for l in range(2):
            s.dma("sp", abrow[:, :], V(self.ada_b.t[l:l + 1, 2 * D:3 * D].partition_broadcast(128), self.ada_b))
            s.dma("sp", pgrow[:, :], V(self.post_g.t[l:l + 1, :].partition_broadcast(128), self.post_g))
            for ch in range(6):
                wb = wbuf[ci % 6]
                ci += 1
                src = self.ada_w.t[l].rearrange("(kb p) n -> p kb n", p=128)[:, :, ch * 512:(ch + 1) * 512]
                s.dma("sp", wb[:, :, :], V(src, self.ada_w))
                if ch < 4:
                    for f in range(4):
                        fb = ch * 4 + f
                        for kb in range(8):
                            s.do("pe", lambda e, wb=wb, f=f, kb=kb, fb=fb: e.matmul(
                                out=self.PC.t[:, fb * 2:fb * 2 + 2], lhsT=wb.t[:, kb, f * 128:(f + 1) * 128], rhs=sc.t[:, kb, :],
                                start=(kb == 0), stop=(kb == 7)), w=[self.PC[:, :]], r=[wb[:, :, :], sc[:, :, :]])
                else:
                    half = ch - 4
                    for j in range(2):
                        pj = self.PD if j == 0 else self.PE_
                        for kb in range(8):
                            s.do("pe", lambda e, wb=wb, kb=kb, j=j, pj=pj: e.matmul(
                                out=pj.t[:, :], lhsT=scb.t[:, kb, j, :], rhs=wb.t[:, kb, :],
                                start=(kb == 0), stop=(kb == 7)), w=[pj[:, :]], r=[wb[:, :, :], scb[:, :, :, :]])
                        gp = self.GP.t[:, l, j, half * 512:(half + 1) * 512]
                        s.do("dve", lambda e, pj=pj, gp=gp, half=half: e.tensor_tensor(
                            out=gptmp.t[:, :], in0=pj.t[:, :], in1=abrow.t[:, half * 512:(half + 1) * 512], op=ALU.add),
                            w=[gptmp[:, :]], r=[pj[:, :], abrow[:, :]])
                        s.do("dve", lambda e, gp=gp, half=half: e.tensor_tensor(
                            out=gp, in0=gptmp.t[:, :], in1=pgrow.t[:, half * 512:(half + 1) * 512], op=ALU.mult),
                            w=[self.GP[:, :, :, :]], r=[pgrow[:, :], gptmp[:, :]])
                if ch == 3:
                    s.do("dve", lambda e, l=l: e.tensor_tensor(
                        out=mod.t[:, :, :], in0=self.PC.t[:, 0:32].rearrange("p (f j) -> p f j", j=2),
                        in1=unsq(abpp.t[:, l, 0:16]).to_broadcast([128, 16, 2]), op=ALU.add),
                        w=[mod[:, :, :]], r=[self.PC[:, :], abpp[:, :, :]])
                    s.do("dve", lambda e, l=l: e.tensor_copy(out=self.modB.t[:, l, :, :], in_=mod.t[:, 0:8, :]),
                         w=[self.modB[:, :, :, :]], r=[mod[:, :, :]])
                    s.do("dve", lambda e, l=l: e.tensor_scalar(out=mod.t[:, 8:16, :], in0=mod.t[:, 8:16, :], scalar1=1.0, scalar2=None, op0=ALU.add),
                         w=[mod[:, :, :]], r=[mod[:, :, :]])
                    s.do("dve", lambda e, l=l: e.tensor_tensor(
                        out=self.modA.t[:, l, :, :], in0=mod.t[:, 8:16, :], in1=unsq(pgpp.t[:, l, :]).to_broadcast([128, 8, 2]), op=ALU.mult),
                        w=[self.modA[:, :, :, :]], r=[mod[:, :, :], pgpp[:, :, :]])

    def norm_a(self, xv, scratch_bf, stat_col):
        s = self.S
        ss = self.small.k(stat_col)[:, stat_col:stat_col + 1]
        rs = self.small.k(stat_col + 1)[:, stat_col + 1:stat_col + 2]
        s.do("act", lambda e: e.activation(out=scratch_bf.t[:, :], in_=xv.ap, func=AF.Square, accum_out=ss.ap),
             w=[scratch_bf[:, :], ss], r=[xv], accum=True)
        s.do("act", lambda e: e.activation(out=rs.ap, in_=ss.ap, func=AF.Sqrt, scale=1.0 / D, bias=self.eps_col.ap),
             w=[rs], r=[ss], rs=[self.eps_col])
        s.do("dve", lambda e: e.reciprocal(out=rs.ap, in_=rs.ap), w=[rs], r=[rs])
        s.do("dve", lambda e: e.tensor_scalar(out=scratch_bf.t[:, :], in0=xv.ap, scalar1=rs.ap, scalar2=None, op0=ALU.mult),
             w=[scratch_bf[:, :]], r=[xv], rs=[rs])

    def norm_b(self, l, t, scratch_bf, tmpf):
        s = self.S
        j = 0 if t >= 2 else 1
        for kb in range(8):
            s.do("pe", lambda e, kb=kb: e.transpose(out=self.PT.t[:, kb * 128:(kb + 1) * 128], in_=scratch_bf.t[:, kb * 128:(kb + 1) * 128],
                                                   identity=self.ident_b.t[:, :]),
                 w=[self.PT[:, :]], r=[scratch_bf[:, :], self.ident_b[:, :]])
        pt3 = self.PT.t[:, :].rearrange("p (k n) -> p k n", n=128)
        s.do("dve", lambda e: e.tensor_tensor(out=tmpf.t[:, :].rearrange("p (k n) -> p k n", n=128), in0=pt3,
                                              in1=self.modA.t[:, l, :, j:j + 1].to_broadcast([128, 8, 128]), op=ALU.mult),
             w=[tmpf[:, :]], r=[self.PT[:, :], self.modA[:, :, :, :]])
        s.do("pool", lambda e: e.tensor_tensor(out=self.xlT.t[:, :, t * 128:(t + 1) * 128], in0=tmpf.t[:, :].rearrange("p (k n) -> p k n", n=128),
                                               in1=self.modB.t[:, l, :, j:j + 1].to_broadcast([128, 8, 128]), op=ALU.add),
             w=[self.xlT.k(t)[:, :, :]], r=[tmpf[:, :], self.modB[:, :, :, :]])

    def norm_to_xlT(self, l, xv, t, scratch_bf, tmpf, stat_col):
        self.norm_a(xv, scratch_bf, stat_col)
        self.norm_b(l, t, scratch_bf, tmpf)

    def phase_prologue0(self):
        s = self.S
        self.new_phase()
        xb = [self.carve("xin%d" % i, [128, D], F32) for i in range(4)]
        sb = [self.carve("xsq%d" % i, [128, D], BF16) for i in range(3)]
        tf = [self.carve("xtf%d" % i, [128, D], F32) for i in range(2)]

        def L(t):
            s.dma("sp", xb[t % 4][:, :], V(self.xs.t[t * 128:(t + 1) * 128, :], self.xs))

        def A(t):
            self.norm_a(xb[t % 4][:, :], sb[t % 3], 2 * (t % 2))

        def B(t):
            self.norm_b(0, t, sb[t % 3], tf[t % 2])

        skew(NTILE, [L, A, B], [0, 1, 2])

    def load_w(self, l, specs, wst, wbf, eng="pool"):
        s = self.S
        wsrc = self.w_in[l].t.rearrange("(kb p) n -> p kb n", p=128)
        tot = 0
        for (c0, n, d0) in specs:
            s.dma("sp", V(wst.t[:, :, d0:d0 + n], wst), V(wsrc[:, :, c0:c0 + n], self.w_in[l]))
            tot = max(tot, d0 + n)
        s.do(eng, lambda e: e.tensor_copy(out=wbf.t[:, :, 0:tot], in_=wst.t[:, :, 0:tot]), w=[wbf[:, :, :]], r=[wst[:, :, :]])

    def proj_fm(self, wbf, evac, colt=COLT, banks=None, mcols=128):
        s = self.S
        banks = banks or [self.PC, self.PD, self.PE_]
        for i, (t0, n) in enumerate(colt):
            pb = banks[i % len(banks)]
            for kb in range(8):
                s.do("pe", lambda e, pb=pb, kb=kb, t0=t0, n=n: e.matmul(
                    out=pb.t[0:mcols, 0:n], lhsT=wbf.t[:, kb, 0:mcols], rhs=self.xlT.t[:, kb, t0:t0 + n],
                    start=(kb == 0), stop=(kb == 7)), w=[pb[:, :]], r=[wbf[:, :, :]] + self.xl_views(t0, n))
            evac(pb, t0, n)

    def proj_tm(self, wbf, ncols, evac, tiles=range(NTILE), banks=None):
        s = self.S
        banks = banks or [self.PC, self.PD, self.PE_]
        for i, t in enumerate(tiles):
            pb = banks[i % len(banks)]
            for kb in range(8):
                s.do("pe", lambda e, pb=pb, kb=kb, t=t: e.matmul(
                    out=pb.t[:, 0:ncols], lhsT=self.xlT.t[:, kb, t * 128:(t + 1) * 128], rhs=wbf.t[:, kb, 0:ncols],
                    start=(kb == 0), stop=(kb == 7)), w=[pb[:, :]], r=[wbf[:, :, :], self.xlT.k(t)[:, :, :]])
            evac(pb, t)

    def tt(self, eng, out, in0, in1, op, extra_r=()):
        self.S.do(eng, lambda e: e.tensor_tensor(out=out.ap, in0=in0.ap, in1=in1.ap, op=op), w=[out], r=[in0, in1] + list(extra_r))

    def ts(self, eng, out, in0, s1, s2, op0, op1=None, extra_r=()):
        r = [in0] + list(extra_r)
        rs = [x for x in (s1, s2) if isinstance(x, V)]
        a1 = s1.ap if isinstance(s1, V) else s1
        a2 = s2.ap if isinstance(s2, V) else s2
        if op1 is None:
            self.S.do(eng, lambda e: e.tensor_scalar(out=out.ap, in0=in0.ap, scalar1=a1, scalar2=None, op0=op0), w=[out], r=r, rs=rs)
        else:
            self.S.do(eng, lambda e: e.tensor_scalar(out=out.ap, in0=in0.ap, scalar1=a1, scalar2=a2, op0=op0, op1=op1), w=[out], r=r, rs=rs)

    def stt(self, out, in0, sc, in1, op0, op1):
        r = [in0, in1]
        rs = [sc] if isinstance(sc, V) else []
        a = sc.ap if isinstance(sc, V) else sc
        self.S.do("dve", lambda e: e.scalar_tensor_tensor(out=out.ap, in0=in0.ap, scalar=a, in1=in1.ap, op0=op0, op1=op1), w=[out], r=r, rs=rs)

    def act(self, out, in_, func, scale=1.0, bias=None, extra_w=(), accum=None):
        r = [in_]
        rs = [x for x in (scale, bias) if isinstance(x, V)]
        sc = scale.ap if isinstance(scale, V) else scale
        kw = {}
        if bias is not None:
            kw["bias"] = bias.ap if isinstance(bias, V) else bias
        w = [out]
        if accum is not None:
            kw["accum_out"] = accum.ap
            w.append(accum)
        self.S.do("act", lambda e: e.activation(out=out.ap, in_=in_.ap, func=func, scale=sc, **kw), w=w, r=r, rs=rs, accum=(accum is not None))

    def cp(self, eng, out, in_):
        if eng == "act":
            self.S.do("act", lambda e: e.copy(out=out.ap, in_=in_.ap), w=[out], r=[in_])
        else:
            self.S.do(eng, lambda e: e.tensor_copy(out=out.ap, in_=in_.ap), w=[out], r=[in_])

    def mm(self, out, lhsT, rhs, start, stop, **kw):
        self.S.do("pe", lambda e: e.matmul(out=out.ap, lhsT=lhsT.ap, rhs=rhs.ap, start=start, stop=stop, **kw), w=[out], r=[lhsT, rhs])

    def capture(self, fn):
        calls = []
        S = self.S
        orig = S.do
        S.do = lambda *a, **k: calls.append((a, k))
        try:
            fn()
        finally:
            S.do = orig
        return calls

    def interleave(self, chains):
        n = max(len(c) for c in chains)
        for k in range(n):
            for c in chains:
                if k < len(c):
                    self.S.do(*c[k][0], **c[k][1])

    def angle_sincos(self, ang, sn, cs, kf, ki, shape):
        self.ts("dve", kf, ang, 1.0 / TWO_PI, None, ALU.mult)
        self.cp("dve", ki, kf)
        self.cp("dve", kf, ki)
        self.stt(ang, kf, -TP_HI, ang, ALU.mult, ALU.add)
        self.stt(ang, kf, -TP_LO, ang, ALU.mult, ALU.add)
        self.ts("dve", ang, ang, math.pi, -math.pi, ALU.min, ALU.max)
        self.act(sn, ang, AF.Sin)
        self.ts("dve", kf, ang, math.pi / 2, None, ALU.add)
        self.ts("dve", cs, kf, math.pi, -TWO_PI, ALU.is_gt, ALU.mult)
        self.tt("dve", kf, kf, cs, ALU.add)
        self.ts("dve", kf, kf, math.pi, -math.pi, ALU.min, ALU.max)
        self.act(cs, kf, AF.Sin)

    def phase_s5(self):
        s = self.S
        self.new_phase()
        C = self.carve
        P = 128
        PH = 64
        lr = C("lr", [128, 64], F32); li = C("li", [128, 64], F32); dt = C("dt", [128, 64], F32)
        zr = C("zr", [128, 64], F32); zi = C("zi", [128, 64], F32)
        rho16 = C("rho16", [128, 64], F32); phi = C("phi", [128, 64], F32)
        lbr = C("lbr", [128, 64], F32); lbi = C("lbi", [128, 64], F32)
        kr_ = C("kr", [128, 64], F32); ki_ = C("ki", [128, 64], F32)
        t1 = C("t1", [128, 64], F32); t2 = C("t2", [128, 64], F32); t3 = C("t3", [128, 64], F32)
        kfs = C("kfs", [128, 64], F32); kis = C("kis", [128, 64], I32)
        Er = C("Er", [128, 17, 64], F32); Ei = C("Ei", [128, 17, 64], F32)
        Akr = C("Akr", [128, 16, 64], F32); Aki = C("Aki", [128, 16, 64], F32)
        bre = C("bre", [128, 64, 16], BF16); bim = C("bim", [128, 64, 16], BF16)
        CTr = C("CTr", [128, 2, 4, 128], BF16); CTi = C("CTi", [128, 2, 4, 128], BF16)
        wst = C("wst", [128, 8, 128], F32); wbf = C("wbf", [128, 8, 128], BF16)
        cst = C("cst", [128, 64], F32)
        iota = C("iota", [128, 272], F32)
        d_pp = C("d_pp", [128, 4], F32)
        u_bf = C("u_ph", [128, 16, 272], BF16)
        ypre = C("ypre", [128, NT], BF16)
        W_in2 = [C("W_in%d" % i, [128, 16, 128], BF16) for i in range(2)]
        Apad2 = [C("Apad%d" % i, [128, 16, 128], BF16) for i in range(2)]
        Xre2 = [C("Xre2_%d" % i, [128, 272], F32) for i in range(2)]; Xim2 = [C("Xim2_%d" % i, [128, 272], F32) for i in range(2)]
        q1b = C("q1b", [128, 256], F32); q2b = C("q2b", [128, 256], F32); kf2 = C("kf2", [128, 272], F32)
        W_out = C("W_out", [128, 16, 16, 32], BF16)
        W_intra = C("W_intra", [128, 2, 16, 128], BF16)
        Cstack = C("Cstack", [128, 2, 8, 128], BF16)
        Hs = C("Hs", [128, 16, 273], BF16)
        Mre = C("Mre", [128, 272], F32); Mim = C("Mim", [128, 272], F32)
        cs2 = [C("cs%d" % i, [128, 272], F32) for i in range(2)]; sn2 = [C("sn%d" % i, [128, 272], F32) for i in range(2)]
        ang = C("ang", [128, 272], F32); kf = C("kf", [128, 272], F32); kin = C("kin", [128, 272], I32)
        q1 = C("q1", [128, 256], F32); q2 = C("q2", [128, 256], F32); q3 = C("q3", [128, 272], F32)
        q4 = q2; q5 = q1b; q6 = q2b
        ystage = [C("ystage%d" % i, [128, 512], BF16) for i in range(2)]
        h = lambda tb, *idx: V(tb.t[(slice(0, P),) + idx], tb)
        for src_, dst_ in ((self.s5_lam_re, lr), (self.s5_lam_im, li)):
            s.dma("sp", V(dst_.t[0:PH, :], dst_), src_[:, :])
            s.dma("sp", V(dst_.t[PH:128, 0:63], dst_), V(src_.t[:, 1:64], src_))
            s.dma("sp", V(dst_.t[PH:128, 63:64], dst_), V(src_.t[:, 63:64], src_), allow_slow_non_contiguous=True)
        s.dma("sp", V(dt.t[0:PH, :], dt), V(self.s5_log_dt.t[0:1, :].partition_broadcast(PH), self.s5_log_dt))
        s.dma("sp", V(dt.t[PH:128, 0:63], dt), V(self.s5_log_dt.t[0:1, 1:64].partition_broadcast(PH), self.s5_log_dt))
        s.dma("sp", V(dt.t[PH:128, 63:64], dt), V(self.s5_log_dt.t[0:1, 63:64].partition_broadcast(PH), self.s5_log_dt), allow_slow_non_contiguous=True)
        s.do("pool", lambda e: e.memset(bre.t[:, :, :], 0.0), w=[bre[:, :, :]])
        s.do("pool", lambda e: e.memset(bim.t[:, :, :], 0.0), w=[bim[:, :, :]])
        for d_ in range(2):
            for src_, dst_ in ((self.s5_b_re, bre), (self.s5_b_im, bim)):
                stg = wst.t[0:PH, 0:4, :].rearrange("p a (b h) -> p (a b) h", h=16)
                s.dma("sp", V(stg, wst), V(src_.t[d_].rearrange("g p h -> p g h"), src_))
                self.cp("dve", V(dst_.t[0:PH, d_ * 32:(d_ + 1) * 32, :], dst_), V(stg, wst))
                stg2 = wst.t[PH:128, 0:4, :].rearrange("p a (b h) -> p (a b) h", h=16)
                s.dma("sp", V(stg2[:, 0:31, :], wst), V(src_.t[d_, 1:32].rearrange("g p h -> p g h"), src_))
                self.cp("dve", V(dst_.t[PH:128, d_ * 32:d_ * 32 + 31, :], dst_), V(stg2[:, 0:31, :], wst))
        s.dma("sp", V(iota.t[0:PH, :], iota), self.c_iota[:, :])
        s.dma("sp", V(iota.t[PH:128, :], iota), self.c_iota[:, :])
        s.dma("sp", d_pp[:, :], self.s5_d_pp[:, :])
        s.do("pool", lambda e: e.memset(CTr.t[:, :, :, :], 0.0), w=[CTr[:, :, :, :]])
        s.do("pool", lambda e: e.memset(CTi.t[:, :, :, :], 0.0), w=[CTi[:, :, :, :]])
        cnats = [q1, q2, q1b, q2b]
        cps = [self.PC, self.PD]
        ci_ = 0
        for ri, (src, dst) in enumerate(((self.s5_c_re, CTr), (self.s5_c_im, CTi))):
            for d_ in range(2):
                for cb in range(4):
                    cnat = cnats[ci_ % 4]; cp_ = cps[ci_ % 2]; ci_ += 1
                    s.dma("sp", V(cnat.t[:, 0:64], cnat), V(src.t[d_, cb * 8:(cb + 1) * 8].rearrange("g h p -> (g h) p"), src))
                    s.do("pe", lambda e, cnat=cnat, cp_=cp_: e.transpose(out=cp_.t[0:64, 0:128], in_=cnat.t[:, 0:64], identity=self.ident_f.t[:, :]),
                         w=[cp_[:, :]], r=[cnat[:, :], self.ident_f[:, :]])
                    self.cp("dve", V(dst.t[0:PH, d_, cb, :], dst), V(cp_.t[0:64, 0:128], cp_))
                    self.cp("act", V(dst.t[PH:128, d_, cb, 0:112], dst), V(cp_.t[0:64, 16:128], cp_))
        H = lambda tb: h(tb, slice(None))
        self.act(H(dt), H(dt), AF.Exp)
        self.tt("dve", H(zr), H(lr), H(dt), ALU.mult)
        self.tt("dve", H(zi), H(li), H(dt), ALU.mult)
        self.act(H(rho16), H(zr), AF.Exp, scale=16.0)
        self.act(H(t3), H(zr), AF.Exp)
        self.ts("dve", H(phi), H(zi), 16.0, None, ALU.mult)
        self.ts("dve", H(kfs), H(phi), 1.0 / TWO_PI, None, ALU.mult)
        self.cp("dve", H(kis), H(kfs)); self.cp("dve", H(kfs), H(kis))
        self.stt(H(phi), H(kfs), -TP_HI, H(phi), ALU.mult, ALU.add)
        self.stt(H(phi), H(kfs), -TP_LO, H(phi), ALU.mult, ALU.add)
        self.cp("dve", H(t1), H(zi))
        self.angle_sincos(H(t1), H(lbi), H(lbr), H(kfs), H(kis), None)
        self.tt("dve", H(lbr), H(lbr), H(t3), ALU.mult)
        self.tt("dve", H(lbi), H(lbi), H(t3), ALU.mult)
        self.ts("dve", H(t1), H(lbr), -1.0, None, ALU.add)
        self.tt("dve", H(t2), H(lr), H(lr), ALU.mult)
        self.tt("dve", H(t3), H(li), H(li), ALU.mult)
        self.tt("dve", H(t2), H(t2), H(t3), ALU.add)
        s.do("dve", lambda e: e.reciprocal(out=t2.t[0:P, :], in_=t2.t[0:P, :]), w=[t2[:, :]], r=[t2[:, :]])
        self.tt("dve", H(kr_), H(t1), H(lr), ALU.mult)
        self.tt("dve", H(t3), H(lbi), H(li), ALU.mult)
        self.tt("dve", H(kr_), H(kr_), H(t3), ALU.add)
        self.tt("dve", H(kr_), H(kr_), H(t2), ALU.mult)
        self.tt("dve", H(ki_), H(lbi), H(lr), ALU.mult)
        self.tt("dve", H(t3), H(t1), H(li), ALU.mult)
        self.tt("dve", H(ki_), H(ki_), H(t3), ALU.subtract)
        self.tt("dve", H(ki_), H(ki_), H(t2), ALU.mult)
        s.do("pool", lambda e: e.memset(Er.t[0:P, 0, :], 1.0), w=[Er[:, :, :]])
        s.do("pool", lambda e: e.memset(Ei.t[0:P, 0, :], 0.0), w=[Ei[:, :, :]])
        self.cp("dve", V(Er.t[0:P, 1, :], Er), H(lbr))
        self.cp("pool", V(Ei.t[0:P, 1, :], Ei), H(lbi))
        T4 = wst.t[0:P, :, :].rearrange("p a (b g) -> p (a b) g", g=64)
        m = 1
        while m < 16:
            pr = V(Er.t[0:P, m:m + 1, :].to_broadcast([P, m, 64]), Er); pi_ = V(Ei.t[0:P, m:m + 1, :].to_broadcast([P, m, 64]), Ei)
            er = V(Er.t[0:P, 1:m + 1, :], Er); ei = V(Ei.t[0:P, 1:m + 1, :], Ei)
            ta = V(T4[:, 0:m, :], wst, 0); tb_ = V(T4[:, 8:8 + m, :], wst, 1)
            self.tt("dve", ta, er, pr, ALU.mult)
            self.tt("pool", tb_, ei, pi_, ALU.mult)
            self.tt("dve", V(Er.t[0:P, m + 1:2 * m + 1, :], Er), ta, tb_, ALU.subtract)
            self.tt("dve", ta, er, pi_, ALU.mult)
            self.tt("pool", tb_, ei, pr, ALU.mult)
            self.tt("dve", V(Ei.t[0:P, m + 1:2 * m + 1, :], Ei), ta, tb_, ALU.add)
            m *= 2
        kb_r = V(kr_.t[0:P, :].rearrange("p (o g) -> p o g", o=1).to_broadcast([P, 16, 64]), kr_)
        kb_i = V(ki_.t[0:P, :].rearrange("p (o g) -> p o g", o=1).to_broadcast([P, 16, 64]), ki_)
        E16r = V(Er.t[0:P, 0:16, :], Er); E16i = V(Ei.t[0:P, 0:16, :], Ei)
        AkrV = V(Akr.t[0:P, :, :], Akr); AkiV = V(Aki.t[0:P, :, :], Aki)
        self.tt("dve", AkrV, E16r, kb_r, ALU.mult)
        self.tt("dve", AkiV, E16i, kb_i, ALU.mult)
        self.tt("dve", AkrV, AkrV, AkiV, ALU.subtract)
        self.tt("dve", AkiV, E16r, kb_i, ALU.mult)
        TA = V(T4[:, :, :], wst)
        self.tt("dve", TA, E16i, kb_r, ALU.mult)
        self.tt("dve", AkiV, AkiV, TA, ALU.add)
        s.do("pool", lambda e: e.memset(Hs.t[:, :, :], 0.0), w=[Hs[:, :, :]])

        for cb in range(self.debug.get('s5_cbs', 4)):
            self.load_w(0, [(cb * 128, 128, 0)], wst, wbf)
            self.proj_fm(wbf, lambda pb, t0, n: self.cp("act", V(u_bf.t[:, :, t0 // 16:(t0 + n) // 16], u_bf), V(pb.t[:, 0:n].rearrange("p (c s) -> p s c", s=16), pb)), banks=[self.PD, self.PE_])
            s.do("pool", lambda e: e.memset(Cstack.t[:, :, :, :], 0.0), w=[Cstack[:, :, :, :]])
            s.do("pool", lambda e: e.memset(W_out.t[:, :, :, :], 0.0), w=[W_out[:, :, :, :]])
            for d_ in range(2):
                for gl in range(8):
                    self.cp("pool", V(Cstack.t[0:PH, d_, gl, 16 * gl:16 * gl + 16], Cstack), V(CTr.t[0:PH, d_, cb, 16 * gl:16 * gl + 16], CTr))
                    self.ts("dve", V(Cstack.t[PH:128, d_, gl, 16 * gl:16 * gl + 16], Cstack), V(CTi.t[0:PH, d_, cb, 16 * gl:16 * gl + 16], CTi), -1.0, None, ALU.mult)
            PDb = SubTB("PDb", self.PCD, 1, self.PD.t[:, :].bitcast(BF16))
            s.do("pool", lambda e: e.memset(Apad2[0].t[:, :, :], 0.0), w=[Apad2[0][:, :, :]])
            s.do("pool", lambda e: e.memset(Apad2[1].t[:, :, :], 0.0), w=[Apad2[1][:, :, :]])

            Xps = [self.PC, self.PE_]

            def G0(i, cb=cb):
                d_, j = divmod(i, 4)
                glA = 2 * j
                dg = d_ * 32 + cb * 8 + glA
                if j >= 1:
                    for it_, Ap in enumerate(Apad2):
                        pc_ = slice(16 * (glA - 2 + it_), 16 * (glA - 2 + it_) + 16)
                        s.do("pool", lambda e, Ap=Ap, pc_=pc_: e.memset(Ap.t[:, :, pc_], 0.0), w=[Ap[:, :, :]])
                elif i >= 1:
                    for it_, Ap in enumerate(Apad2):
                        pc_ = slice(16 * (6 + it_), 16 * (6 + it_) + 16)
                        s.do("pool", lambda e, Ap=Ap, pc_=pc_: e.memset(Ap.t[:, :, pc_], 0.0), w=[Ap[:, :, :]])
                def partA():
                    ar = V(unsq(Akr.t[:, :, dg]).to_broadcast([128, 16, 16]), Akr)
                    ai = V(unsq(Aki.t[:, :, dg]).to_broadcast([128, 16, 16]), Aki)
                    br = V(bre.t[:, dg, :].rearrange("p (o h) -> p o h", o=1).to_broadcast([128, 16, 16]), bre)
                    bi = V(bim.t[:, dg, :].rearrange("p (o h) -> p o h", o=1).to_broadcast([128, 16, 16]), bim)
                    q13 = lambda tb, lo, hi: V(tb.t[lo:hi, 0:256].rearrange("p (k h) -> p k h", h=16), tb)
                    self.tt("dve", q13(q1, 0, 128), ar, br, ALU.mult)
                    self.tt("pool", q13(q2, 0, 128), ai, bi, ALU.mult)
                    for it_, Ap in enumerate(Apad2):
                        cols = slice(16 * (glA + it_), 16 * (glA + it_) + 16)
                        lo, hi = 64 * it_, 64 * it_ + 64
                        self.tt("dve", V(Ap.t[0:PH, :, cols], Ap), q13(q1, lo, hi), q13(q2, lo, hi), ALU.subtract)
                    self.tt("dve", q13(q1b, 0, 128), ar, bi, ALU.mult)
                    self.tt("pool", q13(q2b, 0, 128), ai, br, ALU.mult)
                    for it_, Ap in enumerate(Apad2):
                        cols = slice(16 * (glA + it_), 16 * (glA + it_) + 16)
                        lo, hi = 64 * it_, 64 * it_ + 64
                        self.tt("dve", V(Ap.t[PH:128, :, cols], Ap), q13(q1b, lo, hi), q13(q2b, lo, hi), ALU.add)


                def partT():
                    Fv = lambda tb: V(tb.t[:, :], tb)
                    self.ts("dve", Fv(ang), Fv(iota), V(phi.t[:, dg:dg + 1], phi), None, ALU.mult)
                    self.angle_sincos(Fv(ang), Fv(sn2[i % 2]), Fv(cs2[i % 2]), Fv(q3), Fv(kin), None)

                def partW():
                    wq = lambda qi, lo, hi: V(wst.t[lo:hi, 2 * qi:2 * qi + 2, :].rearrange("p a (b h) -> p (a b) h", h=16), wst, qi)
                    colsA = slice(16 * glA, 16 * glA + 16)
                    if d_ == 0:
                        er = Er.t[:, 1:17, dg]; ei = Ei.t[:, 1:17, dg]
                    else:
                        er = Er.t[:, 16:0:-1, dg]; ei = Ei.t[:, 16:0:-1, dg]
                    erb = V(unsq(er).to_broadcast([128, 16, 16]), Er); eib = V(unsq(ei).to_broadcast([128, 16, 16]), Ei)
                    cr = V(CTr.t[:, d_, cb, colsA].rearrange("p (o h) -> p o h", o=1).to_broadcast([128, 16, 16]), CTr)
                    ci = V(CTi.t[:, d_, cb, colsA].rearrange("p (o h) -> p o h", o=1).to_broadcast([128, 16, 16]), CTi)
                    q13 = lambda tb, lo, hi: V(tb.t[lo:hi, 0:256].rearrange("p (k h) -> p k h", h=16), tb)
                    self.tt("dve", wq(0, 0, 128), erb, cr, ALU.mult)
                    self.tt("pool", wq(1, 0, 128), eib, ci, ALU.mult)
                    for it_ in range(2):
                        slot = (glA + it_) * 2 + d_
                        lo, hi = 64 * it_, 64 * it_ + 64
                        wc = slice(16 * it_, 16 * it_ + 16)
                        self.tt("dve", V(W_out.t[0:PH, slot, :, wc], W_out), wq(0, lo, hi), wq(1, lo, hi), ALU.subtract)
                    self.tt("pool", wq(2, 0, 128), erb, ci, ALU.mult)
                    self.tt("pool", wq(3, 0, 128), eib, cr, ALU.mult)
                    self.tt("pool", wq(2, 0, 128), wq(2, 0, 128), wq(3, 0, 128), ALU.add)
                    for it_ in range(2):
                        slot = (glA + it_) * 2 + d_
                        lo, hi = 64 * it_, 64 * it_ + 64
                        wc = slice(16 * it_, 16 * it_ + 16)
                        self.ts("dve", V(W_out.t[PH:128, slot, :, wc], W_out), wq(2, lo, hi), -1.0, None, ALU.mult)


                partA()
                self.interleave([self.capture(partT), self.capture(partW)])

            def G1(i, cb=cb):
                d_, j = divmod(i, 4)
                Xr = Xre2[i % 2]; Xi = Xim2[i % 2]
                if j == 0:
                    s.do("dve", lambda e: e.memset(self.PA.t[:, :], 0.0), w=[self.PA[:, :]])
                    s.do("dve", lambda e: e.memset(self.PB.t[:, :], 0.0), w=[self.PB[:, :]])
                for it_ in range(2):
                    Ap = Apad2[it_]; Wi = W_in2[it_]
                    for half in range(2):
                        ptb = self.PT if half == 0 else PDb
                        for ii in range(8):
                            sidx = half * 8 + ii
                            k = 15 - sidx if d_ == 0 else sidx
                            s.do("pe", lambda e, ii=ii, k=k, ptb=ptb, Ap=Ap: e.transpose(out=ptb.t[:, ii * 128:(ii + 1) * 128], in_=Ap.t[:, k, :], identity=self.ident_b.t[:, :]),
                                 w=[ptb[:, :]], r=[Ap[:, :, :], self.ident_b[:, :]])
                        self.cp("act", V(Wi.t[:, half * 8:(half + 1) * 8, :], Wi),
                                V(ptb.t[:, :].rearrange("p (s c) -> p s c", c=128), ptb))
                si = []
                if d_ == 0:
                    for sidx in range(16):
                        si.append((slice(0, 272), slice(0, 272), sidx, sidx == 0, sidx == 15))
                else:
                    for sidx in range(16):
                        si.append((slice(0, 16), slice(15, None, -1), sidx, sidx == 0, sidx == 15))
                    for sidx in range(16):
                        si.append((slice(16, 272), slice(271, 15, -1), sidx, False, sidx == 15))
                for tau in range(16):
                    for itk in range(2):
                        gl = 2 * j + itk
                        pk = self.PA if tau < 8 else self.PB
                        tt_ = tau % 8
                        self.mm(V(pk.t[:, tt_ * 128:(tt_ + 1) * 128], pk), V(Apad2[itk].t[:, tau, :], Apad2[itk]), V(Cstack.t[:, d_, gl, :], Cstack),
                                start=False, stop=(gl == 7), skip_group_check=True)
                for (osl, usl, sidx, st_, sp_) in si:
                    for it_ in range(2):
                        Xp = Xps[it_]; Wi = W_in2[it_]
                        self.mm(V(Xp.t[:, osl], Xp), V(Wi.t[:, sidx, :], Wi), V(u_bf.t[:, sidx, usl], u_bf), start=st_, stop=sp_, skip_group_check=True)
                for it_ in range(2):
                    Xp = Xps[it_]
                    lo, hi = 64 * it_, 64 * it_ + 64
                    self.cp("act", V(Xr.t[lo:hi, :], Xr), V(Xp.t[0:PH, 0:272], Xp))
                    self.cp("act", V(Xi.t[lo:hi, :], Xi), V(Xp.t[PH:128, 0:272], Xp))
                if j == 3:
                    self.cp("act", V(W_intra.t[:, d_, 0:8, :], W_intra), V(self.PA.t[:, :].rearrange("p (t c) -> p t c", c=128), self.PA))
                    self.cp("dve", V(W_intra.t[:, d_, 8:16, :], W_intra), V(self.PB.t[:, :].rearrange("p (t c) -> p t c", c=128), self.PB))

            def G2(i, cb=cb):
                d_, j = divmod(i, 4)
                glA = 2 * j
                dg = d_ * 32 + cb * 8 + glA
                Xr = Xre2[i % 2]; Xi = Xim2[i % 2]
                F = lambda tb: V(tb.t[:, :], tb)
                cs = cs2[i % 2]; sn = sn2[i % 2]
                def m_re():
                    self.tt("dve", F(Mre), F(Xr), F(cs), ALU.mult)
                    self.tt("pool", F(kf), F(Xi), F(sn), ALU.mult)
                    self.tt("dve", F(Mre), F(Mre), F(kf), ALU.add)

                def m_im():
                    self.tt("dve", F(Mim), F(Xi), F(cs), ALU.mult)
                    self.tt("pool", F(kf2), F(Xr), F(sn), ALU.mult)
                    self.tt("dve", F(Mim), F(Mim), F(kf2), ALU.subtract)
                self.interleave([self.capture(m_re), self.capture(m_im)])
                rb = V(rho16.t[:, dg:dg + 1].to_broadcast([128, 272]), rho16)
                for M_, X_ in ((Mre, Xr), (Mim, Xi)):
                    s.do("dve", lambda e, M_=M_, X_=X_, rb=rb: e.tensor_tensor_scan(out=X_.t[:, :], data0=rb.ap, data1=M_.t[:, :], initial=0.0,
                                                                                op0=ALU.mult, op1=ALU.add), w=[X_[:, :]], r=[M_[:, :]], rs=[rb])
                self.tt("dve", F(Mre), F(Xr), F(cs), ALU.mult)
                self.tt("pool", F(kf), F(Xi), F(sn), ALU.mult)
                self.tt("dve", F(Mim), F(Xr), F(sn), ALU.mult)
                self.tt("pool", F(kf2), F(Xi), F(cs), ALU.mult)
                for it_ in range(2):
                    slot = (glA + it_) * 2 + d_
                    lo, hi = 64 * it_, 64 * it_ + 64
                    self.tt("dve", V(Hs.t[0:PH, slot, 1:273], Hs), V(Mre.t[lo:hi, :], Mre), V(kf.t[lo:hi, :], kf), ALU.subtract)
                    self.tt("dve", V(Hs.t[PH:128, slot, 1:273], Hs), V(Mim.t[lo:hi, :], Mim), V(kf2.t[lo:hi, :], kf2), ALU.add)

            skew(8, [G0, G2, G1], [0, 1, 0])
            for ps_ in range(4):
                rows = [(self.PA, 0), (self.PA, 512), (self.PB, 0), (self.PB, 512)]
                chains = []
                for r_ in range(4):
                    sp_ = ps_ * 4 + r_
                    pk, off = rows[r_]
                    ch = []
                    Y = V(pk.t[:, off:off + 272], pk, off)
                    first = True
                    for sidx in range(16):
                        tau = sp_ - sidx
                        lst = []
                        if tau >= 0:
                            lst.append(V(W_intra.t[:, 0, tau, :], W_intra))
                        if tau <= 0:
                            lst.append(V(W_intra.t[:, 1, -tau, :], W_intra))
                        for lw in lst:
                            ch.append(lambda Y=Y, lw=lw, sidx=sidx, first=first: self.mm(Y, lw, V(u_bf.t[:, sidx, :], u_bf), start=first, stop=False, skip_group_check=True))
                            first = False
                    for part in range(3):
                        for par in range(2):
                            def unit(pk=pk, off=off, sp_=sp_, part=part, par=par):
                                for gp in range(4):
                                    gl = 2 * gp + par
                                    tp = (0, 32 * gp)
                                    last = (part == 2 and par == 1 and gp == 3)
                                    if part == 0:
                                        self.mm(V(pk.t[32 * gp:32 * gp + 32, off:off + 272], pk, off), V(W_out.t[:, gl * 2, sp_, :], W_out), V(Hs.t[:, gl * 2, 0:272], Hs),
                                                start=False, stop=False, skip_group_check=True, tile_position=tp)
                                    elif part == 1:
                                        self.mm(V(pk.t[32 * gp:32 * gp + 32, off:off + 16], pk, off), V(W_out.t[:, gl * 2 + 1, sp_, :], W_out), V(Hs.t[:, gl * 2 + 1, 15::-1], Hs),
                                                start=False, stop=False, skip_group_check=True, tile_position=tp)
                                    else:
                                        self.mm(V(pk.t[32 * gp:32 * gp + 32, off + 16:off + 272], pk, off), V(W_out.t[:, gl * 2 + 1, sp_, :], W_out), V(Hs.t[:, gl * 2 + 1, 271:15:-1], Hs),
                                                start=False, stop=last, skip_group_check=True, tile_position=tp)
                            ch.append(unit)
                    chains.append(ch)
                for k_ in range(max(len(c) for c in chains)):
                    for ch in chains:
                        if k_ < len(ch):
                            ch[k_]()
                for r_ in range(4):
                    sp_ = ps_ * 4 + r_
                    pk, off = rows[r_]
                    Y = V(pk.t[:, off:off + 272], pk, off)
                    self.stt(V(ypre.t[:, sp_:NT:16], ypre, sp_), V(u_bf.t[:, sp_, :], u_bf), V(d_pp.t[:, cb:cb + 1], d_pp), Y, ALU.mult, ALU.add)
            if "s5_pre" in self.debug:
                self.dump("d_s5pre%d" % cb, ypre[:, :], [128, NT], BF16)
            for i, (t0, n) in enumerate(COLT):
                st = ystage[i % 2]
                self.act(V(st.t[:, 0:n], st), V(ypre.t[:, t0:t0 + n], ypre), AF.Gelu_apprx_tanh)
                self.y_store(cb, V(st.t[:, 0:n], st), t0, n, eng="act")

    def phase_glu(self):
        s = self.S
        self.new_phase()
        C = self.carve
        yg = C("yg", [128, 4, NT], BF16)
        sga = C("sga", [128, 4, NT], BF16)
        wgs = C("wgs", [128, 4, 512], F32)
        wgb = C("wgb", [128, 4, 512], BF16)
        bg = C("bg", [128, 4], F32)
        wst = C("wst", [128, 8, 128], F32); wbf = C("wbf", [128, 8, 128], BF16)
        W2 = [(wst, wbf), (C("wst2", [128, 8, 128], F32), C("wbf2", [128, 8, 128], BF16))]; wi_ = [0]

        def nw():
            p_ = W2[wi_[0] % 2]; wi_[0] += 1
            return p_
        sig = [C("sig%d" % i, [128, 512], BF16) for i in range(2)]
        yst = [C("yst%d" % i, [128, 512], BF16) for i in range(2)]
        for cb in range(4):
            s.do("sp", lambda e, cb=cb: e.dma_start(out=yg.t[:, cb, :], in_=self.yscr.t[cb, :, :]), w=[yg[:, :, :]], r=[V(self.yscr.t, self.yscr)], dma=True)
        s.dma("sp", wgs[:, :, :], V(self.s5_w_glu.t.rearrange("(kb p) n -> p kb n", p=128), self.s5_w_glu))
        self.cp("pool", wgb[:, :, :], wgs[:, :, :])
        s.dma("sp", bg[:, :], self.s5_b_glu_pp[:, :])
        for cb in range(4):
            wst_, wbf_ = nw()
            self.load_w(0, [(512 + cb * 128, 128, 0)], wst_, wbf_)
            self.proj_fm(wbf_, lambda pb, t0, n, cb=cb: self.act(V(sga.t[:, cb, t0:t0 + n], sga), V(pb.t[:, 0:n], pb), AF.Silu), banks=[self.PC, self.PD, self.PE_])
        zb = [(self.PA, 0), (self.PA, 512), (self.PB, 0), (self.PB, 512)]
        cnt = 0
        for (t0, n) in COLT:
            for cbo in range(4):
                pk, off = zb[cbo]
                for cbi in range(4):
                    self.mm(V(pk.t[:, off:off + n], pk), V(wgb.t[:, cbi, cbo * 128:(cbo + 1) * 128], wgb), V(yg.t[:, cbi, t0:t0 + n], yg),
                            start=(cbi == 0), stop=(cbi == 3))
            for cbo in range(4):
                pk, off = zb[cbo]
                sg = sig[cnt % 2]; st = yst[cnt % 2]; cnt += 1
                self.act(V(sg.t[:, 0:n], sg), V(pk.t[:, off:off + n], pk), AF.Sigmoid, bias=V(bg.t[:, cbo:cbo + 1], bg))
                self.tt("dve", V(st.t[:, 0:n], st), V(yg.t[:, cbo, t0:t0 + n], yg), V(sg.t[:, 0:n], sg), ALU.mult)
                self.tt("dve", V(st.t[:, 0:n], st), V(st.t[:, 0:n], st), V(sga.t[:, cbo, t0:t0 + n], sga), ALU.mult)
                self.y_store(cbo, V(st.t[:, 0:n], st), t0, n)

    def attn_stages(self, blocks, qT, kT, vaug, EB, P_bufs, sgate, psS_sets, psO_sets, psB, ones_row, recs, tmps, ysts):
        s = self.S

        def A(i):
            b = blocks[i]; nq = b["nq"]; hp = slice(64 * b["hh"], 64 * b["hh"] + 64)
            pS = psS_sets[i % len(psS_sets)]
            for j, (kt, cfg) in enumerate(b["ktiles"]):
                so = b["so"][j]
                if cfg is None:
                    self.mm(V(pS.t[:, so:so + nq], pS), V(kT.t[:, b["hh"], kt * 128:(kt + 1) * 128], kT), V(qT.t[:, b["q0"]:b["q0"] + nq], qT), start=True, stop=True)
                else:
                    self.mm(V(pS.t[:, so:so + nq], pS), V(self.ident_b.t[:, :], self.ident_b), V(EB.t[:, b["hh"], cfg, 0:nq], EB), start=True, stop=False, skip_group_check=True)
                    self.mm(V(pS.t[:, so:so + nq], pS), V(kT.t[:, b["hh"], kt * 128:(kt + 1) * 128], kT), V(qT.t[:, b["q0"]:b["q0"] + nq], qT), start=False, stop=True, skip_group_check=True)

        def B(i):
            b = blocks[i]; nq = b["nq"]
            pS = psS_sets[i % len(psS_sets)]; Pb = P_bufs[i % len(P_bufs)]
            nk = len(b["ktiles"])
            j = 0
            while j < nk:
                so = b["so"][j]
                bank = so // 512
                j2 = j
                while j2 + 1 < nk and b["so"][j2 + 1] // 512 == bank and b["so"][j2 + 1] == b["so"][j2] + nq:
                    j2 += 1
                cnt = j2 - j + 1
                self.act(V(Pb.t[:, j:j + cnt, 0:nq], Pb), V(pS.t[:, so:so + cnt * nq].rearrange("p (j q) -> p j q", q=nq), pS), AF.Exp, scale=0.125)
                j = j2 + 1

        def Cc(i):
            b = blocks[i]; nq = b["nq"]
            Pb = P_bufs[i % len(P_bufs)]; pO = psO_sets[i % len(psO_sets)]
            nk = len(b["ktiles"])
            for j, (kt, cfg) in enumerate(b["ktiles"]):
                self.mm(V(pO.t[0:65, 0:nq], pO), V(vaug.t[:, kt, b["hh"], :], vaug), V(Pb.t[:, j, 0:nq], Pb), start=(j == 0), stop=(j == nk - 1))

        def D1(i):
            b = blocks[i]; nq = b["nq"]; hp = slice(64 * b["hh"], 64 * b["hh"] + 64)
            pO = psO_sets[i % len(psO_sets)]
            rec = recs[i % len(recs)]; tmp = tmps[i % len(tmps)]
            self.act(V(rec.t[64:65, 0:nq], rec), V(pO.t[64:65, 0:nq], pO), AF.Ln)
            self.tt("dve", V(tmp.t[0:64, 0:nq], tmp), V(pO.t[0:64, 0:nq], pO), V(sgate.t[hp, b["g0"]:b["g0"] + nq], sgate), ALU.mult)
            self.act(V(rec.t[64:65, 0:nq], rec), V(rec.t[64:65, 0:nq], rec), AF.Exp, scale=-1.0)

        def D2(i):
            b = blocks[i]; nq = b["nq"]
            rec = recs[i % len(recs)]; tmp = tmps[i % len(tmps)]; yst = ysts[i % len(ysts)]
            self.mm(V(psB.t[0:64, 0:nq], psB), V(ones_row.t[64:65, 0:64], ones_row), V(rec.t[64:65, 0:nq], rec), start=True, stop=True)
            kb, ph, tok0 = b["ysel"]
            self.tt("dve", V(yst.t[64 * ph:64 * ph + 64, 0:nq], yst), V(tmp.t[0:64, 0:nq], tmp), V(psB.t[0:64, 0:nq], psB), ALU.mult)
            s.do("pool", lambda e: e.dma_start(out=self.yscr.t[kb, 64 * ph:64 * ph + 64, tok0:tok0 + nq], in_=yst.t[64 * ph:64 * ph + 64, 0:nq]),
                 w=[self.yscr.k(t)[:, :, :] for t in tiles_of(tok0, nq)], r=[yst[:, :]], dma=True)

        skew(len(blocks), [A, B, Cc, D1, D2], [0, 1, 1, 2, 3])

    def phase_na(self):
        s = self.S
        self.new_phase()
        C = self.carve
        qT = C("qT", [128, NT], BF16); kT = C("kT2", [128, 2, NT], BF16); sgb = C("sgb", [128, NT], BF16)
        vaug = C("vaug", [128, NTILE, 2, 65], BF16)
        s.do("pool", lambda e: e.memset(kT.t[:, :, :], 0.0), w=[kT[:, :, :]])
        EB = C("EB", [128, 2, 21, 128], BF16)
        ebst = [C("ebst%d" % i, [128, 7, 128], F32) for i in range(3)]
        P_buf = [C("Pbuf%d" % i, [128, 7, 256], BF16) for i in range(2)]
        wst = C("wst", [128, 8, 128], F32); wbf = C("wbf", [128, 8, 128], BF16)
        W2 = [(wst, wbf), (C("wst2", [128, 8, 128], F32), C("wbf2", [128, 8, 128], BF16))]; wi_ = [0]

        def nw():
            p_ = W2[wi_[0] % 2]; wi_[0] += 1
            return p_
        ones_row = C("ones_row", [128, 64], F32)
        recs = [C("rec%d" % i, [128, 256], F32) for i in range(3)]
        tmp = [C("tmp%d" % i, [128, 256], F32) for i in range(4)]
        yst = [C("yst%d" % i, [128, 256], BF16) for i in range(3)]
        s.do("pool", lambda e: e.memset(ones_row.t[:, :], 1.0), w=[ones_row[:, :]])
        s.do("pool", lambda e: e.memset(vaug.t[:, :, :, 64:65], 1.0), w=[vaug[:, :, :, :]])
        it = 0
        for hp in range(self.debug.get("na_hps", 4)):
            wst_, wbf_ = nw()
            self.load_w(0, [(1024 + hp * 128, 128, 0)], wst_, wbf_)
            self.proj_fm(wbf_, lambda pb, t0, n: self.cp("act", V(qT.t[:, t0:t0 + n], qT), V(pb.t[:, 0:n], pb)))
            wst_, wbf_ = nw()
            self.load_w(0, [(1536 + hp * 128, 128, 0)], wst_, wbf_)
            self.proj_fm(wbf_, lambda pb, t0, n: (self.cp("dve", V(kT.t[0:64, 0, t0:t0 + n], kT), V(pb.t[0:64, 0:n], pb)),
                                                 self.cp("pool" if False else "act", V(kT.t[64:128, 1, t0:t0 + n], kT), V(pb.t[64:128, 0:n], pb))))
            wst_, wbf_ = nw()
            self.load_w(0, [(2560 + hp * 128, 128, 0)], wst_, wbf_)
            self.proj_fm(wbf_, lambda pb, t0, n: self.act(V(sgb.t[:, t0:t0 + n], sgb), V(pb.t[:, 0:n], pb), AF.Silu))
            wst_, wbf_ = nw()
            self.load_w(0, [(2048 + hp * 128, 128, 0)], wst_, wbf_)
            self.proj_tm(wbf_, 128, lambda pb, t: self.cp("dve" if t % 2 else "act", V(vaug.t[:, t, :, 0:64], vaug),
                                                          V(pb.t[:, 0:128].rearrange("p (h d) -> p h d", h=2), pb)))
            for hh in range(2):
                for c3 in range(3):
                    st = ebst[(hh * 3 + c3) % 3]
                    s.dma("pool", st[:, :, :], V(self.na_bias.t[hp * 2 + hh, c3 * 7:(c3 + 1) * 7].rearrange("c k q -> k c q"), self.na_bias))
                    self.act(V(EB.t[:, hh, c3 * 7:(c3 + 1) * 7, :], EB), st[:, :, :], AF.Copy, scale=8.0)
            blocks = []
            for hh in range(2):
                for rp in range(32):
                    if 2 <= rp <= 29:
                        kts = [(2 + rp - 2 + i, i) for i in range(5)]
                    elif rp == 0:
                        kts = [(2 + i, 5 + i) for i in range(4)]
                    elif rp == 1:
                        kts = [(2 + i, 9 + i) for i in range(4)]
                    elif rp == 30:
                        kts = [(2 + 28 + i, 13 + i) for i in range(4)]
                    else:
                        kts = [(2 + 28 + i, 17 + i) for i in range(4)]
                    kts = kts + [(0, None), (1, None)]
                    q0 = 256 + rp * 128
                    blocks.append(dict(q0=q0, nq=128, hh=hh, ktiles=kts, so=[128 * j for j in range(len(kts))], g0=q0, ysel=(4 + hp, hh, q0)))
                blocks.append(dict(q0=0, nq=256, hh=hh, ktiles=[(0, None), (1, None)], so=[0, 512], g0=0, ysel=(4 + hp, hh, 0)))
            self.attn_stages(blocks, qT, kT, vaug, EB, P_buf, sgb, [self.PA, self.PB], [self.PC, self.PD], self.PE_, ones_row, recs, tmp, yst)

    def phase_outproj(self, l):
        s = self.S
        self.new_phase()
        C = self.carve
        wos2 = [C("wos%d" % i, [128, 8, 256], F32) for i in range(2)]
        wob = C("wob", [128, 8, D], BF16)
        NB = 5
        yt = [C("yt%d" % i, [128, 8, 128], BF16) for i in range(3)]
        hold = [C("hold%d" % i, [128, D], F32) for i in range(NB)]
        hnew = [C("hnew%d" % i, [128, D], F32) for i in range(3)]
        sq = [C("sq%d" % i, [128, D], BF16) for i in range(2)]
        sq2 = [C("sq2_%d" % i, [128, D], BF16) for i in range(2)]
        tf = [C("tf%d" % i, [128, D], F32) for i in range(2)]
        wsrc = self.w_out[l].t.rearrange("(kb p) n -> p kb n", p=128)
        for c4 in range(4):
            wos = wos2[c4 % 2]
            s.dma("sp", wos[:, :, :], V(wsrc[:, :, c4 * 256:(c4 + 1) * 256], self.w_out[l]))
            self.cp("act" if c4 % 2 else "dve", V(wob.t[:, :, c4 * 256:(c4 + 1) * 256], wob), wos[:, :, :])
        tiles = list(range(NTILE) if l == 0 else range(2, NTILE))
        pOs = [self.PA, self.PB, self.PCD]

        def L(i):
            t = tiles[i]
            s.do("sp", lambda e: e.dma_start(out=yt[i % 3].t[:, :, :], in_=self.yscr.t[:, :, t * 128:(t + 1) * 128].rearrange("k p n -> p k n")),
                 w=[yt[i % 3][:, :, :]], r=[self.yscr.k(t)[:, :, :]], dma=True)
            src = self.xs if l == 0 else self.hscr
            s.dma("act", hold[i % NB][:, :], V(src.t[t * 128:(t + 1) * 128, :], src) if l == 0 else self.hscr.k(t)[t * 128:(t + 1) * 128, :])

        def Mmm(i):
            pO = pOs[i % 3]
            for half in range(2):
                for kb in range(8):
                    self.mm(V(pO.t[:, half * 512:(half + 1) * 512], pO), V(yt[i % 3].t[:, kb, :], yt[i % 3]), V(wob.t[:, kb, half * 512:(half + 1) * 512], wob),
                            start=(kb == 0), stop=(kb == 7))

        def Mst(i):
            pO = pOs[i % 3]
            c0 = 8 + 4 * (i % 2)
            ss = self.small.k(c0)[:, c0:c0 + 1]; rs = self.small.k(c0 + 1)[:, c0 + 1:c0 + 2]
            self.act(sq[i % 2][:, :], pO[:, :], AF.Square, accum=ss)
            self.act(rs, ss, AF.Sqrt, scale=1.0 / D, bias=self.eps_col)
            s.do("dve", lambda e, rs=rs: e.reciprocal(out=rs.ap, in_=rs.ap), w=[rs], r=[rs])

        def E1(i):
            t = tiles[i]
            j = 0 if t >= 2 else 1
            pO = pOs[i % 3]
            c0 = 8 + 4 * (i % 2)
            rs = self.small.k(c0 + 1)[:, c0 + 1:c0 + 2]
            hn = hnew[i % 3]
            self.stt(hn[:, :], pO[:, :], rs, V(self.GP.t[:, l, j, :], self.GP), ALU.mult, ALU.mult)
            self.tt("dve", hn[:, :], hn[:, :], hold[i % NB][:, :], ALU.add)
            if l == 0:
                s.do("pool", lambda e: e.dma_start(out=self.hscr.t[t * 128:(t + 1) * 128, :], in_=hn.t[:, :]),
                     w=[self.hscr.k(t)[t * 128:(t + 1) * 128, :]], r=[hn[:, :]], dma=True)
            else:
                s.do("pool", lambda e: e.dma_start(out=self.out.t[(t - 2) * 128:(t - 1) * 128, :], in_=hn.t[:, :]),
                     w=[V(self.out.t, self.out, t)], r=[hn[:, :]], dma=True)

        def E2a(i):
            self.norm_a(hnew[i % 3][:, :], sq2[i % 2], 16 + 4 * (i % 2))

        def E2b(i):
            self.norm_b(1, tiles[i], sq2[i % 2], tf[i % 2])

        if l == 0:
            skew(len(tiles), [L, Mmm, Mst, E1, E2a, E2b], [0, 1, 2, 3, 4, 5])
        else:
            skew(len(tiles), [L, Mmm, Mst, E1], [0, 1, 2, 3])

    def phase_lru(self):
        s = self.S
        self.new_phase()
        C = self.carve
        NP = 259 + 4099
        xr = C("xr", [128, NP], F32)
        xc = C("xc", [128, NT], F32)
        xcb = C("xcb", [128, NT], BF16)
        a_ = C("a", [128, NT], F32)
        bt = C("bt", [128, NT], F32)
        sgc = C("sgc", [128, NT], BF16)
        wgs4 = [C("wgs%d" % i, [128, 128], F32) for i in range(4)]
        wgb = C("wgb", [128, 16, 128], BF16)
        cw = C("cw", [128, 4, 4], F32); cbias = C("cbias", [128, 4], F32)
        lam = C("lam", [128, 2, 4], F32); cA = C("cA", [128, 2, 4], F32); c2A = C("c2A", [128, 2, 4], F32)
        nba = C("nba", [128, 2, 4], F32); nbx = C("nbx", [128, 2, 4], F32)
        g1 = [C("g1_%d" % i, [128, 512], F32) for i in range(3)]
        g2 = [C("g2_%d" % i, [128, 512], F32) for i in range(5)]
        g3 = [C("g3_%d" % i, [128, 512], F32) for i in range(4)]
        wst = C("wst", [128, 8, 128], F32); wbf = C("wbf", [128, 8, 128], BF16)
        W2 = [(wst, wbf), (C("wst2", [128, 8, 128], F32), C("wbf2", [128, 8, 128], BF16))]; wi_ = [0]

        def nw():
            p_ = W2[wi_[0] % 2]; wi_[0] += 1
            return p_
        yst = [C("yst%d" % i, [128, 512], BF16) for i in range(2)]
        s.dma("sp", cw[:, :, :], self.lru_conv_w_pp[:, :, :]); s.dma("sp", cbias[:, :], self.lru_conv_b_pp[:, :])
        s.dma("sp", lam[:, :, :], self.lru_lam_pp[:, :, :])
        s.dma("sp", nba[:, :, :], self.lru_b_a_pp[:, :, :]); s.dma("sp", nbx[:, :, :], self.lru_b_x_pp[:, :, :])
        pba, pbx = nba, nbx
        self.act(cA[:, :, :], lam[:, :, :], AF.Exp, scale=-1.0)
        self.ts("dve", cA[:, :, :], cA[:, :, :], 1.0, None, ALU.add)
        self.act(cA[:, :, :], cA[:, :, :], AF.Ln)
        self.ts("dve", c2A[:, :, :], cA[:, :, :], -16.0, None, ALU.mult)
        self.ts("dve", cA[:, :, :], cA[:, :, :], -8.0, None, ALU.mult)
        for wg_ in wgs4:
            s.do("pool", lambda e, wg_=wg_: e.memset(wg_.t[:, :], 0.0), w=[wg_[:, :]])
        for d_ in range(2):
            for ax, src in enumerate((self.lru_w_a, self.lru_w_x)):
                for cb in range(4):
                    idx = (d_ * 2 + ax) * 4 + cb
                    wgs = wgs4[idx % 4]
                    for nl in range(2):
                        s.dma("sp", V(wgs.t[64 * nl:64 * nl + 64, 64 * nl:64 * nl + 64], wgs), V(src.t[d_, 2 * cb + nl], src))
                    self.cp("dve", V(wgb.t[:, idx, :], wgb), wgs[:, :])
        segs = [(0, 256, 0), (259, 4096, 256)]
        prev_hf = None
        for cb in range(self.debug.get("lru_cbs", 4)):
            s.do("pool", lambda e: e.memset(xr.t[:, :], 0.0), w=[xr[:, :]] + ([V(prev_hf.t, prev_hf)] if prev_hf is not None else []))
            wst_, wbf_ = nw()
            self.load_w(1, [(cb * 128, 128, 0)], wst_, wbf_)

            def ev_x(pb, t0, n):
                dst = 1 + t0 if t0 < 256 else 259 + 1 + (t0 - 256)
                self.cp("act", V(xr.t[:, dst:dst + n], xr), V(pb.t[:, 0:n], pb))
            self.proj_fm(wbf_, ev_x)
            wst_, wbf_ = nw()
            self.load_w(1, [(512 + cb * 128, 128, 0)], wst_, wbf_)
            self.proj_fm(wbf_, lambda pb, t0, n: self.act(V(sgc.t[:, t0:t0 + n], sgc), V(pb.t[:, 0:n], pb), AF.Silu))
            hf = TB("hf%d" % cb, xr.t[:, 0:NT])
            prev_hf = hf

            def GV(lst, i, n):
                tb = lst[i % len(lst)]
                return V(tb.t[:, 0:n], tb)

            def rev(t0, n):
                return slice(t0 + n - 1, (t0 - 1) if t0 > 0 else None, -1)

            for d_ in range(2):
                order = list(range(len(COLT))) if d_ == 0 else [0] + list(range(len(COLT) - 1, 0, -1))

                def Cv(k, d_=d_, order=order):
                    if d_ == 1:
                        return
                    i = order[k]; t0, n = COLT[i]
                    b0 = t0 if t0 < 256 else t0 + 3
                    XC = V(xc.t[:, t0:t0 + n], xc, i)
                    self.ts("dve", XC, V(xr.t[:, b0:b0 + n], xr), V(cw.t[:, cb, 0:1], cw), V(cbias.t[:, cb:cb + 1], cbias), ALU.mult, ALU.add)
                    for j in range(1, 4):
                        self.stt(XC, V(xr.t[:, b0 + j:b0 + j + n], xr), V(cw.t[:, cb, j:j + 1], cw), XC, ALU.mult, ALU.add)
                    self.cp("act", V(xcb.t[:, t0:t0 + n], xcb, i), XC)

                def A0(k, d_=d_, order=order):
                    i = order[k]; t0, n = COLT[i]; off = 512 * (k % 2)
                    self.mm(V(self.PA.t[:, off:off + n], self.PA, off), V(wgb.t[:, (d_ * 2 + 0) * 4 + cb, :], wgb), V(xcb.t[:, t0:t0 + n], xcb, i), start=True, stop=True)
                    self.mm(V(self.PB.t[:, off:off + n], self.PB, off), V(wgb.t[:, (d_ * 2 + 1) * 4 + cb, :], wgb), V(xcb.t[:, t0:t0 + n], xcb, i), start=True, stop=True)

                def A1(k, d_=d_, order=order):
                    i = order[k]; t0, n = COLT[i]; off = 512 * (k % 2)
                    self.act(V(a_.t[:, t0:t0 + n], a_, i), V(self.PA.t[:, off:off + n], self.PA, off), AF.Sigmoid, bias=V(pba.t[:, d_, cb:cb + 1], pba))
                    self.act(V(bt.t[:, t0:t0 + n], bt, i), V(self.PB.t[:, off:off + n], self.PB, off), AF.Sigmoid, bias=V(pbx.t[:, d_, cb:cb + 1], pbx))

                def B0(k, d_=d_, order=order):
                    i = order[k]; t0, n = COLT[i]
                    GR = V(a_.t[:, t0:t0 + n], a_, i)
                    self.act(GV(g3, k, n), GR, AF.Exp, scale=V(c2A.t[:, d_, cb:cb + 1], c2A))
                    self.act(GR, GR, AF.Exp, scale=V(cA.t[:, d_, cb:cb + 1], cA))
                    self.tt("pool", GV(g2, k, n), V(bt.t[:, t0:t0 + n], bt, i), V(xc.t[:, t0:t0 + n], xc, i), ALU.mult)

                def B1(k, order=order):
                    i = order[k]; t0, n = COLT[i]
                    self.ts("dve", GV(g3, k, n), GV(g3, k, n), 0.99999994, -1.0, ALU.min, ALU.mult)

                def B2(k, order=order):
                    i = order[k]; t0, n = COLT[i]
                    self.act(GV(g3, k, n), GV(g3, k, n), AF.Ln, bias=self.one_col)
                    self.act(GV(g3, k, n), GV(g3, k, n), AF.Exp, scale=0.5)

                def B3(k, order=order):
                    i = order[k]; t0, n = COLT[i]
                    self.tt("dve", V(bt.t[:, t0:t0 + n], bt, i), GV(g3, k, n), GV(g2, k, n), ALU.mult)

                def SC(k, d_=d_, order=order):
                    i = order[k]; t0, n = COLT[i]
                    A_ = a_.t[:, t0:t0 + n]; B_ = bt.t[:, t0:t0 + n]
                    rd = [V(A_, a_, i), V(B_, bt, i)]
                    if d_ == 0:
                        if k == 0:
                            s.do("dve", lambda e: e.tensor_tensor_scan(out=hf.t[:, t0:t0 + n], data0=A_, data1=B_, initial=0.0, op0=ALU.mult, op1=ALU.add),
                                 w=[V(hf.t[:, t0:t0 + n], hf, i)], r=rd)
                        else:
                            ini = V(hf.t[:, t0 - 1:t0], hf, order[k - 1])
                            s.do("dve", lambda e: e.tensor_tensor_scan(out=hf.t[:, t0:t0 + n], data0=A_, data1=B_, initial=ini.ap, op0=ALU.mult, op1=ALU.add),
                                 w=[V(hf.t[:, t0:t0 + n], hf, i)], r=rd, rs=[ini])
                    else:
                        sl = rev(t0, n)
                        if k == 0:
                            s.do("dve", lambda e: e.tensor_tensor_scan(out=xc.t[:, sl], data0=a_.t[:, sl], data1=bt.t[:, sl], initial=0.0, op0=ALU.mult, op1=ALU.add),
                                 w=[V(xc.t[:, t0:t0 + n], xc, i)], r=rd)
                        else:
                            ip = order[k - 1]
                            p0 = 0 if k == 1 else COLT[ip][0]
                            ini = V(xc.t[:, p0:p0 + 1], xc, ip)
                            s.do("dve", lambda e: e.tensor_tensor_scan(out=xc.t[:, sl], data0=a_.t[:, sl], data1=bt.t[:, sl], initial=ini.ap, op0=ALU.mult, op1=ALU.add),
                                 w=[V(xc.t[:, t0:t0 + n], xc, i)], r=rd, rs=[ini])

                def Y(k, d_=d_, order=order):
                    if d_ == 0:
                        return
                    i = order[k]; t0, n = COLT[i]
                    st = yst[k % 2]
                    self.tt("dve", GV(g1, k, n), V(hf.t[:, t0:t0 + n], hf, i), V(xc.t[:, t0:t0 + n], xc, i), ALU.add)
                    self.tt("dve", V(st.t[:, 0:n], st), GV(g1, k, n), V(sgc.t[:, t0:t0 + n], sgc), ALU.mult)
                    self.y_store(cb, V(st.t[:, 0:n], st), t0, n)

                skew(len(COLT), [Cv, A0, A1], [0, 1, 2])
                skew(len(COLT), [B0, B1, B2, B3, SC, Y], [0, 1, 1, 2, 3, 4])

    def qk_norm_rope(self, pb, n, gcol, dst, d0, rope_t0, bufs, psn, psr):
        s = self.S
        sqb, rsb, knb, t1b = bufs
        SQ = V(sqb.t[:, 0:n], sqb); RS = V(rsb.t[:, 0:n], rsb); KN = V(knb.t[:, 0:n], knb); T1 = V(t1b.t[:, 0:n], t1b)
        self.act(SQ, V(pb.t[:, 0:n], pb), AF.Square)
        self.mm(V(psn.t[:, 0:n], psn), self.onesbd[:, :], SQ, start=True, stop=True)
        self.act(RS, V(psn.t[:, 0:n], psn), AF.Ln, scale=1.0 / 64, bias=self.eps_col)
        self.act(RS, RS, AF.Exp, scale=-0.5)
        if rope_t0 is None:
            self.stt(V(dst.t[:, d0:d0 + n], dst), V(pb.t[:, 0:n], pb), gcol, RS, ALU.mult, ALU.mult)
            return
        self.stt(KN, V(pb.t[:, 0:n], pb), gcol, RS, ALU.mult, ALU.mult)
        self.mm(V(psr.t[:, 0:n], psr), self.swapb[:, :], KN, start=True, stop=True)
        self.tt("pool", T1, KN, V(self.cos.t[:, rope_t0:rope_t0 + n], self.cos), ALU.mult)
        self.tt("dve", RS, V(psr.t[:, 0:n], psr), V(self.sin.t[:, rope_t0:rope_t0 + n], self.sin), ALU.mult)
        self.tt("dve", V(dst.t[:, d0:d0 + n], dst), T1, RS, ALU.add)

    def qk_proj_staged(self, wbf, colt, gcol, dst, dst_off, rope, nb):
        s = self.S
        pbs = [(self.PA, 0), (self.PA, 512), (self.PB, 0)]
        pns = [(self.PB, 512), (self.PCD, 0)]
        prs = [(self.PCD, 512), (self.PE_, 0)]
        n_t = len(colt)

        def pv(lst, i, n):
            tb, off = lst[i % len(lst)]
            return V(tb.t[:, off:off + n], tb, off)

        def bv(name, i, n):
            tb = nb[name][i % len(nb[name])]
            return V(tb.t[:, 0:n], tb)

        def P(i):
            t0, n = colt[i]
            for kb in range(8):
                self.mm(pv(pbs, i, n), V(wbf.t[:, kb, 0:128], wbf), V(self.xlT.t[:, kb, t0:t0 + n], self.xlT, None), start=(kb == 0), stop=(kb == 7))
            self.act(bv("sq", i, n), pv(pbs, i, n), AF.Square)

        def N1(i):
            t0, n = colt[i]
            self.mm(pv(pns, i, n), self.onesbd[:, :], bv("sq", i, n), start=True, stop=True)
            self.act(bv("rs", i, n), pv(pns, i, n), AF.Ln, scale=1.0 / 64, bias=self.eps_col)
            self.act(bv("rs", i, n), bv("rs", i, n), AF.Exp, scale=-0.5)

        def N2(i):
            t0, n = colt[i]
            d0 = t0 - dst_off
            if not rope or t0 < 256:
                self.stt(V(dst.t[:, d0:d0 + n], dst), pv(pbs, i, n), gcol, bv("rs", i, n), ALU.mult, ALU.mult)
                return
            self.stt(bv("kn", i, n), pv(pbs, i, n), gcol, bv("rs", i, n), ALU.mult, ALU.mult)
            self.mm(pv(prs, i, n), self.swapb[:, :], bv("kn", i, n), start=True, stop=True)
            r0 = t0 - 256
            self.tt("pool", bv("t1", i, n), bv("kn", i, n), V(self.cos.t[:, r0:r0 + n], self.cos), ALU.mult)

        def N3(i):
            t0, n = colt[i]
            d0 = t0 - dst_off
            if not rope or t0 < 256:
                return
            r0 = t0 - 256
            self.tt("dve", bv("rs", i, n), pv(prs, i, n), V(self.sin.t[:, r0:r0 + n], self.sin), ALU.mult)
            self.tt("dve", V(dst.t[:, d0:d0 + n], dst), bv("t1", i, n), bv("rs", i, n), ALU.add)

        skew(n_t, [P, N1, N2, N3], [0, 1, 2, 3])

    def phase_gqa(self):
        s = self.S
        self.new_phase()
        C = self.carve
        kT = C("kT", [128, NT], BF16)
        kT2 = C("kT2", [128, 2, NT], BF16)
        vaug = C("vaug", [128, NTILE, 2, 65], BF16)
        qT = C("qT", [128, NLAT], BF16)
        s.do("pool", lambda e: e.memset(kT2.t[:, :, :], 0.0), w=[kT2[:, :, :]])
        sgd = C("sgd", [128, NLAT], BF16)
        self.cos = C("cos", [128, NLAT], F32); self.sin = C("sin", [128, NLAT], F32)
        cst = C("cst", [128, 128], F32)
        self.swapb = C("swapb", [128, 128], BF16); self.onesbd = C("onesbd", [128, 128], BF16)
        gq = C("gq", [128, 1], F32); gk = C("gk", [128, 1], F32)
        nbufs = dict(sq=[C("sqb%d" % i, [128, 512], BF16) for i in range(3)], rs=[C("rsb%d" % i, [128, 512], F32) for i in range(4)],
                     kn=[C("knb%d" % i, [128, 512], BF16) for i in range(3)], t1=[C("t1b%d" % i, [128, 512], F32) for i in range(3)])
        Pb = [C("Pb%d" % i, [128, 1024], BF16) for i in range(4)]
        wst = C("wst", [128, 8, 128], F32); wbf = C("wbf", [128, 8, 128], BF16)
        ones_row = C("ones_row", [128, 64], F32); osbs = [C("osb%d" % i, [128, 512], F32) for i in range(2)]
        tmps = [C("tmp%d" % i, [128, 512], F32) for i in range(2)]; yst = [C("yst%d" % i, [128, 512], BF16) for i in range(2)]
        s.dma("sp", self.cos[:, :], self.cosT[:, :]); s.dma("act", self.sin[:, :], self.sinT[:, :])
        s.dma("sp", cst[:, :], self.c_swap[:, :]); self.cp("dve", self.swapb[:, :], cst[:, :])
        s.dma("sp", cst[:, :], self.c_onesbd[:, :]); self.cp("dve", self.onesbd[:, :], cst[:, :])
        s.dma("sp", gq[:, :], self.qn_pp[:, :]); s.dma("sp", gk[:, :], self.kn_pp[:, :])
        s.do("pool", lambda e: e.memset(ones_row.t[:, :], 1.0), w=[ones_row[:, :]])
        s.do("pool", lambda e: e.memset(vaug.t[:, :, :, 64:65], 1.0), w=[vaug[:, :, :, :]])
        self.load_w(1, [(1536, 128, 0)], wst, wbf)
        self.qk_proj_staged(wbf, COLT, gk[:, :], kT, 0, True, nbufs)
        self.cp("pool", V(kT2.t[0:64, 0, :], kT2), V(kT.t[0:64, :], kT))
        self.cp("pool", V(kT2.t[64:128, 1, :], kT2), V(kT.t[64:128, :], kT))
        self.load_w(1, [(1664, 128, 0)], wst, wbf)
        self.proj_tm(wbf, 128, lambda pb, t: self.cp("dve" if t % 2 else "act", V(vaug.t[:, t, :, 0:64], vaug),
                                                      V(pb.t[:, 0:128].rearrange("p (h d) -> p h d", h=2), pb)))
        LATT = COLT[1:]
        cnt = 0
        for a in range(self.debug.get("gqa_blocks", 4)):
            self.load_w(1, [(1024 + 64 * a, 64, 0), (1024 + 64 * (a + 4), 64, 64)], wst, wbf)
            self.qk_proj_staged(wbf, LATT, gq[:, :], qT, 256, True, nbufs)
            self.load_w(1, [(1792 + 64 * a, 64, 0), (1792 + 64 * (a + 4), 64, 64)], wst, wbf)
            self.proj_fm(wbf, lambda pb, t0, n: self.act(V(sgd.t[:, t0 - 256:t0 - 256 + n], sgd), V(pb.t[:, 0:n], pb), AF.Silu), colt=LATT)
            if "gqa_q" in self.debug:
                self.dump("d_q%d" % a, qT[:, :], [128, NLAT], BF16)
            PTf = TB("PTf", self.PT.t[:, :].bitcast(F32))
            PTf.parts = self.PT.parts; PTf.default = self.PT.default
            slots = [self.PA, self.PB, self.PCD]
            NS = len(slots)
            NP2 = NTILE // 2
            items = [(hh, qt, jp) for hh in range(2) for qt in range(8) for jp in range(NP2)]
            pO = self.PE_

            def S_(n):
                hh, qt, jp = items[n]
                pS = slots[n % NS]
                for h2 in range(2):
                    j = 2 * jp + h2
                    self.mm(V(pS.t[:, 512 * h2:512 * h2 + 512], pS), V(kT2.t[:, hh, j * 128:(j + 1) * 128], kT2), V(qT.t[:, qt * 512:qt * 512 + 512], qT), start=True, stop=True)

            def E_(n):
                pS = slots[n % NS]
                self.act(Pb[n % len(Pb)][:, :], V(pS.t[:, :], pS), AF.Exp, scale=0.125)

            def PV_(n):
                hh, qt, jp = items[n]
                blk = n // NP2
                for h2 in range(2):
                    j = 2 * jp + h2
                    self.mm(V(pO.t[0:65, 0:512], pO), V(vaug.t[:, j, hh, :], vaug), V(Pb[n % len(Pb)].t[:, 512 * h2:512 * h2 + 512], Pb[n % len(Pb)]),
                            start=(j == 0), stop=(j == NTILE - 1))
                if jp == NP2 - 1:
                    hd = a + 4 * hh
                    hp = slice(64 * hh, 64 * hh + 64)
                    q0 = qt * 512
                    osb = osbs[blk % 2]; tmp = tmps[blk % 2]; st = yst[blk % 2]
                    self.act(V(osb.t[64:65, :], osb), V(pO.t[64:65, 0:512], pO), AF.Ln)
                    self.tt("dve", V(tmp.t[0:64, :], tmp), V(pO.t[0:64, 0:512], pO), V(sgd.t[hp, q0:q0 + 512], sgd), ALU.mult)
                    self.act(V(osb.t[64:65, :], osb), V(osb.t[64:65, :], osb), AF.Exp, scale=-1.0)
                    kb = 4 + hd // 2; ph = hd % 2
                    tok0 = 256 + q0

                    def part2(osb=osb, tmp=tmp, st=st, kb=kb, ph=ph, tok0=tok0):
                        self.mm(V(PTf.t[0:64, 0:512], PTf), V(ones_row.t[64:65, 0:64], ones_row), V(osb.t[64:65, :], osb), start=True, stop=True)
                        self.tt("dve", V(st.t[64 * ph:64 * ph + 64, :], st), V(tmp.t[0:64, :], tmp), V(PTf.t[0:64, 0:512], PTf), ALU.mult)
                        s.do("pool", lambda e: e.dma_start(out=self.yscr.t[kb, 64 * ph:64 * ph + 64, tok0:tok0 + 512], in_=st.t[64 * ph:64 * ph + 64, :]),
                             w=[self.yscr.k(t)[:, :, :] for t in tiles_of(tok0, 512)], r=[st[:, :]], dma=True)
                    pending.append((n + 6, part2))

            def PVw(n):
                while pending and pending[0][0] <= n:
                    pending.pop(0)[1]()
                PV_(n)

            pending = []
            skew(len(items), [S_, E_, PVw], [0, 1, 2])
            while pending:
                pending.pop(0)[1]()


GRID_W = 64

def na_bias_table(rel_bias):
    cfgs = [(2, kp) for kp in range(0, 5)] + [(0, kp) for kp in range(4)] + [(1, kp) for kp in range(4)] + \
           [(30, kp) for kp in range(28, 32)] + [(31, kp) for kp in range(28, 32)]
    H = rel_bias.shape[0]
    tab = np.full((H, len(cfgs), 128, 128), -30000.0, np.float32)
    kidx = np.arange(128); qidx = np.arange(128)
    for ci, (rp, kp) in enumerate(cfgs):
        krow = 2 * kp + kidx // 64; kcol = kidx % 64
        qrow = 2 * rp + qidx // 64; qcol = qidx % 64
        rs = np.clip(qrow - 4, 0, 64 - 8); cs = np.clip(qcol - 8, 0, 64 - 16)
        KR, QR = np.meshgrid(krow, qrow, indexing='ij'); KC, QC = np.meshgrid(kcol, qcol, indexing='ij')
        RS = np.broadcast_to(rs[None, :], KR.shape); CS = np.broadcast_to(cs[None, :], KR.shape)
        valid = (KR >= RS) & (KR < RS + 8) & (KC >= CS) & (KC < CS + 16)
        dr = np.clip(KR - QR + 7, 0, 14); dc = np.clip(KC - QC + 15, 0, 30)
        g = rel_bias[:, dr, dc]
        tab[:, ci] = np.where(valid[None], g, np.float32(-30000.0))
    return tab

def consts():
    half = 32
    inv = (10000.0 ** (-np.arange(0, half, 2, dtype=np.float32) / half)).astype(np.float32)
    t = np.arange(4096)
    row = (t // GRID_W).astype(np.float32); col = (t % GRID_W).astype(np.float32)
    ang = np.concatenate([row[:, None] * inv, col[:, None] * inv], axis=-1).astype(np.float32)
    cos = np.cos(ang).astype(np.float32); sin = np.sin(ang).astype(np.float32)
    cosT = np.repeat(cos.T, 2, axis=0)
    sinT = np.repeat(sin.T, 2, axis=0)
    sgn = np.where(np.arange(64) % 2 == 0, -1.0, 1.0).astype(np.float32)[:, None]
    sinT = sinT * sgn
    cosT = np.concatenate([cosT, cosT], 0).astype(np.float32); sinT = np.concatenate([sinT, sinT], 0).astype(np.float32)
    swap = np.zeros((128, 128), np.float32)
    for k in range(128):
        swap[k, k ^ 1] = 1.0
    onesbd = np.zeros((128, 128), np.float32); onesbd[:64, :64] = 1; onesbd[64:, 64:] = 1
    iota = np.broadcast_to(np.arange(272, dtype=np.float32)[None, :], (64, 272)).copy()
    return dict(cosT=cosT, sinT=sinT, c_swap=swap, c_onesbd=onesbd, c_iota=iota)

def pp(v, nb):
    return np.ascontiguousarray(v.reshape(nb, 128).T)

def prep(inp, ncores=8):
    f = lambda a: np.ascontiguousarray(np.asarray(a, np.float32))
    shared = dict(
        ada_w=f(inp["ada_w"]), ada_b=f(inp["ada_b"]),
        ada_b_pp=f(np.stack([pp(inp["ada_b"][l], 24) for l in range(2)])),
        pre_g_pp=f(np.stack([pp(inp["pre_g"][l], 8) for l in range(2)])),
        post_g=f(inp["post_g"]),
        ev_w_in=f(inp["ev_w_in"][0]), od_w_in=f(inp["od_w_in"][0]), ev_w_out=f(inp["ev_w_out"][0]), od_w_out=f(inp["od_w_out"][0]),
        s5_lam_re=f(inp["s5_lam_re"][0].reshape(64, 64).T), s5_lam_im=f(inp["s5_lam_im"][0].reshape(64, 64).T), s5_log_dt=f(inp["s5_log_dt"][0].reshape(1, 64)),
        s5_b_re=f(inp["s5_b_re"][0]), s5_b_im=f(inp["s5_b_im"][0]), s5_c_re=f(inp["s5_c_re"][0]), s5_c_im=f(inp["s5_c_im"][0]),
        s5_d_pp=f(pp(inp["s5_d"][0], 4)), s5_w_glu=f(inp["s5_w_glu"][0]), s5_b_glu_pp=f(pp(inp["s5_b_glu"][0], 4)),
        na_bias=na_bias_table(np.asarray(inp["na_rel_bias"][0], np.float32)),
        lru_conv_w_pp=f(np.stack([pp(inp["lru_conv_w"][0][j], 4) for j in range(4)], axis=-1)),
        lru_conv_b_pp=f(pp(inp["lru_conv_b"][0], 4)),
        lru_lam_pp=f(np.stack([pp(inp["lru_lam"][0][d], 4) for d in range(2)], axis=1)),
        lru_w_a=f(inp["lru_w_a"][0]), lru_w_x=f(inp["lru_w_x"][0]),
        lru_b_a_pp=f(np.stack([pp(inp["lru_b_a"][0][d], 4) for d in range(2)], axis=1)),
        lru_b_x_pp=f(np.stack([pp(inp["lru_b_x"][0][d], 4) for d in range(2)], axis=1)),
        qn_pp=f(np.tile(inp["gqa_q_norm"][0], 2).reshape(128, 1)), kn_pp=f(np.tile(inp["gqa_k_norm"][0], 2).reshape(128, 1)),
    )
    shared.update(consts())
    maps = []
    for i in range(ncores):
        b = i % 4
        m = dict(shared)
        m["xs"] = f(np.concatenate([inp["ctx"][b], inp["x"][b]], axis=0))
        cT = np.stack([pp(inp["c"][b], 8), pp(inp["c_ctx"], 8)], axis=-1)
        m["cT"] = f(cT)
        maps.append(m)
    return maps


def build_program():
    k = KB()
    k.phase_consts(); k.phase_adaln(); k.phase_prologue0()
    k.phase_s5(); k.phase_glu(); k.phase_na(); k.phase_outproj(0)
    k.phase_lru(); k.phase_gqa(); k.phase_outproj(1)
    return k.finish()


def kernel(**inputs):
    inp = {k_: np.asarray(v) for k_, v in inputs.items()}
    maps = prep(inp, 8)
    nc = build_program()
    res = run_bass_kernel_spmd(nc, maps, core_ids=list(range(8)))
    out = np.stack([np.asarray(res.results[b]["out"], dtype=np.float32) for b in range(4)], axis=0)
    return out
```

```python
import math
from concourse.bass_utils import run_bass_kernel_spmd
import numpy as np
from contextlib import ExitStack
import concourse.bass as bass
import concourse.mybir as mybir

F32 = mybir.dt.float32
BF16 = mybir.dt.bfloat16
I32 = mybir.dt.int32
AF = mybir.ActivationFunctionType
ALU = mybir.AluOpType
AX = mybir.AxisListType

SAME_ENGINE_SYNC = True
N_DMA_SEMS = 12
FULL_SAME_SYNC = True


class _St:
    __slots__ = ("w", "r")

    def __init__(self):
        self.w = None
        self.r = {}


class V:
    __slots__ = ("ap", "buf", "key")

    def __init__(self, ap, buf, key=None):
        self.ap = ap
        self.buf = buf
        self.key = key


class _Keyed:
    def __init__(self, tb, key):
        self.tb = tb
        self.key = key

    def __getitem__(self, idx):
        return V(self.tb.t[idx], self.tb, self.key)


class TB:
    def __init__(self, name, t):
        self.name = name
        self.t = t
        self.parts = {}
        self.default = _St()

    def k(self, key=None):
        return _Keyed(self, key)

    def __getitem__(self, idx):
        return V(self.t[idx], self, None)

    def states(self, key):
        if key is None:
            return [self.default] + list(self.parts.values())
        if key not in self.parts:
            s = _St()
            s.w = self.default.w
            s.r = dict(self.default.r)
            self.parts[key] = s
        return [self.parts[key]]


class SubTB(TB):
    def __init__(self, name, parent, pkey, ap):
        self.name = name
        self.t = ap
        self.parent = parent
        self.pkey = pkey

    def states(self, key):
        return self.parent.states(self.pkey)


class Rec:
    __slots__ = ("eng", "idx", "fn", "dma", "deps", "raw", "rawsc", "accum", "signal", "semval", "dsem", "name")

    def __init__(self, eng, idx, fn, dma, name):
        self.eng = eng
        self.idx = idx
        self.fn = fn
        self.dma = dma
        self.deps = set()
        self.raw = set()
        self.rawsc = set()
        self.accum = False
        self.signal = False
        self.semval = None
        self.dsem = None
        self.name = name


ENGS = ["pe", "act", "dve", "pool", "sp"]


def need_same(rec, d):
    if not SAME_ENGINE_SYNC:
        return False
    if rec.eng == "pe":
        return False
    if d not in rec.raw:
        return False
    if FULL_SAME_SYNC or rec.eng == "pool":
        return True
    return (d in rec.rawsc) or d.accum


class Sched:
    def __init__(self, nc):
        self.nc = nc
        self.q = {e: [] for e in ENGS}
        self.es = ExitStack()
        self.n_dma = {e: 0 for e in ENGS}

    def sbuf(self, name, shape, dtype):
        t = self.es.enter_context(self.nc.sbuf_tensor(name, list(shape), dtype))
        return TB(name, t)

    def psum(self, name, shape, dtype):
        t = self.es.enter_context(self.nc.psum_tensor(name, list(shape), dtype))
        return TB(name, t)

    def dram(self, name, shape, dtype, kind):
        t = self.nc.dram_tensor(name, list(shape), dtype, kind=kind)
        return TB(name, t.ap())

    def do(self, eng, fn, w=(), r=(), dma=False, name="", rs=(), accum=False):
        rec = Rec(eng, len(self.q[eng]), fn, dma, name)
        rec.accum = accum
        for v in rs:
            for st in v.buf.states(v.key):
                if st.w is not None:
                    rec.rawsc.add(st.w)
        r = list(r) + list(rs)
        for v in r:
            for st in v.buf.states(v.key):
                if st.w is not None:
                    rec.deps.add(st.w)
                    rec.raw.add(st.w)
        for v in w:
            for st in v.buf.states(v.key):
                if st.w is not None:
                    rec.deps.add(st.w)
                for rr in st.r.values():
                    rec.deps.add(rr)
        for v in r:
            for st in v.buf.states(v.key):
                if dma:
                    st.r[("dma", eng, rec.idx)] = rec
                else:
                    st.r[eng] = rec
        for v in w:
            for st in v.buf.states(v.key):
                st.w = rec
                st.r = {}
        rec.deps.discard(rec)
        self.q[eng].append(rec)
        return rec

    def dma(self, eng, out, in_, **kw):
        return self.do(eng, lambda e: e.dma_start(out=out.ap, in_=in_.ap, **kw), w=[out], r=[in_], dma=True)

    def finalize(self):
        nc = self.nc
        for e in ENGS:
            for rec in self.q[e]:
                for d in rec.deps:
                    if d.dma:
                        continue
                    if d.eng == rec.eng and not rec.dma and not need_same(rec, d):
                        continue
                    d.signal = True
        sems = {e: self.es.enter_context(nc.semaphore("s_" + e)) for e in ENGS}
        dsems = {e: [self.es.enter_context(nc.semaphore("d_%s%d" % (e, i))) for i in range(N_DMA_SEMS)]
                 for e in ENGS if any(r.dma for r in self.q[e])}
        for e in ENGS:
            cnt = 0
            nd = 0
            for rec in self.q[e]:
                if rec.dma:
                    rec.dsem = (dsems[e][nd % N_DMA_SEMS], 16 * (nd // N_DMA_SEMS + 1), nd)
                    nd += 1
                elif rec.signal:
                    cnt += 1
                    rec.semval = cnt
        block = self.es.enter_context(nc.Block())
        sched = self

        def run(e, engobj):
            waited = {}
            for rec in sched.q[e]:
                need = {}
                for d in rec.deps:
                    if d.dma:
                        s, val, _ = d.dsem
                        key = ("d", d.eng, id(s))
                        if need.get(key, (None, 0))[1] < val:
                            need[key] = (s, val)
                    else:
                        if d.eng == e and not rec.dma and not need_same(rec, d):
                            continue
                        key = ("c", d.eng)
                        if need.get(key, (None, 0))[1] < d.semval:
                            need[key] = (sems[d.eng], d.semval)
                if rec.dma:
                    s, val, nd = rec.dsem
                    if nd >= N_DMA_SEMS:
                        key = ("d", e, id(s))
                        pv = val - 16
                        if need.get(key, (None, 0))[1] < pv:
                            need[key] = (s, pv)
                for key, (s, val) in need.items():
                    if waited.get(key, 0) >= val:
                        continue
                    waited[key] = val
                    engobj.wait_ge(s, val)
                ins = rec.fn(engobj)
                if rec.dma:
                    ins.then_inc(rec.dsem[0], 16)
                elif rec.signal:
                    ins.then_inc(sems[e], 1)

        names = {"pe": "tensor", "act": "scalar", "dve": "vector", "pool": "gpsimd", "sp": "sync"}
        for e in ENGS:
            if not self.q[e]:
                continue

            def mk(e):
                def f(engobj):
                    run(e, engobj)
                return f
            getattr(block, names[e])(mk(e))
        self.es.close()


D = 1024
NCTX = 256
NLAT = 4096
NT = NCTX + NLAT
NTILE = NT // 128
EPS = 1e-6
COLT = [(0, 256)] + [(256 + 512 * i, 512) for i in range(8)]
TWO_PI = 2.0 * math.pi
TP_HI = 6.28125
TP_LO = TWO_PI - 6.28125


def unsq(ap):
    nd = len(ap.shape)
    names = ["d%d" % i for i in range(nd)]
    src = " ".join(names[:-1] + ["(%s o)" % names[-1]])
    dst = " ".join(names + ["o"])
    return ap.rearrange("%s -> %s" % (src, dst), o=1)


def skew(n, stages, lags):
    for step in range(n + max(lags)):
        for st, lag in zip(stages, lags):
            i = step - lag
            if 0 <= i < n:
                st(i)


def tiles_of(t0, n):
    return list(range(t0 // 128, (t0 + n + 127) // 128))


class KB:
    def __init__(self, debug=None):
        self.debug = debug or {}
        nc = bass.Bass("TRN2", target_bir_lowering=False)
        self.nc = nc
        S = Sched(nc)
        self.S = S
        self.dbg_outs = []
        di = lambda name, shape, dt=F32: S.dram(name, shape, dt, "ExternalInput")
        self.xs = di("xs", [NT, D])
        self.cT = di("cT", [128, 8, 2])
        self.ada_w = di("ada_w", [2, D, 3 * D])
        self.ada_b = di("ada_b", [2, 3 * D])
        self.ada_b_pp = di("ada_b_pp", [2, 128, 24])
        self.pre_g_pp = di("pre_g_pp", [2, 128, 8])
        self.post_g = di("post_g", [2, D])
        self.w_in = [di("ev_w_in", [D, 3072]), di("od_w_in", [D, 2304])]
        self.w_out = [di("ev_w_out", [D, D]), di("od_w_out", [D, D])]
        self.s5_lam_re = di("s5_lam_re", [64, 64])
        self.s5_lam_im = di("s5_lam_im", [64, 64])
        self.s5_log_dt = di("s5_log_dt", [1, 64])
        self.s5_b_re = di("s5_b_re", [2, 32, 64, 16])
        self.s5_b_im = di("s5_b_im", [2, 32, 64, 16])
        self.s5_c_re = di("s5_c_re", [2, 32, 16, 64])
        self.s5_c_im = di("s5_c_im", [2, 32, 16, 64])
        self.s5_d_pp = di("s5_d_pp", [128, 4])
        self.s5_w_glu = di("s5_w_glu", [512, 512])
        self.s5_b_glu_pp = di("s5_b_glu_pp", [128, 4])
        self.na_bias = di("na_bias", [8, 21, 128, 128])
        self.lru_conv_w_pp = di("lru_conv_w_pp", [128, 4, 4])
        self.lru_conv_b_pp = di("lru_conv_b_pp", [128, 4])
        self.lru_lam_pp = di("lru_lam_pp", [128, 2, 4])
        self.lru_w_a = di("lru_w_a", [2, 8, 64, 64])
        self.lru_b_a_pp = di("lru_b_a_pp", [128, 2, 4])
        self.lru_w_x = di("lru_w_x", [2, 8, 64, 64])
        self.lru_b_x_pp = di("lru_b_x_pp", [128, 2, 4])
        self.qn_pp = di("qn_pp", [128, 1])
        self.kn_pp = di("kn_pp", [128, 1])
        self.cosT = di("cosT", [128, NLAT])
        self.sinT = di("sinT", [128, NLAT])
        self.c_swap = di("c_swap", [128, 128])
        self.c_onesbd = di("c_onesbd", [128, 128])
        self.c_iota = di("c_iota", [64, 272])
        self.out = S.dram("out", [NLAT, D], F32, "ExternalOutput")
        self.hscr = S.dram("hscr", [NT, D], F32, "Internal")
        self.xlT = S.sbuf("xlT", [128, 8, NT], BF16)
        self.yscr = S.dram("yscr", [8, 128, NT], BF16, "Internal")
        self.ident_f = S.sbuf("ident_f", [128, 128], F32)
        self.ident_b = S.sbuf("ident_b", [128, 128], BF16)
        self.modA = S.sbuf("modA", [128, 2, 8, 2], F32)
        self.modB = S.sbuf("modB", [128, 2, 8, 2], F32)
        self.GP = S.sbuf("GP", [128, 2, 2, D], BF16)
        self.small = S.sbuf("small", [128, 64], F32)
        self.RSZ = 130 * 1024
        self.R = S.sbuf("R", [128, self.RSZ // 4], F32)
        self.PA = S.psum("PA", [128, 1024], F32)
        self.PB = S.psum("PB", [128, 1024], F32)
        self.PCD = S.psum("PCD", [128, 1024], F32)
        self.PC = SubTB("PC", self.PCD, 0, self.PCD.t[:, 0:512])
        self.PD = SubTB("PD", self.PCD, 1, self.PCD.t[:, 512:1024])
        self.PE_ = S.psum("PE", [128, 512], F32)
        self.PT = S.psum("PT", [128, 1024], BF16)
        self._carve_off = 0
        self._phase = 0
        self._bar_idx = {}
        self.eps_col = self.small.k(60)[:, 60:61]
        self.one_col = self.small.k(61)[:, 61:62]

    def barrier(self):
        S = self.S
        lasts = []
        for e in ENGS:
            comp = [r for r in S.q[e] if not r.dma]
            if comp:
                lasts.append(comp[-1])
            lasts += [r for r in S.q[e][self._bar_idx.get(e, 0):] if r.dma]
        for e in ENGS:
            rec = S.do(e, lambda en: en.nop())
            for l in lasts:
                rec.deps.add(l)
        for e in ENGS:
            self._bar_idx[e] = len(S.q[e])

    def new_phase(self):
        self.barrier()
        self._carve_off = 0
        self._phase += 1

    def carve(self, name, shape, dtype):
        esz = 4 if dtype in (F32, I32) else 2
        n = 1
        for s in shape[1:]:
            n *= s
        nbytes = (n * esz + 15) // 16 * 16
        off = self._carve_off
        assert off + nbytes <= self.RSZ, (name, off, nbytes)
        self._carve_off += nbytes
        ap = self.R.t[:, off // 4:(off + nbytes) // 4]
        if dtype != F32:
            ap = ap.bitcast(dtype)
        ap = ap[:, 0:n]
        if len(shape) > 2:
            names = " ".join("a%d" % i for i in range(len(shape) - 1))
            kw = {"a%d" % i: shape[i + 1] for i in range(len(shape) - 1)}
            ap = ap.rearrange("p (%s) -> p %s" % (names, names), **kw)
        return TB("%s_%d" % (name, self._phase), ap)

    def dump(self, name, view, shape, dtype=F32):
        o = self.S.dram(name, shape, dtype, "ExternalOutput")
        self.S.dma("sp", o[tuple(slice(None) for _ in shape)], view)
        self.dbg_outs.append(o)

    def finish(self):
        S = self.S
        allouts = [self.out] + self.dbg_outs
        S.do("sp", lambda e: e.nop(), r=[V(o.t, o) for o in allouts])
        S.finalize()
        return self.nc

    def xl_views(self, t0, n):
        return [self.xlT.k(t)[:, :, :] for t in tiles_of(t0, n)]

    def y_store(self, kb, src, t0, n, eng="pool"):
        for t in tiles_of(t0, n):
            pass
        self.S.do(eng, lambda e: e.dma_start(out=self.yscr.t[kb, :, t0:t0 + n], in_=src.ap),
                  w=[self.yscr.k(t)[:, :, :] for t in tiles_of(t0, n)], r=[src], dma=True)

    def phase_consts(self):
        S = self
        s = self.S
        s.do("pool", lambda e: e.memset(self.ident_f.t[:, :], 1.0), w=[self.ident_f[:, :]])
        s.do("pool", lambda e: e.affine_select(out=self.ident_f.t[:, :], in_=self.ident_f.t[:, :], pattern=[[-1, 128]],
                                                compare_op=ALU.is_equal, fill=0.0, base=0, channel_multiplier=1),
             w=[self.ident_f[:, :]], r=[self.ident_f[:, :]])
        s.do("pool", lambda e: e.tensor_copy(out=self.ident_b.t[:, :], in_=self.ident_f.t[:, :]), w=[self.ident_b[:, :]], r=[self.ident_f[:, :]])
        s.do("pool", lambda e: e.memset(self.small.t[:, 60:61], EPS), w=[self.eps_col])
        s.do("pool", lambda e: e.memset(self.small.t[:, 61:62], 1.0), w=[self.one_col])

    def phase_adaln(self):
        s = self.S
        self.new_phase()
        cT = self.carve("cT", [128, 8, 2], F32)
        sc = self.carve("sc", [128, 8, 2], F32)
        scb = self.carve("scb", [128, 8, 2, 128], F32)
        abpp = self.carve("abpp", [128, 2, 24], F32)
        pgpp = self.carve("pgpp", [128, 2, 8], F32)
        mod = self.carve("mod", [128, 16, 2], F32)
        abrow = self.carve("abrow", [128, D], F32)
        pgrow = self.carve("pgrow", [128, D], F32)
        wbuf = [self.carve("adaw%d" % i, [128, 8, 512], F32) for i in range(6)]
        gptmp = self.carve("gptmp", [128, 512], F32)
        s.dma("sp", cT[:, :, :], self.cT[:, :, :])
        s.dma("sp", abpp[:, :, :], V(self.ada_b_pp.t.rearrange("l p f -> p l f"), self.ada_b_pp))
        s.dma("sp", pgpp[:, :, :], V(self.pre_g_pp.t.rearrange("l p f -> p l f"), self.pre_g_pp))
        s.do("act", lambda e: e.activation(out=sc.t[:, :, :], in_=cT.t[:, :, :], func=AF.Silu), w=[sc[:, :, :]], r=[cT[:, :, :]])
        s.do("dve", lambda e: e.tensor_copy(out=scb.t[:, :, :, :], in_=unsq(sc.t[:, :, :]).to_broadcast([128, 8, 2, 128])),
             w=[scb[:, :, :, :]], r=[sc[:, :, :]])
        ci = 0
        for l in range(2):
            s.dma("sp", abrow[:, :], V(self.ada_b.t[l:l + 1, 2 * D:3 * D].partition_broadcast(128), self.ada_b))
            s.dma("sp", pgrow[:, :], V(self.post_g.t[l:l + 1, :].partition_broadcast(128), self.post_g))
            for ch in range(6):
                wb = wbuf[ci % 6]
                ci += 1
                src = self.ada_w.t[l].rearrange("(kb p) n -> p kb n", p=128)[:, :, ch * 512:(ch + 1) * 512]
                s.dma("sp", wb[:, :, :], V(src, self.ada_w))
                if ch < 4:
                    for f in range(4):
                        fb = ch * 4 + f
                        for kb in range(8):
                            s.do("pe", lambda e, wb=wb, f=f, kb=kb, fb=fb: e.matmul(
                                out=self.PC.t[:, fb * 2:fb * 2 + 2], lhsT=wb.t[:, kb, f * 128:(f + 1) * 128], rhs=sc.t[:, kb, :],
                                start=(kb == 0), stop=(kb == 7)), w=[self.PC[:, :]], r=[wb[:, :, :], sc[:, :, :]])
                else:
                    half = ch - 4
                    for j in range(2):
                        pj = self.PD if j == 0 else self.PE_
                        for kb in range(8):
                            s.do("pe", lambda e, wb=wb, kb=kb, j=j, pj=pj: e.matmul(
                                out=pj.t[:, :], lhsT=scb.t[:, kb, j, :], rhs=wb.t[:, kb, :],
                                start=(kb == 0), stop=(kb == 7)), w=[pj[:, :]], r=[wb[:, :, :], scb[:, :, :, :]])
                        gp = self.GP.t[:, l, j, half * 512:(half + 1) * 512]
                        s.do("dve", lambda e, pj=pj, gp=gp, half=half: e.tensor_tensor(
                            out=gptmp.t[:, :], in0=pj.t[:, :], in1=abrow.t[:, half * 512:(half + 1) * 512], op=ALU.add),
                            w=[gptmp[:, :]], r=[pj[:, :], abrow[:, :]])
                        s.do("dve", lambda e, gp=gp, half=half: e.tensor_tensor(
                            out=gp, in0=gptmp.t[:, :], in1=pgrow.t[:, half * 512:(half + 1) * 512], op=ALU.mult),
                            w=[self.GP[:, :, :, :]], r=[pgrow[:, :], gptmp[:, :]])
                if ch == 3:
                    s.do("dve", lambda e, l=l: e.tensor_tensor(
                        out=mod.t[:, :, :], in0=self.PC.t[:, 0:32].rearrange("p (f j) -> p f j", j=2),
                        in1=unsq(abpp.t[:, l, 0:16]).to_broadcast([128, 16, 2]), op=ALU.add),
                        w=[mod[:, :, :]], r=[self.PC[:, :], abpp[:, :, :]])
                    s.do("dve", lambda e, l=l: e.tensor_copy(out=self.modB.t[:, l, :, :], in_=mod.t[:, 0:8, :]),
                         w=[self.modB[:, :, :, :]], r=[mod[:, :, :]])
                    s.do("dve", lambda e, l=l: e.tensor_scalar(out=mod.t[:, 8:16, :], in0=mod.t[:, 8:16, :], scalar1=1.0, scalar2=None, op0=ALU.add),
                         w=[mod[:, :, :]], r=[mod[:, :, :]])
                    s.do("dve", lambda e, l=l: e.tensor_tensor(
                        out=self.modA.t[:, l, :, :], in0=mod.t[:, 8:16, :], in1=unsq(pgpp.t[:, l, :]).to_broadcast([128, 8, 2]), op=ALU.mult),
                        w=[self.modA[:, :, :, :]], r=[mod[:, :, :], pgpp[:, :, :]])

    def norm_a(self, xv, scratch_bf, stat_col):
        s = self.S
        ss = self.small.k(stat_col)[:, stat_col:stat_col + 1]
        rs = self.small.k(stat_col + 1)[:, stat_col + 1:stat_col + 2]
        s.do("act", lambda e: e.activation(out=scratch_bf.t[:, :], in_=xv.ap, func=AF.Square, accum_out=ss.ap),
             w=[scratch_bf[:, :], ss], r=[xv], accum=True)
        s.do("act", lambda e: e.activation(out=rs.ap, in_=ss.ap, func=AF.Sqrt, scale=1.0 / D, bias=self.eps_col.ap),
             w=[rs], r=[ss], rs=[self.eps_col])
        s.do("dve", lambda e: e.reciprocal(out=rs.ap, in_=rs.ap), w=[rs], r=[rs])
        s.do("dve", lambda e: e.tensor_scalar(out=scratch_bf.t[:, :], in0=xv.ap, scalar1=rs.ap, scalar2=None, op0=ALU.mult),
             w=[scratch_bf[:, :]], r=[xv], rs=[rs])

    def norm_b(self, l, t, scratch_bf, tmpf):
        s = self.S
        j = 0 if t >= 2 else 1
        for kb in range(8):
            s.do("pe", lambda e, kb=kb: e.transpose(out=self.PT.t[:, kb * 128:(kb + 1) * 128], in_=scratch_bf.t[:, kb * 128:(kb + 1) * 128],
                                                   identity=self.ident_b.t[:, :]),
                 w=[self.PT[:, :]], r=[scratch_bf[:, :], self.ident_b[:, :]])
        pt3 = self.PT.t[:, :].rearrange("p (k n) -> p k n", n=128)
        s.do("dve", lambda e: e.tensor_tensor(out=tmpf.t[:, :].rearrange("p (k n) -> p k n", n=128), in0=pt3,
                                              in1=self.modA.t[:, l, :, j:j + 1].to_broadcast([128, 8, 128]), op=ALU.mult),
             w=[tmpf[:, :]], r=[self.PT[:, :], self.modA[:, :, :, :]])
        s.do("pool", lambda e: e.tensor_tensor(out=self.xlT.t[:, :, t * 128:(t + 1) * 128], in0=tmpf.t[:, :].rearrange("p (k n) -> p k n", n=128),
                                               in1=self.modB.t[:, l, :, j:j + 1].to_broadcast([128, 8, 128]), op=ALU.add),
             w=[self.xlT.k(t)[:, :, :]], r=[tmpf[:, :], self.modB[:, :, :, :]])

    def norm_to_xlT(self, l, xv, t, scratch_bf, tmpf, stat_col):
        self.norm_a(xv, scratch_bf, stat_col)
        self.norm_b(l, t, scratch_bf, tmpf)

    def phase_prologue0(self):
        s = self.S
        self.new_phase()
        xb = [self.carve("xin%d" % i, [128, D], F32) for i in range(4)]
        sb = [self.carve("xsq%d" % i, [128, D], BF16) for i in range(3)]
        tf = [self.carve("xtf%d" % i, [128, D], F32) for i in range(2)]

        def L(t):
            s.dma("sp", xb[t % 4][:, :], V(self.xs.t[t * 128:(t + 1) * 128, :], self.xs))

        def A(t):
            self.norm_a(xb[t % 4][:, :], sb[t % 3], 2 * (t % 2))

        def B(t):
            self.norm_b(0, t, sb[t % 3], tf[t % 2])

        skew(NTILE, [L, A, B], [0, 1, 2])

    def load_w(self, l, specs, wst, wbf, eng="pool"):
        s = self.S
        wsrc = self.w_in[l].t.rearrange("(kb p) n -> p kb n", p=128)
        tot = 0
        for (c0, n, d0) in specs:
            s.dma("sp", V(wst.t[:, :, d0:d0 + n], wst), V(wsrc[:, :, c0:c0 + n], self.w_in[l]))
            tot = max(tot, d0 + n)
        s.do(eng, lambda e: e.tensor_copy(out=wbf.t[:, :, 0:tot], in_=wst.t[:, :, 0:tot]), w=[wbf[:, :, :]], r=[wst[:, :, :]])

    def proj_fm(self, wbf, evac, colt=COLT, banks=None, mcols=128):
        s = self.S
        banks = banks or [self.PC, self.PD, self.PE_]
        for i, (t0, n) in enumerate(colt):
            pb = banks[i % len(banks)]
            for kb in range(8):
                s.do("pe", lambda e, pb=pb, kb=kb, t0=t0, n=n: e.matmul(
                    out=pb.t[0:mcols, 0:n], lhsT=wbf.t[:, kb, 0:mcols], rhs=self.xlT.t[:, kb, t0:t0 + n],
                    start=(kb == 0), stop=(kb == 7)), w=[pb[:, :]], r=[wbf[:, :, :]] + self.xl_views(t0, n))
            evac(pb, t0, n)

    def proj_tm(self, wbf, ncols, evac, tiles=range(NTILE), banks=None):
        s = self.S
        banks = banks or [self.PC, self.PD, self.PE_]
        for i, t in enumerate(tiles):
            pb = banks[i % len(banks)]
            for kb in range(8):
                s.do("pe", lambda e, pb=pb, kb=kb, t=t: e.matmul(
                    out=pb.t[:, 0:ncols], lhsT=self.xlT.t[:, kb, t * 128:(t + 1) * 128], rhs=wbf.t[:, kb, 0:ncols],
                    start=(kb == 0), stop=(kb == 7)), w=[pb[:, :]], r=[wbf[:, :, :], self.xlT.k(t)[:, :, :]])
            evac(pb, t)

    def tt(self, eng, out, in0, in1, op, extra_r=()):
        self.S.do(eng, lambda e: e.tensor_tensor(out=out.ap, in0=in0.ap, in1=in1.ap, op=op), w=[out], r=[in0, in1] + list(extra_r))

    def ts(self, eng, out, in0, s1, s2, op0, op1=None, extra_r=()):
        r = [in0] + list(extra_r)
        rs = [x for x in (s1, s2) if isinstance(x, V)]
        a1 = s1.ap if isinstance(s1, V) else s1
        a2 = s2.ap if isinstance(s2, V) else s2
        if op1 is None:
            self.S.do(eng, lambda e: e.tensor_scalar(out=out.ap, in0=in0.ap, scalar1=a1, scalar2=None, op0=op0), w=[out], r=r, rs=rs)
        else:
            self.S.do(eng, lambda e: e.tensor_scalar(out=out.ap, in0=in0.ap, scalar1=a1, scalar2=a2, op0=op0, op1=op1), w=[out], r=r, rs=rs)

    def stt(self, out, in0, sc, in1, op0, op1):
        r = [in0, in1]
        rs = [sc] if isinstance(sc, V) else []
        a = sc.ap if isinstance(sc, V) else sc
        self.S.do("dve", lambda e: e.scalar_tensor_tensor(out=out.ap, in0=in0.ap, scalar=a, in1=in1.ap, op0=op0, op1=op1), w=[out], r=r, rs=rs)

    def act(self, out, in_, func, scale=1.0, bias=None, extra_w=(), accum=None):
        r = [in_]
        rs = [x for x in (scale, bias) if isinstance(x, V)]
        sc = scale.ap if isinstance(scale, V) else scale
        kw = {}
        if bias is not None:
            kw["bias"] = bias.ap if isinstance(bias, V) else bias
        w = [out]
        if accum is not None:
            kw["accum_out"] = accum.ap
            w.append(accum)
        self.S.do("act", lambda e: e.activation(out=out.ap, in_=in_.ap, func=func, scale=sc, **kw), w=w, r=r, rs=rs, accum=(accum is not None))

    def cp(self, eng, out, in_):
        if eng == "act":
            self.S.do("act", lambda e: e.copy(out=out.ap, in_=in_.ap), w=[out], r=[in_])
        else:
            self.S.do(eng, lambda e: e.tensor_copy(out=out.ap, in_=in_.ap), w=[out], r=[in_])

    def mm(self, out, lhsT, rhs, start, stop, **kw):
        self.S.do("pe", lambda e: e.matmul(out=out.ap, lhsT=lhsT.ap, rhs=rhs.ap, start=start, stop=stop, **kw), w=[out], r=[lhsT, rhs])

    def capture(self, fn):
        calls = []
        S = self.S
        orig = S.do
        S.do = lambda *a, **k: calls.append((a, k))
        try:
            fn()
        finally:
            S.do = orig
        return calls

    def interleave(self, chains):
        n = max(len(c) for c in chains)
        for k in range(n):
            for c in chains:
                if k < len(c):
                    self.S.do(*c[k][0], **c[k][1])

    def angle_sincos(self, ang, sn, cs, kf, ki, shape):
        self.ts("dve", kf, ang, 1.0 / TWO_PI, None, ALU.mult)
        self.cp("dve", ki, kf)
        self.cp("dve", kf, ki)
        self.stt(ang, kf, -TP_HI, ang, ALU.mult, ALU.add)
        self.stt(ang, kf, -TP_LO, ang, ALU.mult, ALU.add)
        self.ts("dve", ang, ang, math.pi, -math.pi, ALU.min, ALU.max)
        self.act(sn, ang, AF.Sin)
        self.ts("dve", kf, ang, math.pi / 2, None, ALU.add)
        self.ts("dve", cs, kf, math.pi, -TWO_PI, ALU.is_gt, ALU.mult)
        self.tt("dve", kf, kf, cs, ALU.add)
        self.ts("dve", kf, kf, math.pi, -math.pi, ALU.min, ALU.max)
        self.act(cs, kf, AF.Sin)

    def phase_s5(self):
        s = self.S
        self.new_phase()
        C = self.carve
        P = 128
        PH = 64
        lr = C("lr", [128, 64], F32); li = C("li", [128, 64], F32); dt = C("dt", [128, 64], F32)
        zr = C("zr", [128, 64], F32); zi = C("zi", [128, 64], F32)
        rho16 = C("rho16", [128, 64], F32); phi = C("phi", [128, 64], F32)
        lbr = C("lbr", [128, 64], F32); lbi = C("lbi", [128, 64], F32)
        kr_ = C("kr", [128, 64], F32); ki_ = C("ki", [128, 64], F32)
        t1 = C("t1", [128, 64], F32); t2 = C("t2", [128, 64], F32); t3 = C("t3", [128, 64], F32)
        kfs = C("kfs", [128, 64], F32); kis = C("kis", [128, 64], I32)
        Er = C("Er", [128, 17, 64], F32); Ei = C("Ei", [128, 17, 64], F32)
        Akr = C("Akr", [128, 16, 64], F32); Aki = C("Aki", [128, 16, 64], F32)
        bre = C("bre", [128, 64, 16], BF16); bim = C("bim", [128, 64, 16], BF16)
        CTr = C("CTr", [128, 2, 4, 128], BF16); CTi = C("CTi", [128, 2, 4, 128], BF16)
        wst = C("wst", [128, 8, 128], F32); wbf = C("wbf", [128, 8, 128], BF16)
        cst = C("cst", [128, 64], F32)
        iota = C("iota", [128, 272], F32)
        d_pp = C("d_pp", [128, 4], F32)
        u_bf = C("u_ph", [128, 16, 272], BF16)
        ypre = C("ypre", [128, NT], BF16)
        W_in2 = [C("W_in%d" % i, [128, 16, 128], BF16) for i in range(2)]
        Apad2 = [C("Apad%d" % i, [128, 16, 128], BF16) for i in range(2)]
        Xre2 = [C("Xre2_%d" % i, [128, 272], F32) for i in range(2)]; Xim2 = [C("Xim2_%d" % i, [128, 272], F32) for i in range(2)]
        q1b = C("q1b", [128, 256], F32); q2b = C("q2b", [128, 256], F32); kf2 = C("kf2", [128, 272], F32)
        W_out = C("W_out", [128, 16, 16, 32], BF16)
        W_intra = C("W_intra", [128, 2, 16, 128], BF16)
        Cstack = C("Cstack", [128, 2, 8, 128], BF16)
        Hs = C("Hs", [128, 16, 273], BF16)
        Mre = C("Mre", [128, 272], F32); Mim = C("Mim", [128, 272], F32)
        cs2 = [C("cs%d" % i, [128, 272], F32) for i in range(2)]; sn2 = [C("sn%d" % i, [128, 272], F32) for i in range(2)]
        ang = C("ang", [128, 272], F32); kf = C("kf", [128, 272], F32); kin = C("kin", [128, 272], I32)
        q1 = C("q1", [128, 256], F32); q2 = C("q2", [128, 256], F32); q3 = C("q3", [128, 272], F32)
        q4 = q2; q5 = q1b; q6 = q2b
        ystage = [C("ystage%d" % i, [128, 512], BF16) for i in range(2)]
        h = lambda tb, *idx: V(tb.t[(slice(0, P),) + idx], tb)
        for src_, dst_ in ((self.s5_lam_re, lr), (self.s5_lam_im, li)):
            s.dma("sp", V(dst_.t[0:PH, :], dst_), src_[:, :])
            s.dma("sp", V(dst_.t[PH:128, 0:63], dst_), V(src_.t[:, 1:64], src_))
            s.dma("sp", V(dst_.t[PH:128, 63:64], dst_), V(src_.t[:, 63:64], src_), allow_slow_non_contiguous=True)
        s.dma("sp", V(dt.t[0:PH, :], dt), V(self.s5_log_dt.t[0:1, :].partition_broadcast(PH), self.s5_log_dt))
        s.dma("sp", V(dt.t[PH:128, 0:63], dt), V(self.s5_log_dt.t[0:1, 1:64].partition_broadcast(PH), self.s5_log_dt))
        s.dma("sp", V(dt.t[PH:128, 63:64], dt), V(self.s5_log_dt.t[0:1, 63:64].partition_broadcast(PH), self.s5_log_dt), allow_slow_non_contiguous=True)
        s.do("pool", lambda e: e.memset(bre.t[:, :, :], 0.0), w=[bre[:, :, :]])
        s.do("pool", lambda e: e.memset(bim.t[:, :, :], 0.0), w=[bim[:, :, :]])
        for d_ in range(2):
            for src_, dst_ in ((self.s5_b_re, bre), (self.s5_b_im, bim)):
                stg = wst.t[0:PH, 0:4, :].rearrange("p a (b h) -> p (a b) h", h=16)
                s.dma("sp", V(stg, wst), V(src_.t[d_].rearrange("g p h -> p g h"), src_))
                self.cp("dve", V(dst_.t[0:PH, d_ * 32:(d_ + 1) * 32, :], dst_), V(stg, wst))
                stg2 = wst.t[PH:128, 0:4, :].rearrange("p a (b h) -> p (a b) h", h=16)
                s.dma("sp", V(stg2[:, 0:31, :], wst), V(src_.t[d_, 1:32].rearrange("g p h -> p g h"), src_))
                self.cp("dve", V(dst_.t[PH:128, d_ * 32:d_ * 32 + 31, :], dst_), V(stg2[:, 0:31, :], wst))
        s.dma("sp", V(iota.t[0:PH, :], iota), self.c_iota[:, :])
        s.dma("sp", V(iota.t[PH:128, :], iota), self.c_iota[:, :])
        s.dma("sp", d_pp[:, :], self.s5_d_pp[:, :])
        s.do("pool", lambda e: e.memset(CTr.t[:, :, :, :], 0.0), w=[CTr[:, :, :, :]])
        s.do("pool", lambda e: e.memset(CTi.t[:, :, :, :], 0.0), w=[CTi[:, :, :, :]])
        cnats = [q1, q2, q1b, q2b]
        cps = [self.PC, self.PD]
        ci_ = 0
        for ri, (src, dst) in enumerate(((self.s5_c_re, CTr), (self.s5_c_im, CTi))):
            for d_ in range(2):
                for cb in range(4):
                    cnat = cnats[ci_ % 4]; cp_ = cps[ci_ % 2]; ci_ += 1
                    s.dma("sp", V(cnat.t[:, 0:64], cnat), V(src.t[d_, cb * 8:(cb + 1) * 8].rearrange("g h p -> (g h) p"), src))
                    s.do("pe", lambda e, cnat=cnat, cp_=cp_: e.transpose(out=cp_.t[0:64, 0:128], in_=cnat.t[:, 0:64], identity=self.ident_f.t[:, :]),
                         w=[cp_[:, :]], r=[cnat[:, :], self.ident_f[:, :]])
                    self.cp("dve", V(dst.t[0:PH, d_, cb, :], dst), V(cp_.t[0:64, 0:128], cp_))
                    self.cp("act", V(dst.t[PH:128, d_, cb, 0:112], dst), V(cp_.t[0:64, 16:128], cp_))
        H = lambda tb: h(tb, slice(None))
        self.act(H(dt), H(dt), AF.Exp)
        self.tt("dve", H(zr), H(lr), H(dt), ALU.mult)
        self.tt("dve", H(zi), H(li), H(dt), ALU.mult)
        self.act(H(rho16), H(zr), AF.Exp, scale=16.0)
        self.act(H(t3), H(zr), AF.Exp)
        self.ts("dve", H(phi), H(zi), 16.0, None, ALU.mult)
        self.ts("dve", H(kfs), H(phi), 1.0 / TWO_PI, None, ALU.mult)
        self.cp("dve", H(kis), H(kfs)); self.cp("dve", H(kfs), H(kis))
        self.stt(H(phi), H(kfs), -TP_HI, H(phi), ALU.mult, ALU.add)
        self.stt(H(phi), H(kfs), -TP_LO, H(phi), ALU.mult, ALU.add)
        self.cp("dve", H(t1), H(zi))
        self.angle_sincos(H(t1), H(lbi), H(lbr), H(kfs), H(kis), None)
        self.tt("dve", H(lbr), H(lbr), H(t3), ALU.mult)
        self.tt("dve", H(lbi), H(lbi), H(t3), ALU.mult)
        self.ts("dve", H(t1), H(lbr), -1.0, None, ALU.add)
        self.tt("dve", H(t2), H(lr), H(lr), ALU.mult)
        self.tt("dve", H(t3), H(li), H(li), ALU.mult)
        self.tt("dve", H(t2), H(t2), H(t3), ALU.add)
        s.do("dve", lambda e: e.reciprocal(out=t2.t[0:P, :], in_=t2.t[0:P, :]), w=[t2[:, :]], r=[t2[:, :]])
        self.tt("dve", H(kr_), H(t1), H(lr), ALU.mult)
        self.tt("dve", H(t3), H(lbi), H(li), ALU.mult)
        self.tt("dve", H(kr_), H(kr_), H(t3), ALU.add)
        self.tt("dve", H(kr_), H(kr_), H(t2), ALU.mult)
        self.tt("dve", H(ki_), H(lbi), H(lr), ALU.mult)
        self.tt("dve", H(t3), H(t1), H(li), ALU.mult)
        self.tt("dve", H(ki_), H(ki_), H(t3), ALU.subtract)
        self.tt("dve", H(ki_), H(ki_), H(t2), ALU.mult)
        s.do("pool", lambda e: e.memset(Er.t[0:P, 0, :], 1.0), w=[Er[:, :, :]])
        s.do("pool", lambda e: e.memset(Ei.t[0:P, 0, :], 0.0), w=[Ei[:, :, :]])
        self.cp("dve", V(Er.t[0:P, 1, :], Er), H(lbr))
        self.cp("pool", V(Ei.t[0:P, 1, :], Ei), H(lbi))
        T4 = wst.t[0:P, :, :].rearrange("p a (b g) -> p (a b) g", g=64)
        m = 1
        while m < 16:
            pr = V(Er.t[0:P, m:m + 1, :].to_broadcast([P, m, 64]), Er); pi_ = V(Ei.t[0:P, m:m + 1, :].to_broadcast([P, m, 64]), Ei)
            er = V(Er.t[0:P, 1:m + 1, :], Er); ei = V(Ei.t[0:P, 1:m + 1, :], Ei)
            ta = V(T4[:, 0:m, :], wst, 0); tb_ = V(T4[:, 8:8 + m, :], wst, 1)
            self.tt("dve", ta, er, pr, ALU.mult)
            self.tt("pool", tb_, ei, pi_, ALU.mult)
            self.tt("dve", V(Er.t[0:P, m + 1:2 * m + 1, :], Er), ta, tb_, ALU.subtract)
            self.tt("dve", ta, er, pi_, ALU.mult)
            self.tt("pool", tb_, ei, pr, ALU.mult)
            self.tt("dve", V(Ei.t[0:P, m + 1:2 * m + 1, :], Ei), ta, tb_, ALU.add)
            m *= 2
        kb_r = V(kr_.t[0:P, :].rearrange("p (o g) -> p o g", o=1).to_broadcast([P, 16, 64]), kr_)
        kb_i = V(ki_.t[0:P, :].rearrange("p (o g) -> p o g", o=1).to_broadcast([P, 16, 64]), ki_)
        E16r = V(Er.t[0:P, 0:16, :], Er); E16i = V(Ei.t[0:P, 0:16, :], Ei)
        AkrV = V(Akr.t[0:P, :, :], Akr); AkiV = V(Aki.t[0:P, :, :], Aki)
        self.tt("dve", AkrV, E16r, kb_r, ALU.mult)
        self.tt("dve", AkiV, E16i, kb_i, ALU.mult)
        self.tt("dve", AkrV, AkrV, AkiV, ALU.subtract)
        self.tt("dve", AkiV, E16r, kb_i, ALU.mult)
        TA = V(T4[:, :, :], wst)
        self.tt("dve", TA, E16i, kb_r, ALU.mult)
        self.tt("dve", AkiV, AkiV, TA, ALU.add)
        s.do("pool", lambda e: e.memset(Hs.t[:, :, :], 0.0), w=[Hs[:, :, :]])

        for cb in range(self.debug.get('s5_cbs', 4)):
            self.load_w(0, [(cb * 128, 128, 0)], wst, wbf)
            self.proj_fm(wbf, lambda pb, t0, n: self.cp("act", V(u_bf.t[:, :, t0 // 16:(t0 + n) // 16], u_bf), V(pb.t[:, 0:n].rearrange("p (c s) -> p s c", s=16), pb)), banks=[self.PD, self.PE_])
            s.do("pool", lambda e: e.memset(Cstack.t[:, :, :, :], 0.0), w=[Cstack[:, :, :, :]])
            s.do("pool", lambda e: e.memset(W_out.t[:, :, :, :], 0.0), w=[W_out[:, :, :, :]])
            for d_ in range(2):
                for gl in range(8):
                    self.cp("pool", V(Cstack.t[0:PH, d_, gl, 16 * gl:16 * gl + 16], Cstack), V(CTr.t[0:PH, d_, cb, 16 * gl:16 * gl + 16], CTr))
                    self.ts("dve", V(Cstack.t[PH:128, d_, gl, 16 * gl:16 * gl + 16], Cstack), V(CTi.t[0:PH, d_, cb, 16 * gl:16 * gl + 16], CTi), -1.0, None, ALU.mult)
            PDb = SubTB("PDb", self.PCD, 1, self.PD.t[:, :].bitcast(BF16))
            s.do("pool", lambda e: e.memset(Apad2[0].t[:, :, :], 0.0), w=[Apad2[0][:, :, :]])
            s.do("pool", lambda e: e.memset(Apad2[1].t[:, :, :], 0.0), w=[Apad2[1][:, :, :]])

            Xps = [self.PC, self.PE_]

            def G0(i, cb=cb):
                d_, j = divmod(i, 4)
                glA = 2 * j
                dg = d_ * 32 + cb * 8 + glA
                if j >= 1:
                    for it_, Ap in enumerate(Apad2):
                        pc_ = slice(16 * (glA - 2 + it_), 16 * (glA - 2 + it_) + 16)
                        s.do("pool", lambda e, Ap=Ap, pc_=pc_: e.memset(Ap.t[:, :, pc_], 0.0), w=[Ap[:, :, :]])
                elif i >= 1:
                    for it_, Ap in enumerate(Apad2):
                        pc_ = slice(16 * (6 + it_), 16 * (6 + it_) + 16)
                        s.do("pool", lambda e, Ap=Ap, pc_=pc_: e.memset(Ap.t[:, :, pc_], 0.0), w=[Ap[:, :, :]])
                def partA():
                    ar = V(unsq(Akr.t[:, :, dg]).to_broadcast([128, 16, 16]), Akr)
                    ai = V(unsq(Aki.t[:, :, dg]).to_broadcast([128, 16, 16]), Aki)
                    br = V(bre.t[:, dg, :].rearrange("p (o h) -> p o h", o=1).to_broadcast([128, 16, 16]), bre)
                    bi = V(bim.t[:, dg, :].rearrange("p (o h) -> p o h", o=1).to_broadcast([128, 16, 16]), bim)
                    q13 = lambda tb, lo, hi: V(tb.t[lo:hi, 0:256].rearrange("p (k h) -> p k h", h=16), tb)
                    self.tt("dve", q13(q1, 0, 128), ar, br, ALU.mult)
                    self.tt("pool", q13(q2, 0, 128), ai, bi, ALU.mult)
                    for it_, Ap in enumerate(Apad2):
                        cols = slice(16 * (glA + it_), 16 * (glA + it_) + 16)
                        lo, hi = 64 * it_, 64 * it_ + 64
                        self.tt("dve", V(Ap.t[0:PH, :, cols], Ap), q13(q1, lo, hi), q13(q2, lo, hi), ALU.subtract)
                    self.tt("dve", q13(q1b, 0, 128), ar, bi, ALU.mult)
                    self.tt("pool", q13(q2b, 0, 128), ai, br, ALU.mult)
                    for it_, Ap in enumerate(Apad2):
                        cols = slice(16 * (glA + it_), 16 * (glA + it_) + 16)
                        lo, hi = 64 * it_, 64 * it_ + 64
                        self.tt("dve", V(Ap.t[PH:128, :, cols], Ap), q13(q1b, lo, hi), q13(q2b, lo, hi), ALU.add)


                def partT():
                    Fv = lambda tb: V(tb.t[:, :], tb)
                    self.ts("dve", Fv(ang), Fv(iota), V(phi.t[:, dg:dg + 1], phi), None, ALU.mult)
                    self.angle_sincos(Fv(ang), Fv(sn2[i % 2]), Fv(cs2[i % 2]), Fv(q3), Fv(kin), None)

                def partW():
                    wq = lambda qi, lo, hi: V(wst.t[lo:hi, 2 * qi:2 * qi + 2, :].rearrange("p a (b h) -> p (a b) h", h=16), wst, qi)
                    colsA = slice(16 * glA, 16 * glA + 16)
                    if d_ == 0:
                        er = Er.t[:, 1:17, dg]; ei = Ei.t[:, 1:17, dg]
                    else:
                        er = Er.t[:, 16:0:-1, dg]; ei = Ei.t[:, 16:0:-1, dg]
                    erb = V(unsq(er).to_broadcast([128, 16, 16]), Er); eib = V(unsq(ei).to_broadcast([128, 16, 16]), Ei)
                    cr = V(CTr.t[:, d_, cb, colsA].rearrange("p (o h) -> p o h", o=1).to_broadcast([128, 16, 16]), CTr)
                    ci = V(CTi.t[:, d_, cb, colsA].rearrange("p (o h) -> p o h", o=1).to_broadcast([128, 16, 16]), CTi)
                    q13 = lambda tb, lo, hi: V(tb.t[lo:hi, 0:256].rearrange("p (k h) -> p k h", h=16), tb)
                    self.tt("dve", wq(0, 0, 128), erb, cr, ALU.mult)
                    self.tt("pool", wq(1, 0, 128), eib, ci, ALU.mult)
                    for it_ in range(2):
                        slot = (glA + it_) * 2 + d_
                        lo, hi = 64 * it_, 64 * it_ + 64
                        wc = slice(16 * it_, 16 * it_ + 16)
                        self.tt("dve", V(W_out.t[0:PH, slot, :, wc], W_out), wq(0, lo, hi), wq(1, lo, hi), ALU.subtract)
                    self.tt("pool", wq(2, 0, 128), erb, ci, ALU.mult)
                    self.tt("pool", wq(3, 0, 128), eib, cr, ALU.mult)
                    self.tt("pool", wq(2, 0, 128), wq(2, 0, 128), wq(3, 0, 128), ALU.add)
                    for it_ in range(2):
                        slot = (glA + it_) * 2 + d_
                        lo, hi = 64 * it_, 64 * it_ + 64
                        wc = slice(16 * it_, 16 * it_ + 16)
                        self.ts("dve", V(W_out.t[PH:128, slot, :, wc], W_out), wq(2, lo, hi), -1.0, None, ALU.mult)


                partA()
                self.interleave([self.capture(partT), self.capture(partW)])

            def G1(i, cb=cb):
                d_, j = divmod(i, 4)
                Xr = Xre2[i % 2]; Xi = Xim2[i % 2]
                if j == 0:
                    s.do("dve", lambda e: e.memset(self.PA.t[:, :], 0.0), w=[self.PA[:, :]])
                    s.do("dve", lambda e: e.memset(self.PB.t[:, :], 0.0), w=[self.PB[:, :]])
                for it_ in range(2):
                    Ap = Apad2[it_]; Wi = W_in2[it_]
                    for half in range(2):
                        ptb = self.PT if half == 0 else PDb
                        for ii in range(8):
                            sidx = half * 8 + ii
                            k = 15 - sidx if d_ == 0 else sidx
                            s.do("pe", lambda e, ii=ii, k=k, ptb=ptb, Ap=Ap: e.transpose(out=ptb.t[:, ii * 128:(ii + 1) * 128], in_=Ap.t[:, k, :], identity=self.ident_b.t[:, :]),
                                 w=[ptb[:, :]], r=[Ap[:, :, :], self.ident_b[:, :]])
                        self.cp("act", V(Wi.t[:, half * 8:(half + 1) * 8, :], Wi),
                                V(ptb.t[:, :].rearrange("p (s c) -> p s c", c=128), ptb))
                si = []
                if d_ == 0:
                    for sidx in range(16):
                        si.append((slice(0, 272), slice(0, 272), sidx, sidx == 0, sidx == 15))
                else:
                    for sidx in range(16):
                        si.append((slice(0, 16), slice(15, None, -1), sidx, sidx == 0, sidx == 15))
                    for sidx in range(16):
                        si.append((slice(16, 272), slice(271, 15, -1), sidx, False, sidx == 15))
                for tau in range(16):
                    for itk in range(2):
                        gl = 2 * j + itk
                        pk = self.PA if tau < 8 else self.PB
                        tt_ = tau % 8
                        self.mm(V(pk.t[:, tt_ * 128:(tt_ + 1) * 128], pk), V(Apad2[itk].t[:, tau, :], Apad2[itk]), V(Cstack.t[:, d_, gl, :], Cstack),
                                start=False, stop=(gl == 7), skip_group_check=True)
                for (osl, usl, sidx, st_, sp_) in si:
                    for it_ in range(2):
                        Xp = Xps[it_]; Wi = W_in2[it_]
                        self.mm(V(Xp.t[:, osl], Xp), V(Wi.t[:, sidx, :], Wi), V(u_bf.t[:, sidx, usl], u_bf), start=st_, stop=sp_, skip_group_check=True)
                for it_ in range(2):
                    Xp = Xps[it_]
                    lo, hi = 64 * it_, 64 * it_ + 64
                    self.cp("act", V(Xr.t[lo:hi, :], Xr), V(Xp.t[0:PH, 0:272], Xp))
                    self.cp("act", V(Xi.t[lo:hi, :], Xi), V(Xp.t[PH:128, 0:272], Xp))
                if j == 3:
                    self.cp("act", V(W_intra.t[:, d_, 0:8, :], W_intra), V(self.PA.t[:, :].rearrange("p (t c) -> p t c", c=128), self.PA))
                    self.cp("dve", V(W_intra.t[:, d_, 8:16, :], W_intra), V(self.PB.t[:, :].rearrange("p (t c) -> p t c", c=128), self.PB))

            def G2(i, cb=cb):
                d_, j = divmod(i, 4)
                glA = 2 * j
                dg = d_ * 32 + cb * 8 + glA
                Xr = Xre2[i % 2]; Xi = Xim2[i % 2]
                F = lambda tb: V(tb.t[:, :], tb)
                cs = cs2[i % 2]; sn = sn2[i % 2]
                def m_re():
                    self.tt("dve", F(Mre), F(Xr), F(cs), ALU.mult)
                    self.tt("pool", F(kf), F(Xi), F(sn), ALU.mult)
                    self.tt("dve", F(Mre), F(Mre), F(kf), ALU.add)

                def m_im():
                    self.tt("dve", F(Mim), F(Xi), F(cs), ALU.mult)
                    self.tt("pool", F(kf2), F(Xr), F(sn), ALU.mult)
                    self.tt("dve", F(Mim), F(Mim), F(kf2), ALU.subtract)
                self.interleave([self.capture(m_re), self.capture(m_im)])
                rb = V(rho16.t[:, dg:dg + 1].to_broadcast([128, 272]), rho16)
                for M_, X_ in ((Mre, Xr), (Mim, Xi)):
                    s.do("dve", lambda e, M_=M_, X_=X_, rb=rb: e.tensor_tensor_scan(out=X_.t[:, :], data0=rb.ap, data1=M_.t[:, :], initial=0.0,
                                                                                op0=ALU.mult, op1=ALU.add), w=[X_[:, :]], r=[M_[:, :]], rs=[rb])
                self.tt("dve", F(Mre), F(Xr), F(cs), ALU.mult)
                self.tt("pool", F(kf), F(Xi), F(sn), ALU.mult)
                self.tt("dve", F(Mim), F(Xr), F(sn), ALU.mult)
                self.tt("pool", F(kf2), F(Xi), F(cs), ALU.mult)
                for it_ in range(2):
                    slot = (glA + it_) * 2 + d_
                    lo, hi = 64 * it_, 64 * it_ + 64
                    self.tt("dve", V(Hs.t[0:PH, slot, 1:273], Hs), V(Mre.t[lo:hi, :], Mre), V(kf.t[lo:hi, :], kf), ALU.subtract)
                    self.tt("dve", V(Hs.t[PH:128, slot, 1:273], Hs), V(Mim.t[lo:hi, :], Mim), V(kf2.t[lo:hi, :], kf2), ALU.add)

            skew(8, [G0, G2, G1], [0, 1, 0])
            for ps_ in range(4):
                rows = [(self.PA, 0), (self.PA, 512), (self.PB, 0), (self.PB, 512)]
                chains = []
                for r_ in range(4):
                    sp_ = ps_ * 4 + r_
                    pk, off = rows[r_]
                    ch = []
                    Y = V(pk.t[:, off:off + 272], pk, off)
                    first = True
                    for sidx in range(16):
                        tau = sp_ - sidx
                        lst = []
                        if tau >= 0:
                            lst.append(V(W_intra.t[:, 0, tau, :], W_intra))
                        if tau <= 0:
                            lst.append(V(W_intra.t[:, 1, -tau, :], W_intra))
                        for lw in lst:
                            ch.append(lambda Y=Y, lw=lw, sidx=sidx, first=first: self.mm(Y, lw, V(u_bf.t[:, sidx, :], u_bf), start=first, stop=False, skip_group_check=True))
                            first = False
                    for part in range(3):
                        for par in range(2):
                            def unit(pk=pk, off=off, sp_=sp_, part=part, par=par):
                                for gp in range(4):
                                    gl = 2 * gp + par
                                    tp = (0, 32 * gp)
                                    last = (part == 2 and par == 1 and gp == 3)
                                    if part == 0:
                                        self.mm(V(pk.t[32 * gp:32 * gp + 32, off:off + 272], pk, off), V(W_out.t[:, gl * 2, sp_, :], W_out), V(Hs.t[:, gl * 2, 0:272], Hs),
                                                start=False, stop=False, skip_group_check=True, tile_position=tp)
                                    elif part == 1:
                                        self.mm(V(pk.t[32 * gp:32 * gp + 32, off:off + 16], pk, off), V(W_out.t[:, gl * 2 + 1, sp_, :], W_out), V(Hs.t[:, gl * 2 + 1, 15::-1], Hs),
                                                start=False, stop=False, skip_group_check=True, tile_position=tp)
                                    else:
                                        self.mm(V(pk.t[32 * gp:32 * gp + 32, off + 16:off + 272], pk, off), V(W_out.t[:, gl * 2 + 1, sp_, :], W_out), V(Hs.t[:, gl * 2 + 1, 271:15:-1], Hs),
                                                start=False, stop=last, skip_group_check=True, tile_position=tp)
                            ch.append(unit)
                    chains.append(ch)
                for k_ in range(max(len(c) for c in chains)):
                    for ch in chains:
                        if k_ < len(ch):
                            ch[k_]()
                for r_ in range(4):
                    sp_ = ps_ * 4 + r_
                    pk, off = rows[r_]
                    Y = V(pk.t[:, off:off + 272], pk, off)
                    self.stt(V(ypre.t[:, sp_:NT:16], ypre, sp_), V(u_bf.t[:, sp_, :], u_bf), V(d_pp.t[:, cb:cb + 1], d_pp), Y, ALU.mult, ALU.add)
            if "s5_pre" in self.debug:
                self.dump("d_s5pre%d" % cb, ypre[:, :], [128, NT], BF16)
            for i, (t0, n) in enumerate(COLT):
                st = ystage[i % 2]
                self.act(V(st.t[:, 0:n], st), V(ypre.t[:, t0:t0 + n], ypre), AF.Gelu_apprx_tanh)
                self.y_store(cb, V(st.t[:, 0:n], st), t0, n, eng="act")

    def phase_glu(self):
        s = self.S
        self.new_phase()
        C = self.carve
        yg = C("yg", [128, 4, NT], BF16)
        sga = C("sga", [128, 4, NT], BF16)
        wgs = C("wgs", [128, 4, 512], F32)
        wgb = C("wgb", [128, 4, 512], BF16)
        bg = C("bg", [128, 4], F32)
        wst = C("wst", [128, 8, 128], F32); wbf = C("wbf", [128, 8, 128], BF16)
        W2 = [(wst, wbf), (C("wst2", [128, 8, 128], F32), C("wbf2", [128, 8, 128], BF16))]; wi_ = [0]

        def nw():
            p_ = W2[wi_[0] % 2]; wi_[0] += 1
            return p_
        sig = [C("sig%d" % i, [128, 512], BF16) for i in range(2)]
        yst = [C("yst%d" % i, [128, 512], BF16) for i in range(2)]
        for cb in range(4):
            s.do("sp", lambda e, cb=cb: e.dma_start(out=yg.t[:, cb, :], in_=self.yscr.t[cb, :, :]), w=[yg[:, :, :]], r=[V(self.yscr.t, self.yscr)], dma=True)
        s.dma("sp", wgs[:, :, :], V(self.s5_w_glu.t.rearrange("(kb p) n -> p kb n", p=128), self.s5_w_glu))
        self.cp("pool", wgb[:, :, :], wgs[:, :, :])
        s.dma("sp", bg[:, :], self.s5_b_glu_pp[:, :])
        for cb in range(4):
            wst_, wbf_ = nw()
            self.load_w(0, [(512 + cb * 128, 128, 0)], wst_, wbf_)
            self.proj_fm(wbf_, lambda pb, t0, n, cb=cb: self.act(V(sga.t[:, cb, t0:t0 + n], sga), V(pb.t[:, 0:n], pb), AF.Silu), banks=[self.PC, self.PD, self.PE_])
        zb = [(self.PA, 0), (self.PA, 512), (self.PB, 0), (self.PB, 512)]
        cnt = 0
        for (t0, n) in COLT:
            for cbo in range(4):
                pk, off = zb[cbo]
                for cbi in range(4):
                    self.mm(V(pk.t[:, off:off + n], pk), V(wgb.t[:, cbi, cbo * 128:(cbo + 1) * 128], wgb), V(yg.t[:, cbi, t0:t0 + n], yg),
                            start=(cbi == 0), stop=(cbi == 3))
            for cbo in range(4):
                pk, off = zb[cbo]
                sg = sig[cnt % 2]; st = yst[cnt % 2]; cnt += 1
                self.act(V(sg.t[:, 0:n], sg), V(pk.t[:, off:off + n], pk), AF.Sigmoid, bias=V(bg.t[:, cbo:cbo + 1], bg))
                self.tt("dve", V(st.t[:, 0:n], st), V(yg.t[:, cbo, t0:t0 + n], yg), V(sg.t[:, 0:n], sg), ALU.mult)
                self.tt("dve", V(st.t[:, 0:n], st), V(st.t[:, 0:n], st), V(sga.t[:, cbo, t0:t0 + n], sga), ALU.mult)
                self.y_store(cbo, V(st.t[:, 0:n], st), t0, n)

    def attn_stages(self, blocks, qT, kT, vaug, EB, P_bufs, sgate, psS_sets, psO_sets, psB, ones_row, recs, tmps, ysts):
        s = self.S

        def A(i):
            b = blocks[i]; nq = b["nq"]; hp = slice(64 * b["hh"], 64 * b["hh"] + 64)
            pS = psS_sets[i % len(psS_sets)]
            for j, (kt, cfg) in enumerate(b["ktiles"]):
                so = b["so"][j]
                if cfg is None:
                    self.mm(V(pS.t[:, so:so + nq], pS), V(kT.t[:, b["hh"], kt * 128:(kt + 1) * 128], kT), V(qT.t[:, b["q0"]:b["q0"] + nq], qT), start=True, stop=True)
                else:
                    self.mm(V(pS.t[:, so:so + nq], pS), V(self.ident_b.t[:, :], self.ident_b), V(EB.t[:, b["hh"], cfg, 0:nq], EB), start=True, stop=False, skip_group_check=True)
                    self.mm(V(pS.t[:, so:so + nq], pS), V(kT.t[:, b["hh"], kt * 128:(kt + 1) * 128], kT), V(qT.t[:, b["q0"]:b["q0"] + nq], qT), start=False, stop=True, skip_group_check=True)

        def B(i):
            b = blocks[i]; nq = b["nq"]
            pS = psS_sets[i % len(psS_sets)]; Pb = P_bufs[i % len(P_bufs)]
            nk = len(b["ktiles"])
            j = 0
            while j < nk:
                so = b["so"][j]
                bank = so // 512
                j2 = j
                while j2 + 1 < nk and b["so"][j2 + 1] // 512 == bank and b["so"][j2 + 1] == b["so"][j2] + nq:
                    j2 += 1
                cnt = j2 - j + 1
                self.act(V(Pb.t[:, j:j + cnt, 0:nq], Pb), V(pS.t[:, so:so + cnt * nq].rearrange("p (j q) -> p j q", q=nq), pS), AF.Exp, scale=0.125)
                j = j2 + 1

        def Cc(i):
            b = blocks[i]; nq = b["nq"]
            Pb = P_bufs[i % len(P_bufs)]; pO = psO_sets[i % len(psO_sets)]
            nk = len(b["ktiles"])
            for j, (kt, cfg) in enumerate(b["ktiles"]):
                self.mm(V(pO.t[0:65, 0:nq], pO), V(vaug.t[:, kt, b["hh"], :], vaug), V(Pb.t[:, j, 0:nq], Pb), start=(j == 0), stop=(j == nk - 1))

        def D1(i):
            b = blocks[i]; nq = b["nq"]; hp = slice(64 * b["hh"], 64 * b["hh"] + 64)
            pO = psO_sets[i % len(psO_sets)]
            rec = recs[i % len(recs)]; tmp = tmps[i % len(tmps)]
            self.act(V(rec.t[64:65, 0:nq], rec), V(pO.t[64:65, 0:nq], pO), AF.Ln)
            self.tt("dve", V(tmp.t[0:64, 0:nq], tmp), V(pO.t[0:64, 0:nq], pO), V(sgate.t[hp, b["g0"]:b["g0"] + nq], sgate), ALU.mult)
            self.act(V(rec.t[64:65, 0:nq], rec), V(rec.t[64:65, 0:nq], rec), AF.Exp, scale=-1.0)

        def D2(i):
            b = blocks[i]; nq = b["nq"]
            rec = recs[i % len(recs)]; tmp = tmps[i % len(tmps)]; yst = ysts[i % len(ysts)]
            self.mm(V(psB.t[0:64, 0:nq], psB), V(ones_row.t[64:65, 0:64], ones_row), V(rec.t[64:65, 0:nq], rec), start=True, stop=True)
            kb, ph, tok0 = b["ysel"]
            self.tt("dve", V(yst.t[64 * ph:64 * ph + 64, 0:nq], yst), V(tmp.t[0:64, 0:nq], tmp), V(psB.t[0:64, 0:nq], psB), ALU.mult)
            s.do("pool", lambda e: e.dma_start(out=self.yscr.t[kb, 64 * ph:64 * ph + 64, tok0:tok0 + nq], in_=yst.t[64 * ph:64 * ph + 64, 0:nq]),
                 w=[self.yscr.k(t)[:, :, :] for t in tiles_of(tok0, nq)], r=[yst[:, :]], dma=True)

        skew(len(blocks), [A, B, Cc, D1, D2], [0, 1, 1, 2, 3])

    def phase_na(self):
        s = self.S
        self.new_phase()
        C = self.carve
        qT = C("qT", [128, NT], BF16); kT = C("kT2", [128, 2, NT], BF16); sgb = C("sgb", [128, NT], BF16)
        vaug = C("vaug", [128, NTILE, 2, 65], BF16)
        s.do("pool", lambda e: e.memset(kT.t[:, :, :], 0.0), w=[kT[:, :, :]])
        EB = C("EB", [128, 2, 21, 128], BF16)
        ebst = [C("ebst%d" % i, [128, 7, 128], F32) for i in range(3)]
        P_buf = [C("Pbuf%d" % i, [128, 7, 256], BF16) for i in range(2)]
        wst = C("wst", [128, 8, 128], F32); wbf = C("wbf", [128, 8, 128], BF16)
        W2 = [(wst, wbf), (C("wst2", [128, 8, 128], F32), C("wbf2", [128, 8, 128], BF16))]; wi_ = [0]

        def nw():
            p_ = W2[wi_[0] % 2]; wi_[0] += 1
            return p_
        ones_row = C("ones_row", [128, 64], F32)
        recs = [C("rec%d" % i, [128, 256], F32) for i in range(3)]
        tmp = [C("tmp%d" % i, [128, 256], F32) for i in range(4)]
        yst = [C("yst%d" % i, [128, 256], BF16) for i in range(3)]
        s.do("pool", lambda e: e.memset(ones_row.t[:, :], 1.0), w=[ones_row[:, :]])
        s.do("pool", lambda e: e.memset(vaug.t[:, :, :, 64:65], 1.0), w=[vaug[:, :, :, :]])
        it = 0
        for hp in range(self.debug.get("na_hps", 4)):
            wst_, wbf_ = nw()
            self.load_w(0, [(1024 + hp * 128, 128, 0)], wst_, wbf_)
            self.proj_fm(wbf_, lambda pb, t0, n: self.cp("act", V(qT.t[:, t0:t0 + n], qT), V(pb.t[:, 0:n], pb)))
            wst_, wbf_ = nw()
            self.load_w(0, [(1536 + hp * 128, 128, 0)], wst_, wbf_)
            self.proj_fm(wbf_, lambda pb, t0, n: (self.cp("dve", V(kT.t[0:64, 0, t0:t0 + n], kT), V(pb.t[0:64, 0:n], pb)),
                                                 self.cp("pool" if False else "act", V(kT.t[64:128, 1, t0:t0 + n], kT), V(pb.t[64:128, 0:n], pb))))
            wst_, wbf_ = nw()
            self.load_w(0, [(2560 + hp * 128, 128, 0)], wst_, wbf_)
            self.proj_fm(wbf_, lambda pb, t0, n: self.act(V(sgb.t[:, t0:t0 + n], sgb), V(pb.t[:, 0:n], pb), AF.Silu))
            wst_, wbf_ = nw()
            self.load_w(0, [(2048 + hp * 128, 128, 0)], wst_, wbf_)
            self.proj_tm(wbf_, 128, lambda pb, t: self.cp("dve" if t % 2 else "act", V(vaug.t[:, t, :, 0:64], vaug),
                                                          V(pb.t[:, 0:128].rearrange("p (h d) -> p h d", h=2), pb)))
            for hh in range(2):
                for c3 in range(3):
                    st = ebst[(hh * 3 + c3) % 3]
                    s.dma("sp", st[:, :, :], V(self.na_bias.t[hp * 2 + hh, c3 * 7:(c3 + 1) * 7].rearrange("c k q -> k c q"), self.na_bias))
                    self.act(V(EB.t[:, hh, c3 * 7:(c3 + 1) * 7, :], EB), st[:, :, :], AF.Copy, scale=8.0)
            blocks = []
            for hh in range(2):
                for rp in range(32):
                    if 2 <= rp <= 29:
                        kts = [(2 + rp - 2 + i, i) for i in range(5)]
                    elif rp == 0:
                        kts = [(2 + i, 5 + i) for i in range(4)]
                    elif rp == 1:
                        kts = [(2 + i, 9 + i) for i in range(4)]
                    elif rp == 30:
                        kts = [(2 + 28 + i, 13 + i) for i in range(4)]
                    else:
                        kts = [(2 + 28 + i, 17 + i) for i in range(4)]
                    kts = kts + [(0, None), (1, None)]
                    q0 = 256 + rp * 128
                    blocks.append(dict(q0=q0, nq=128, hh=hh, ktiles=kts, so=[128 * j for j in range(len(kts))], g0=q0, ysel=(4 + hp, hh, q0)))
                blocks.append(dict(q0=0, nq=256, hh=hh, ktiles=[(0, None), (1, None)], so=[0, 512], g0=0, ysel=(4 + hp, hh, 0)))
            self.attn_stages(blocks, qT, kT, vaug, EB, P_buf, sgb, [self.PA, self.PB], [self.PC, self.PD], self.PE_, ones_row, recs, tmp, yst)

    def phase_outproj(self, l):
        s = self.S
        self.new_phase()
        C = self.carve
        wos2 = [C("wos%d" % i, [128, 8, 256], F32) for i in range(2)]
        wob = C("wob", [128, 8, D], BF16)
        NB = 6
        NY = 4
        yt = [C("yt%d" % i, [128, 8, 128], BF16) for i in range(NY)]
        hold = [C("hold%d" % i, [128, D], F32) for i in range(NB)]
        hnew = [C("hnew%d" % i, [128, D], F32) for i in range(3)]
        sq = [C("sq%d" % i, [128, D], BF16) for i in range(2)]
        sq2 = [C("sq2_%d" % i, [128, D], BF16) for i in range(2)]
        tf = [C("tf%d" % i, [128, D], F32) for i in range(2)]
        wsrc = self.w_out[l].t.rearrange("(kb p) n -> p kb n", p=128)
        for c4 in range(4):
            wos = wos2[c4 % 2]
            s.dma("sp", wos[:, :, :], V(wsrc[:, :, c4 * 256:(c4 + 1) * 256], self.w_out[l]))
            self.cp("act" if c4 % 2 else "dve", V(wob.t[:, :, c4 * 256:(c4 + 1) * 256], wob), wos[:, :, :])
        tiles = list(range(NTILE) if l == 0 else range(2, NTILE))
        pOs = [self.PA, self.PB, self.PCD]

        def L(i):
            t = tiles[i]
            s.do("sp", lambda e: e.dma_start(out=yt[i % NY].t[:, :, :], in_=self.yscr.t[:, :, t * 128:(t + 1) * 128].rearrange("k p n -> p k n")),
                 w=[yt[i % NY][:, :, :]], r=[self.yscr.k(t)[:, :, :]], dma=True)
            src = self.xs if l == 0 else self.hscr
            s.dma("act", hold[i % NB][:, :], V(src.t[t * 128:(t + 1) * 128, :], src) if l == 0 else self.hscr.k(t)[t * 128:(t + 1) * 128, :])

        def Mmm(i):
            pO = pOs[i % 3]
            for half in range(2):
                for kb in range(8):
                    self.mm(V(pO.t[:, half * 512:(half + 1) * 512], pO), V(yt[i % NY].t[:, kb, :], yt[i % NY]), V(wob.t[:, kb, half * 512:(half + 1) * 512], wob),
                            start=(kb == 0), stop=(kb == 7))

        def Mst(i):
            pO = pOs[i % 3]
            c0 = 8 + 4 * (i % 2)
            ss = self.small.k(c0)[:, c0:c0 + 1]; rs = self.small.k(c0 + 1)[:, c0 + 1:c0 + 2]
            self.act(sq[i % 2][:, :], pO[:, :], AF.Square, accum=ss)
            self.act(rs, ss, AF.Sqrt, scale=1.0 / D, bias=self.eps_col)
            s.do("dve", lambda e, rs=rs: e.reciprocal(out=rs.ap, in_=rs.ap), w=[rs], r=[rs])

        def E1(i):
            t = tiles[i]
            j = 0 if t >= 2 else 1
            pO = pOs[i % 3]
            c0 = 8 + 4 * (i % 2)
            rs = self.small.k(c0 + 1)[:, c0 + 1:c0 + 2]
            hn = hnew[i % 3]
            self.stt(hn[:, :], pO[:, :], rs, V(self.GP.t[:, l, j, :], self.GP), ALU.mult, ALU.mult)
            self.tt("dve", hn[:, :], hn[:, :], hold[i % NB][:, :], ALU.add)
            if l == 0:
                s.do("pool", lambda e: e.dma_start(out=self.hscr.t[t * 128:(t + 1) * 128, :], in_=hn.t[:, :]),
                     w=[self.hscr.k(t)[t * 128:(t + 1) * 128, :]], r=[hn[:, :]], dma=True)
            else:
                s.do("pool", lambda e: e.dma_start(out=self.out.t[(t - 2) * 128:(t - 1) * 128, :], in_=hn.t[:, :]),
                     w=[V(self.out.t, self.out, t)], r=[hn[:, :]], dma=True)

        def E2a(i):
            self.norm_a(hnew[i % 3][:, :], sq2[i % 2], 16 + 4 * (i % 2))

        def E2b(i):
            self.norm_b(1, tiles[i], sq2[i % 2], tf[i % 2])

        if l == 0:
            skew(len(tiles), [L, Mmm, Mst, E1, E2a, E2b], [0, 2, 3, 4, 5, 6])
        else:
            skew(len(tiles), [L, Mmm, Mst, E1], [0, 2, 3, 4])

    def phase_lru(self):
        s = self.S
        self.new_phase()
        C = self.carve
        NP = 259 + 4099
        xr = C("xr", [128, NP], F32)
        xc = C("xc", [128, NT], F32)
        xcb = C("xcb", [128, NT], BF16)
        a_ = C("a", [128, NT], F32)
        bt = C("bt", [128, NT], F32)
        sgc = C("sgc", [128, NT], BF16)
        wgs4 = [C("wgs%d" % i, [128, 128], F32) for i in range(4)]
        wgb = C("wgb", [128, 16, 128], BF16)
        cw = C("cw", [128, 4, 4], F32); cbias = C("cbias", [128, 4], F32)
        lam = C("lam", [128, 2, 4], F32); cA = C("cA", [128, 2, 4], F32); c2A = C("c2A", [128, 2, 4], F32)
        nba = C("nba", [128, 2, 4], F32); nbx = C("nbx", [128, 2, 4], F32)
        g1 = [C("g1_%d" % i, [128, 512], F32) for i in range(3)]
        g2 = [C("g2_%d" % i, [128, 512], F32) for i in range(5)]
        g3 = [C("g3_%d" % i, [128, 512], F32) for i in range(4)]
        wst = C("wst", [128, 8, 128], F32); wbf = C("wbf", [128, 8, 128], BF16)
        W2 = [(wst, wbf), (C("wst2", [128, 8, 128], F32), C("wbf2", [128, 8, 128], BF16))]; wi_ = [0]

        def nw():
            p_ = W2[wi_[0] % 2]; wi_[0] += 1
            return p_
        yst = [C("yst%d" % i, [128, 512], BF16) for i in range(2)]
        s.dma("sp", cw[:, :, :], self.lru_conv_w_pp[:, :, :]); s.dma("sp", cbias[:, :], self.lru_conv_b_pp[:, :])
        s.dma("sp", lam[:, :, :], self.lru_lam_pp[:, :, :])
        s.dma("sp", nba[:, :, :], self.lru_b_a_pp[:, :, :]); s.dma("sp", nbx[:, :, :], self.lru_b_x_pp[:, :, :])
        pba, pbx = nba, nbx
        self.act(cA[:, :, :], lam[:, :, :], AF.Exp, scale=-1.0)
        self.ts("dve", cA[:, :, :], cA[:, :, :], 1.0, None, ALU.add)
        self.act(cA[:, :, :], cA[:, :, :], AF.Ln)
        self.ts("dve", c2A[:, :, :], cA[:, :, :], -16.0, None, ALU.mult)
        self.ts("dve", cA[:, :, :], cA[:, :, :], -8.0, None, ALU.mult)
        for wg_ in wgs4:
            s.do("pool", lambda e, wg_=wg_: e.memset(wg_.t[:, :], 0.0), w=[wg_[:, :]])
        for d_ in range(2):
            for ax, src in enumerate((self.lru_w_a, self.lru_w_x)):
                for cb in range(4):
                    idx = (d_ * 2 + ax) * 4 + cb
                    wgs = wgs4[idx % 4]
                    for nl in range(2):
                        s.dma("sp", V(wgs.t[64 * nl:64 * nl + 64, 64 * nl:64 * nl + 64], wgs), V(src.t[d_, 2 * cb + nl], src))
                    self.cp("dve", V(wgb.t[:, idx, :], wgb), wgs[:, :])
        segs = [(0, 256, 0), (259, 4096, 256)]
        prev_hf = None
        for cb in range(self.debug.get("lru_cbs", 4)):
            s.do("pool", lambda e: e.memset(xr.t[:, :], 0.0), w=[xr[:, :]] + ([V(prev_hf.t, prev_hf)] if prev_hf is not None else []))
            wst_, wbf_ = nw()
            self.load_w(1, [(cb * 128, 128, 0)], wst_, wbf_)

            def ev_x(pb, t0, n):
                dst = 1 + t0 if t0 < 256 else 259 + 1 + (t0 - 256)
                self.cp("act", V(xr.t[:, dst:dst + n], xr), V(pb.t[:, 0:n], pb))
            self.proj_fm(wbf_, ev_x)
            wst_, wbf_ = nw()
            self.load_w(1, [(512 + cb * 128, 128, 0)], wst_, wbf_)
            self.proj_fm(wbf_, lambda pb, t0, n: self.act(V(sgc.t[:, t0:t0 + n], sgc), V(pb.t[:, 0:n], pb), AF.Silu))
            hf = TB("hf%d" % cb, xr.t[:, 0:NT])
            prev_hf = hf

            def GV(lst, i, n):
                tb = lst[i % len(lst)]
                return V(tb.t[:, 0:n], tb)

            def rev(t0, n):
                return slice(t0 + n - 1, (t0 - 1) if t0 > 0 else None, -1)

            for d_ in range(2):
                order = list(range(len(COLT))) if d_ == 0 else [0] + list(range(len(COLT) - 1, 0, -1))

                def Cv(k, d_=d_, order=order):
                    if d_ == 1:
                        return
                    i = order[k]; t0, n = COLT[i]
                    b0 = t0 if t0 < 256 else t0 + 3
                    XC = V(xc.t[:, t0:t0 + n], xc, i)
                    self.ts("dve", XC, V(xr.t[:, b0:b0 + n], xr), V(cw.t[:, cb, 0:1], cw), V(cbias.t[:, cb:cb + 1], cbias), ALU.mult, ALU.add)
                    for j in range(1, 4):
                        self.stt(XC, V(xr.t[:, b0 + j:b0 + j + n], xr), V(cw.t[:, cb, j:j + 1], cw), XC, ALU.mult, ALU.add)
                    self.cp("act", V(xcb.t[:, t0:t0 + n], xcb, i), XC)

                def A0(k, d_=d_, order=order):
                    i = order[k]; t0, n = COLT[i]; off = 512 * (k % 2)
                    self.mm(V(self.PA.t[:, off:off + n], self.PA, off), V(wgb.t[:, (d_ * 2 + 0) * 4 + cb, :], wgb), V(xcb.t[:, t0:t0 + n], xcb, i), start=True, stop=True)
                    self.mm(V(self.PB.t[:, off:off + n], self.PB, off), V(wgb.t[:, (d_ * 2 + 1) * 4 + cb, :], wgb), V(xcb.t[:, t0:t0 + n], xcb, i), start=True, stop=True)

                def A1(k, d_=d_, order=order):
                    i = order[k]; t0, n = COLT[i]; off = 512 * (k % 2)
                    self.act(V(a_.t[:, t0:t0 + n], a_, i), V(self.PA.t[:, off:off + n], self.PA, off), AF.Sigmoid, bias=V(pba.t[:, d_, cb:cb + 1], pba))
                    self.act(V(bt.t[:, t0:t0 + n], bt, i), V(self.PB.t[:, off:off + n], self.PB, off), AF.Sigmoid, bias=V(pbx.t[:, d_, cb:cb + 1], pbx))

                def B0(k, d_=d_, order=order):
                    i = order[k]; t0, n = COLT[i]
                    GR = V(a_.t[:, t0:t0 + n], a_, i)
                    self.act(GV(g3, k, n), GR, AF.Exp, scale=V(c2A.t[:, d_, cb:cb + 1], c2A))
                    self.act(GR, GR, AF.Exp, scale=V(cA.t[:, d_, cb:cb + 1], cA))
                    self.tt("pool", GV(g2, k, n), V(bt.t[:, t0:t0 + n], bt, i), V(xc.t[:, t0:t0 + n], xc, i), ALU.mult)

                def B1(k, order=order):
                    i = order[k]; t0, n = COLT[i]
                    self.ts("dve", GV(g3, k, n), GV(g3, k, n), 0.99999994, -1.0, ALU.min, ALU.mult)

                def B2(k, order=order):
                    i = order[k]; t0, n = COLT[i]
                    self.act(GV(g3, k, n), GV(g3, k, n), AF.Ln, bias=self.one_col)
                    self.act(GV(g3, k, n), GV(g3, k, n), AF.Exp, scale=0.5)

                def B3(k, order=order):
                    i = order[k]; t0, n = COLT[i]
                    self.tt("dve", V(bt.t[:, t0:t0 + n], bt, i), GV(g3, k, n), GV(g2, k, n), ALU.mult)

                def SC(k, d_=d_, order=order):
                    i = order[k]; t0, n = COLT[i]
                    A_ = a_.t[:, t0:t0 + n]; B_ = bt.t[:, t0:t0 + n]
                    rd = [V(A_, a_, i), V(B_, bt, i)]
                    if d_ == 0:
                        if k == 0:
                            s.do("dve", lambda e: e.tensor_tensor_scan(out=hf.t[:, t0:t0 + n], data0=A_, data1=B_, initial=0.0, op0=ALU.mult, op1=ALU.add),
                                 w=[V(hf.t[:, t0:t0 + n], hf, i)], r=rd)
                        else:
                            ini = V(hf.t[:, t0 - 1:t0], hf, order[k - 1])
                            s.do("dve", lambda e: e.tensor_tensor_scan(out=hf.t[:, t0:t0 + n], data0=A_, data1=B_, initial=ini.ap, op0=ALU.mult, op1=ALU.add),
                                 w=[V(hf.t[:, t0:t0 + n], hf, i)], r=rd, rs=[ini])
                    else:
                        sl = rev(t0, n)
                        if k == 0:
                            s.do("dve", lambda e: e.tensor_tensor_scan(out=xc.t[:, sl], data0=a_.t[:, sl], data1=bt.t[:, sl], initial=0.0, op0=ALU.mult, op1=ALU.add),
                                 w=[V(xc.t[:, t0:t0 + n], xc, i)], r=rd)
                        else:
                            ip = order[k - 1]
                            p0 = 0 if k == 1 else COLT[ip][0]
                            ini = V(xc.t[:, p0:p0 + 1], xc, ip)
                            s.do("dve", lambda e: e.tensor_tensor_scan(out=xc.t[:, sl], data0=a_.t[:, sl], data1=bt.t[:, sl], initial=ini.ap, op0=ALU.mult, op1=ALU.add),
                                 w=[V(xc.t[:, t0:t0 + n], xc, i)], r=rd, rs=[ini])

                def Y(k, d_=d_, order=order):
                    if d_ == 0:
                        return
                    i = order[k]; t0, n = COLT[i]
                    st = yst[k % 2]
                    self.tt("dve", GV(g1, k, n), V(hf.t[:, t0:t0 + n], hf, i), V(xc.t[:, t0:t0 + n], xc, i), ALU.add)
                    self.tt("dve", V(st.t[:, 0:n], st), GV(g1, k, n), V(sgc.t[:, t0:t0 + n], sgc), ALU.mult)
                    self.y_store(cb, V(st.t[:, 0:n], st), t0, n)

                skew(len(COLT), [Cv, A0, A1], [0, 1, 2])
                skew(len(COLT), [B0, B1, B2, B3, SC, Y], [0, 1, 1, 2, 3, 4])

    def qk_norm_rope(self, pb, n, gcol, dst, d0, rope_t0, bufs, psn, psr):
        s = self.S
        sqb, rsb, knb, t1b = bufs
        SQ = V(sqb.t[:, 0:n], sqb); RS = V(rsb.t[:, 0:n], rsb); KN = V(knb.t[:, 0:n], knb); T1 = V(t1b.t[:, 0:n], t1b)
        self.act(SQ, V(pb.t[:, 0:n], pb), AF.Square)
        self.mm(V(psn.t[:, 0:n], psn), self.onesbd[:, :], SQ, start=True, stop=True)
        self.act(RS, V(psn.t[:, 0:n], psn), AF.Ln, scale=1.0 / 64, bias=self.eps_col)
        self.act(RS, RS, AF.Exp, scale=-0.5)
        if rope_t0 is None:
            self.stt(V(dst.t[:, d0:d0 + n], dst), V(pb.t[:, 0:n], pb), gcol, RS, ALU.mult, ALU.mult)
            return
        self.stt(KN, V(pb.t[:, 0:n], pb), gcol, RS, ALU.mult, ALU.mult)
        self.mm(V(psr.t[:, 0:n], psr), self.swapb[:, :], KN, start=True, stop=True)
        self.tt("pool", T1, KN, V(self.cos.t[:, rope_t0:rope_t0 + n], self.cos), ALU.mult)
        self.tt("dve", RS, V(psr.t[:, 0:n], psr), V(self.sin.t[:, rope_t0:rope_t0 + n], self.sin), ALU.mult)
        self.tt("dve", V(dst.t[:, d0:d0 + n], dst), T1, RS, ALU.add)

    def qk_proj_staged(self, wbf, colt, gcol, dst, dst_off, rope, nb):
        s = self.S
        pbs = [(self.PA, 0), (self.PA, 512), (self.PB, 0)]
        pns = [(self.PB, 512), (self.PCD, 0)]
        prs = [(self.PCD, 512), (self.PE_, 0)]
        n_t = len(colt)

        def pv(lst, i, n):
            tb, off = lst[i % len(lst)]
            return V(tb.t[:, off:off + n], tb, off)

        def bv(name, i, n):
            tb = nb[name][i % len(nb[name])]
            return V(tb.t[:, 0:n], tb)

        def P(i):
            t0, n = colt[i]
            for kb in range(8):
                self.mm(pv(pbs, i, n), V(wbf.t[:, kb, 0:128], wbf), V(self.xlT.t[:, kb, t0:t0 + n], self.xlT, None), start=(kb == 0), stop=(kb == 7))
            self.act(bv("sq", i, n), pv(pbs, i, n), AF.Square)

        def N1(i):
            t0, n = colt[i]
            self.mm(pv(pns, i, n), self.onesbd[:, :], bv("sq", i, n), start=True, stop=True)
            self.act(bv("rs", i, n), pv(pns, i, n), AF.Ln, scale=1.0 / 64, bias=self.eps_col)
            self.act(bv("rs", i, n), bv("rs", i, n), AF.Exp, scale=-0.5)

        def N2(i):
            t0, n = colt[i]
            d0 = t0 - dst_off
            if not rope or t0 < 256:
                self.stt(V(dst.t[:, d0:d0 + n], dst), pv(pbs, i, n), gcol, bv("rs", i, n), ALU.mult, ALU.mult)
                return
            self.stt(bv("kn", i, n), pv(pbs, i, n), gcol, bv("rs", i, n), ALU.mult, ALU.mult)
            self.mm(pv(prs, i, n), self.swapb[:, :], bv("kn", i, n), start=True, stop=True)
            r0 = t0 - 256
            self.tt("pool", bv("t1", i, n), bv("kn", i, n), V(self.cos.t[:, r0:r0 + n], self.cos), ALU.mult)

        def N3(i):
            t0, n = colt[i]
            d0 = t0 - dst_off
            if not rope or t0 < 256:
                return
            r0 = t0 - 256
            self.tt("dve", bv("rs", i, n), pv(prs, i, n), V(self.sin.t[:, r0:r0 + n], self.sin), ALU.mult)
            self.tt("dve", V(dst.t[:, d0:d0 + n], dst), bv("t1", i, n), bv("rs", i, n), ALU.add)

        skew(n_t, [P, N1, N2, N3], [0, 1, 2, 3])

    def phase_gqa(self):
        s = self.S
        self.new_phase()
        C = self.carve
        kT = C("kT", [128, NT], BF16)
        kT2 = C("kT2", [128, 2, NT], BF16)
        vaug = C("vaug", [128, NTILE, 2, 65], BF16)
        qT = C("qT", [128, NLAT], BF16)
        s.do("pool", lambda e: e.memset(kT2.t[:, :, :], 0.0), w=[kT2[:, :, :]])
        sgd = C("sgd", [128, NLAT], BF16)
        self.cos = C("cos", [128, NLAT], F32); self.sin = C("sin", [128, NLAT], F32)
        cst = C("cst", [128, 128], F32)
        self.swapb = C("swapb", [128, 128], BF16); self.onesbd = C("onesbd", [128, 128], BF16)
        gq = C("gq", [128, 1], F32); gk = C("gk", [128, 1], F32)
        nbufs = dict(sq=[C("sqb%d" % i, [128, 512], BF16) for i in range(3)], rs=[C("rsb%d" % i, [128, 512], F32) for i in range(4)],
                     kn=[C("knb%d" % i, [128, 512], BF16) for i in range(3)], t1=[C("t1b%d" % i, [128, 512], F32) for i in range(3)])
        Pb = [C("Pb%d" % i, [128, 1024], BF16) for i in range(4)]
        wst = C("wst", [128, 8, 128], F32); wbf = C("wbf", [128, 8, 128], BF16)
        ones_row = C("ones_row", [128, 64], F32); osbs = [C("osb%d" % i, [128, 512], F32) for i in range(2)]
        tmps = [C("tmp%d" % i, [128, 512], F32) for i in range(2)]; yst = [C("yst%d" % i, [128, 512], BF16) for i in range(2)]
        s.dma("sp", self.cos[:, :], self.cosT[:, :]); s.dma("act", self.sin[:, :], self.sinT[:, :])
        s.dma("sp", cst[:, :], self.c_swap[:, :]); self.cp("dve", self.swapb[:, :], cst[:, :])
        s.dma("sp", cst[:, :], self.c_onesbd[:, :]); self.cp("dve", self.onesbd[:, :], cst[:, :])
        s.dma("sp", gq[:, :], self.qn_pp[:, :]); s.dma("sp", gk[:, :], self.kn_pp[:, :])
        s.do("pool", lambda e: e.memset(ones_row.t[:, :], 1.0), w=[ones_row[:, :]])
        s.do("pool", lambda e: e.memset(vaug.t[:, :, :, 64:65], 1.0), w=[vaug[:, :, :, :]])
        self.load_w(1, [(1536, 128, 0)], wst, wbf)
        self.qk_proj_staged(wbf, COLT, gk[:, :], kT, 0, True, nbufs)
        self.cp("pool", V(kT2.t[0:64, 0, :], kT2), V(kT.t[0:64, :], kT))
        self.cp("pool", V(kT2.t[64:128, 1, :], kT2), V(kT.t[64:128, :], kT))
        self.load_w(1, [(1664, 128, 0)], wst, wbf)
        self.proj_tm(wbf, 128, lambda pb, t: self.cp("dve" if t % 2 else "act", V(vaug.t[:, t, :, 0:64], vaug),
                                                      V(pb.t[:, 0:128].rearrange("p (h d) -> p h d", h=2), pb)))
        LATT = COLT[1:]
        cnt = 0
        for a in range(self.debug.get("gqa_blocks", 4)):
            self.load_w(1, [(1024 + 64 * a, 64, 0), (1024 + 64 * (a + 4), 64, 64)], wst, wbf)
            self.qk_proj_staged(wbf, LATT, gq[:, :], qT, 256, True, nbufs)
            self.load_w(1, [(1792 + 64 * a, 64, 0), (1792 + 64 * (a + 4), 64, 64)], wst, wbf)
            self.proj_fm(wbf, lambda pb, t0, n: self.act(V(sgd.t[:, t0 - 256:t0 - 256 + n], sgd), V(pb.t[:, 0:n], pb), AF.Silu), colt=LATT)
            if "gqa_q" in self.debug:
                self.dump("d_q%d" % a, qT[:, :], [128, NLAT], BF16)
            PTf = TB("PTf", self.PT.t[:, :].bitcast(F32))
            PTf.parts = self.PT.parts; PTf.default = self.PT.default
            slots = [self.PA, self.PB, self.PCD]
            NS = len(slots)
            NP2 = NTILE // 2
            items = [(hh, qt, jp) for hh in range(2) for qt in range(8) for jp in range(NP2)]
            pO = self.PE_

            def S_(n):
                hh, qt, jp = items[n]
                pS = slots[n % NS]
                for h2 in range(2):
                    j = 2 * jp + h2
                    self.mm(V(pS.t[:, 512 * h2:512 * h2 + 512], pS), V(kT2.t[:, hh, j * 128:(j + 1) * 128], kT2), V(qT.t[:, qt * 512:qt * 512 + 512], qT), start=True, stop=True)

            def E_(n):
                pS = slots[n % NS]
                self.act(Pb[n % len(Pb)][:, :], V(pS.t[:, :], pS), AF.Exp, scale=0.125)

            def PV_(n):
                hh, qt, jp = items[n]
                blk = n // NP2
                for h2 in range(2):
                    j = 2 * jp + h2
                    self.mm(V(pO.t[0:65, 0:512], pO), V(vaug.t[:, j, hh, :], vaug), V(Pb[n % len(Pb)].t[:, 512 * h2:512 * h2 + 512], Pb[n % len(Pb)]),
                            start=(j == 0), stop=(j == NTILE - 1))
                if jp == NP2 - 1:
                    hd = a + 4 * hh
                    hp = slice(64 * hh, 64 * hh + 64)
                    q0 = qt * 512
                    osb = osbs[blk % 2]; tmp = tmps[blk % 2]; st = yst[blk % 2]
                    self.act(V(osb.t[64:65, :], osb), V(pO.t[64:65, 0:512], pO), AF.Ln)
                    self.tt("dve", V(tmp.t[0:64, :], tmp), V(pO.t[0:64, 0:512], pO), V(sgd.t[hp, q0:q0 + 512], sgd), ALU.mult)
                    self.act(V(osb.t[64:65, :], osb), V(osb.t[64:65, :], osb), AF.Exp, scale=-1.0)
                    kb = 4 + hd // 2; ph = hd % 2
                    tok0 = 256 + q0

                    def part2(osb=osb, tmp=tmp, st=st, kb=kb, ph=ph, tok0=tok0):
                        self.mm(V(PTf.t[0:64, 0:512], PTf), V(ones_row.t[64:65, 0:64], ones_row), V(osb.t[64:65, :], osb), start=True, stop=True)
                        self.tt("dve", V(st.t[64 * ph:64 * ph + 64, :], st), V(tmp.t[0:64, :], tmp), V(PTf.t[0:64, 0:512], PTf), ALU.mult)
                        s.do("pool", lambda e: e.dma_start(out=self.yscr.t[kb, 64 * ph:64 * ph + 64, tok0:tok0 + 512], in_=st.t[64 * ph:64 * ph + 64, :]),
                             w=[self.yscr.k(t)[:, :, :] for t in tiles_of(tok0, 512)], r=[st[:, :]], dma=True)
                    pending.append((n + 6, part2))

            def PVw(n):
                while pending and pending[0][0] <= n:
                    pending.pop(0)[1]()
                PV_(n)

            pending = []
            skew(len(items), [S_, E_, PVw], [0, 1, 2])
            while pending:
                pending.pop(0)[1]()


GRID_W = 64

def na_bias_table(rel_bias):
    cfgs = [(2, kp) for kp in range(0, 5)] + [(0, kp) for kp in range(4)] + [(1, kp) for kp in range(4)] + \
           [(30, kp) for kp in range(28, 32)] + [(31, kp) for kp in range(28, 32)]
    H = rel_bias.shape[0]
    tab = np.full((H, len(cfgs), 128, 128), -30000.0, np.float32)
    kidx = np.arange(128); qidx = np.arange(128)
    for ci, (rp, kp) in enumerate(cfgs):
        krow = 2 * kp + kidx // 64; kcol = kidx % 64
        qrow = 2 * rp + qidx // 64; qcol = qidx % 64
        rs = np.clip(qrow - 4, 0, 64 - 8); cs = np.clip(qcol - 8, 0, 64 - 16)
        KR, QR = np.meshgrid(krow, qrow, indexing='ij'); KC, QC = np.meshgrid(kcol, qcol, indexing='ij')
        RS = np.broadcast_to(rs[None, :], KR.shape); CS = np.broadcast_to(cs[None, :], KR.shape)
        valid = (KR >= RS) & (KR < RS + 8) & (KC >= CS) & (KC < CS + 16)
        dr = np.clip(KR - QR + 7, 0, 14); dc = np.clip(KC - QC + 15, 0, 30)
        g = rel_bias[:, dr, dc]
        tab[:, ci] = np.where(valid[None], g, np.float32(-30000.0))
    return tab

def consts():
    half = 32
    inv = (10000.0 ** (-np.arange(0, half, 2, dtype=np.float32) / half)).astype(np.float32)
    t = np.arange(4096)
    row = (t // GRID_W).astype(np.float32); col = (t % GRID_W).astype(np.float32)
    ang = np.concatenate([row[:, None] * inv, col[:, None] * inv], axis=-1).astype(np.float32)
    cos = np.cos(ang).astype(np.float32); sin = np.sin(ang).astype(np.float32)
    cosT = np.repeat(cos.T, 2, axis=0)
    sinT = np.repeat(sin.T, 2, axis=0)
    sgn = np.where(np.arange(64) % 2 == 0, -1.0, 1.0).astype(np.float32)[:, None]
    sinT = sinT * sgn
    cosT = np.concatenate([cosT, cosT], 0).astype(np.float32); sinT = np.concatenate([sinT, sinT], 0).astype(np.float32)
    swap = np.zeros((128, 128), np.float32)
    for k in range(128):
        swap[k, k ^ 1] = 1.0
    onesbd = np.zeros((128, 128), np.float32); onesbd[:64, :64] = 1; onesbd[64:, 64:] = 1
    iota = np.broadcast_to(np.arange(272, dtype=np.float32)[None, :], (64, 272)).copy()
    return dict(cosT=cosT, sinT=sinT, c_swap=swap, c_onesbd=onesbd, c_iota=iota)

def pp(v, nb):
    return np.ascontiguousarray(v.reshape(nb, 128).T)

def prep(inp, ncores=8):
    f = lambda a: np.ascontiguousarray(np.asarray(a, np.float32))
    shared = dict(
        ada_w=f(inp["ada_w"]), ada_b=f(inp["ada_b"]),
        ada_b_pp=f(np.stack([pp(inp["ada_b"][l], 24) for l in range(2)])),
        pre_g_pp=f(np.stack([pp(inp["pre_g"][l], 8) for l in range(2)])),
        post_g=f(inp["post_g"]),
        ev_w_in=f(inp["ev_w_in"][0]), od_w_in=f(inp["od_w_in"][0]), ev_w_out=f(inp["ev_w_out"][0]), od_w_out=f(inp["od_w_out"][0]),
        s5_lam_re=f(inp["s5_lam_re"][0].reshape(64, 64).T), s5_lam_im=f(inp["s5_lam_im"][0].reshape(64, 64).T), s5_log_dt=f(inp["s5_log_dt"][0].reshape(1, 64)),
        s5_b_re=f(inp["s5_b_re"][0]), s5_b_im=f(inp["s5_b_im"][0]), s5_c_re=f(inp["s5_c_re"][0]), s5_c_im=f(inp["s5_c_im"][0]),
        s5_d_pp=f(pp(inp["s5_d"][0], 4)), s5_w_glu=f(inp["s5_w_glu"][0]), s5_b_glu_pp=f(pp(inp["s5_b_glu"][0], 4)),
        na_bias=na_bias_table(np.asarray(inp["na_rel_bias"][0], np.float32)),
        lru_conv_w_pp=f(np.stack([pp(inp["lru_conv_w"][0][j], 4) for j in range(4)], axis=-1)),
        lru_conv_b_pp=f(pp(inp["lru_conv_b"][0], 4)),
        lru_lam_pp=f(np.stack([pp(inp["lru_lam"][0][d], 4) for d in range(2)], axis=1)),
        lru_w_a=f(inp["lru_w_a"][0]), lru_w_x=f(inp["lru_w_x"][0]),
        lru_b_a_pp=f(np.stack([pp(inp["lru_b_a"][0][d], 4) for d in range(2)], axis=1)),
        lru_b_x_pp=f(np.stack([pp(inp["lru_b_x"][0][d], 4) for d in range(2)], axis=1)),
        qn_pp=f(np.tile(inp["gqa_q_norm"][0], 2).reshape(128, 1)), kn_pp=f(np.tile(inp["gqa_k_norm"][0], 2).reshape(128, 1)),
    )
    shared.update(consts())
    maps = []
    for i in range(ncores):
        b = i % 4
        m = dict(shared)
        m["xs"] = f(np.concatenate([inp["ctx"][b], inp["x"][b]], axis=0))
        cT = np.stack([pp(inp["c"][b], 8), pp(inp["c_ctx"], 8)], axis=-1)
        m["cT"] = f(cT)
        maps.append(m)
    return maps


def build_program():
    k = KB()
    k.phase_consts(); k.phase_adaln(); k.phase_prologue0()
    k.phase_s5(); k.phase_glu(); k.phase_na(); k.phase_outproj(0)
    k.phase_lru(); k.phase_gqa(); k.phase_outproj(1)
    return k.finish()


def kernel(**inputs):
    inp = {k_: np.asarray(v) for k_, v in inputs.items()}
    maps = prep(inp, 8)
    nc = build_program()
    res = run_bass_kernel_spmd(nc, maps, core_ids=list(range(8)))
    out = np.stack([np.asarray(res.results[b]["out"], dtype=np.float32) for b in range(4)], axis=0)
    return out
```
